# Optimizing a Trainium2 kernel written in Bass

```python
import jax
import jax.numpy as jnp
from jax import lax
import numpy as np

D_MODEL = 1024
BATCH = 8
SEQ = 8192
DEPTH = 1

NORM_EPS = 1e-6
D_FF = 2816

ATTN_HEAD_DIM = 64
ATTN_HEADS_PER_GROUP = 4
ATTN_GROUPS = ((128, 1), (512, 4), (2048, 16))
N_ATTN_GROUPS = 3
ATTN_WIDTH = N_ATTN_GROUPS * ATTN_HEADS_PER_GROUP * ATTN_HEAD_DIM
ATTN_OUT_WIDTH = ATTN_HEADS_PER_GROUP * ATTN_HEAD_DIM
ROPE_THETA = 500000.0
ROPE_DIM = ATTN_HEAD_DIM // 4

RWKV_HEAD_DIM = 64
RWKV_WIDTH = D_MODEL
RWKV_HEADS = RWKV_WIDTH // RWKV_HEAD_DIM
DECAY_LORA = 64
ICLR_LORA = 64
GATE_LORA = 160
RWKV_GN_EPS = 64e-5
RWKV_STREAM = 3 * RWKV_WIDTH + DECAY_LORA + ICLR_LORA + GATE_LORA

IN_COLS = 3 * ATTN_WIDTH + RWKV_STREAM + 2 * D_MODEL

kernel_name = 'hybrid_dilated_attn_rwkv7_macaron'


def rms_norm(x, gain):
    xf = x.astype(jnp.float32)
    y = xf * lax.rsqrt(jnp.mean(xf * xf, axis=-1, keepdims=True) + NORM_EPS)
    return (y * gain.astype(jnp.float32)).astype(x.dtype)


def swiglu(h, w_gate, w_up, w_down):
    return (jax.nn.silu(h @ w_gate) * (h @ w_up)) @ w_down


def partial_rope(x, positions):
    half = ROPE_DIM // 2
    inv_freq = jnp.power(ROPE_THETA, -jnp.arange(half, dtype=jnp.float32) * (2.0 / ROPE_DIM))
    ang = positions.astype(jnp.float32)[:, None] * inv_freq[None, :]
    cos = jnp.cos(ang)[None, :, None, :]
    sin = jnp.sin(ang)[None, :, None, :]
    xr = x[..., :ROPE_DIM].astype(jnp.float32)
    x1, x2 = xr[..., :half], xr[..., half:]
    rot = jnp.concatenate([x1 * cos - x2 * sin, x2 * cos + x1 * sin], axis=-1).astype(x.dtype)
    return jnp.concatenate([rot, x[..., ROPE_DIM:]], axis=-1)


def dilated_group(q, k, v, window, dilation):
    B, S, H, Dh = q.shape
    n = window // dilation
    L = S // dilation
    nb = -(-L // n)
    Lp = nb * n

    def to_sub(t):
        t = t.reshape(B, L, dilation, H, Dh).transpose(0, 2, 3, 1, 4)
        t = jnp.pad(t, ((0, 0), (0, 0), (0, 0), (0, Lp - L), (0, 0)))
        return t.reshape(B, dilation, H, nb, n, Dh)

    def with_prev(t):
        prev = jnp.pad(t, ((0, 0), (0, 0), (0, 0), (1, 0), (0, 0), (0, 0)))[:, :, :, :-1]
        return jnp.concatenate([prev, t], axis=4)

    def from_sub(t):
        t = t.reshape(B, dilation, H, Lp, t.shape[-1])[:, :, :, :L]
        return t.transpose(0, 3, 1, 2, 4).reshape(B, S, H, t.shape[-1])

    qb = to_sub(q)
    kw = with_prev(to_sub(k))
    vw = with_prev(to_sub(v))
    s = jnp.einsum('brhiqd,brhikd->brhiqk', qb, kw, preferred_element_type=jnp.float32)
    qi = jnp.arange(n)[:, None]
    ki = jnp.arange(2 * n)[None, :]
    rel = qi + n - ki
    band = (rel >= 0) & (rel <= n)
    blk = jnp.arange(nb)[:, None, None]
    valid = band[None] & ((blk > 0) | (ki[None] >= n))
    s = jnp.where(valid, s, -jnp.inf)
    m = jnp.max(s, axis=-1)
    p = jnp.exp(s - m[..., None])
    l = jnp.sum(p, axis=-1)
    acc = jnp.einsum('brhiqk,brhikd->brhiqd', p, vw.astype(jnp.float32))
    return from_sub(acc), from_sub(m[..., None])[..., 0], from_sub(l[..., None])[..., 0]


def dilated_attention(qkv, positions):
    B, S, _ = qkv.shape
    G, Hg, Dh = N_ATTN_GROUPS, ATTN_HEADS_PER_GROUP, ATTN_HEAD_DIM
    q, k, v = jnp.split(qkv, 3, axis=-1)
    q = partial_rope(q.reshape(B, S, G * Hg, Dh), positions) * (Dh ** -0.5)
    k = partial_rope(k.reshape(B, S, G * Hg, Dh), positions)
    v = v.reshape(B, S, G * Hg, Dh)
    accs, ms, ls = [], [], []
    for g, (window, dilation) in enumerate(ATTN_GROUPS):
        sl = slice(g * Hg, (g + 1) * Hg)
        acc, m, l = dilated_group(q[:, :, sl], k[:, :, sl], v[:, :, sl], window, dilation)
        accs.append(acc)
        ms.append(m)
        ls.append(l)
    m_all = jnp.stack(ms)
    c = jnp.exp(m_all - jnp.max(m_all, axis=0))
    num = jnp.sum(c[..., None] * jnp.stack(accs), axis=0)
    den = jnp.sum(c * jnp.stack(ls), axis=0)
    out = (num / den[..., None]).astype(qkv.dtype)
    return out.reshape(B, S, Hg * Dh)


def token_shift(z, mu):
    prev = jnp.pad(z, ((0, 0), (1, 0), (0, 0)))[:, :-1]
    return z + (prev - z) * mu


def wkv7_scan(r, w, k, v, a, b):
    B, S, H, N = r.shape

    def step(state, inp):
        r_t, w_t, k_t, v_t, a_t, b_t = inp
        sa = jnp.einsum('bhvk,bhk->bhv', state, a_t)
        state = (state * w_t[:, :, None, :] + sa[..., None] * b_t[:, :, None, :]
                 + v_t[..., None] * k_t[:, :, None, :])
        return state, jnp.einsum('bhvk,bhk->bhv', state, r_t)

    xs = tuple(jnp.moveaxis(t.astype(jnp.float32), 1, 0) for t in (r, w, k, v, a, b))
    _, ys = lax.scan(step, jnp.zeros((B, H, N, N), jnp.float32), xs)
    return jnp.moveaxis(ys, 0, 1)


def rwkv7_time_mix(z, w0, w2, a0, a2, g2, k_k, k_a, r_k, ln_w, ln_b, w_out):
    B, S, _ = z.shape
    RW, H, N = RWKV_WIDTH, RWKV_HEADS, RWKV_HEAD_DIM
    o1, o2, o3 = RW, 2 * RW, 3 * RW
    o4, o5 = o3 + DECAY_LORA, o3 + DECAY_LORA + ICLR_LORA
    r, k, v = z[..., :o1], z[..., o1:o2], z[..., o2:o3]
    zw, za, zg = z[..., o3:o4], z[..., o4:o5], z[..., o5:]
    logw = -jax.nn.softplus(-(w0 + jnp.tanh(zw) @ w2)) - 0.5
    decay = jnp.exp(-jnp.exp(logw.astype(jnp.float32)))
    a = jax.nn.sigmoid(a0 + za @ a2)
    g = jax.nn.sigmoid(zg) @ g2
    kk = (k * k_k).astype(jnp.float32).reshape(B, S, H, N)
    kk = kk / jnp.maximum(jnp.sqrt(jnp.sum(kk * kk, axis=-1, keepdims=True)), 1e-12)
    k = k * (1.0 + (a - 1.0) * k_a)
    rh = r.reshape(B, S, H, N).astype(jnp.float32)
    kh = k.reshape(B, S, H, N).astype(jnp.float32)
    vh = v.reshape(B, S, H, N).astype(jnp.float32)
    ah = a.reshape(B, S, H, N).astype(jnp.float32)
    y = wkv7_scan(rh, decay.reshape(B, S, H, N), kh, vh, -kk, kk * ah)
    mu = jnp.mean(y, axis=-1, keepdims=True)
    var = jnp.mean(jnp.square(y - mu), axis=-1, keepdims=True)
    yn = ((y - mu) * lax.rsqrt(var + RWKV_GN_EPS)).reshape(B, S, RW)
    yn = yn * ln_w.astype(jnp.float32) + ln_b.astype(jnp.float32)
    bonus = (jnp.sum(rh * kh * r_k.astype(jnp.float32), axis=-1, keepdims=True) * vh).reshape(B, S, RW)
    return ((yn + bonus) * g.astype(jnp.float32)).astype(z.dtype) @ w_out


def hybrid_mixer(h, positions, w_in, gate_bias, attn_w_up, rwkv_mu, rwkv_w0, rwkv_w2, rwkv_a0,
                 rwkv_a2, rwkv_g2, rwkv_k_k, rwkv_k_a, rwkv_r_k, rwkv_ln_w, rwkv_ln_b,
                 rwkv_w_out, w_o):
    proj = h @ w_in
    aw, rs = 3 * ATTN_WIDTH, RWKV_STREAM
    gates = jax.nn.sigmoid(proj[..., aw + rs:] + gate_bias)
    y_attn = dilated_attention(proj[..., :aw], positions) @ attn_w_up
    y_rwkv = rwkv7_time_mix(token_shift(proj[..., aw:aw + rs], rwkv_mu), rwkv_w0, rwkv_w2,
                            rwkv_a0, rwkv_a2, rwkv_g2, rwkv_k_k, rwkv_k_a, rwkv_r_k,
                            rwkv_ln_w, rwkv_ln_b, rwkv_w_out)
    merged = gates[..., :D_MODEL] * y_attn + gates[..., D_MODEL:] * y_rwkv
    return merged @ w_o


def setup_inputs(seed: int = 0) -> dict:
    key = jax.random.key(seed)
    ks = iter(jax.random.split(key, 40))
    f32 = jnp.float32

    def nrm(shape, scale):
        return jax.random.normal(next(ks), shape, f32) * scale

    def gain(shape):
        return 1.0 + 0.05 * jax.random.normal(next(ks), shape, f32)

    L, D, F = DEPTH, D_MODEL, D_FF
    RW, H, N = RWKV_WIDTH, RWKV_HEADS, RWKV_HEAD_DIM
    return {
        'x': nrm((BATCH, SEQ, D), 1.0),
        'ffn1_norm': gain((L, D)),
        'ffn1_w_gate': nrm((L, D, F), D ** -0.5),
        'ffn1_w_up': nrm((L, D, F), D ** -0.5),
        'ffn1_w_down': nrm((L, F, D), F ** -0.5),
        'mix_norm': gain((L, D)),
        'w_in': nrm((L, D, IN_COLS), D ** -0.5),
        'gate_bias': nrm((L, 2 * D), 0.1),
        'attn_w_up': nrm((L, ATTN_OUT_WIDTH, D), ATTN_OUT_WIDTH ** -0.5),
        'rwkv_mu': jax.random.uniform(next(ks), (L, RWKV_STREAM), f32),
        'rwkv_w0': jax.random.uniform(next(ks), (L, RW), f32, -6.0, -1.0),
        'rwkv_w2': nrm((L, DECAY_LORA, RW), 0.1 * DECAY_LORA ** -0.5),
        'rwkv_a0': nrm((L, RW), 0.1),
        'rwkv_a2': nrm((L, ICLR_LORA, RW), 0.1 * ICLR_LORA ** -0.5),
        'rwkv_g2': nrm((L, GATE_LORA, RW), GATE_LORA ** -0.5),
        'rwkv_k_k': 0.85 + 0.05 * jax.random.normal(next(ks), (L, RW), f32),
        'rwkv_k_a': gain((L, RW)),
        'rwkv_r_k': nrm((L, H, N), 0.1),
        'rwkv_ln_w': gain((L, RW)),
        'rwkv_ln_b': nrm((L, RW), 0.02),
        'rwkv_w_out': nrm((L, RW, D), RW ** -0.5),
        'w_o': nrm((L, D, D), D ** -0.5),
        'ffn2_norm': gain((L, D)),
        'ffn2_w_gate': nrm((L, D, F), D ** -0.5),
        'ffn2_w_up': nrm((L, D, F), D ** -0.5),
        'ffn2_w_down': nrm((L, F, D), F ** -0.5),
        'final_norm': gain((D,)),
    }


def reference(x, ffn1_norm, ffn1_w_gate, ffn1_w_up, ffn1_w_down, mix_norm, w_in, gate_bias,
              attn_w_up, rwkv_mu, rwkv_w0, rwkv_w2, rwkv_a0, rwkv_a2, rwkv_g2, rwkv_k_k,
              rwkv_k_a, rwkv_r_k, rwkv_ln_w, rwkv_ln_b, rwkv_w_out, w_o, ffn2_norm,
              ffn2_w_gate, ffn2_w_up, ffn2_w_down, final_norm):
    positions = jnp.arange(x.shape[1], dtype=jnp.int32)
    for layer in range(DEPTH):
        x = x + 0.5 * swiglu(rms_norm(x, ffn1_norm[layer]), ffn1_w_gate[layer],
                             ffn1_w_up[layer], ffn1_w_down[layer])
        x = x + hybrid_mixer(rms_norm(x, mix_norm[layer]), positions, w_in[layer],
                             gate_bias[layer], attn_w_up[layer], rwkv_mu[layer],
                             rwkv_w0[layer], rwkv_w2[layer], rwkv_a0[layer], rwkv_a2[layer],
                             rwkv_g2[layer], rwkv_k_k[layer], rwkv_k_a[layer],
                             rwkv_r_k[layer], rwkv_ln_w[layer], rwkv_ln_b[layer],
                             rwkv_w_out[layer], w_o[layer])
        x = x + 0.5 * swiglu(rms_norm(x, ffn2_norm[layer]), ffn2_w_gate[layer],
                             ffn2_w_up[layer], ffn2_w_down[layer])
    return rms_norm(x, final_norm)
```

```python
import contextlib
import math
import numpy as np
import concourse.bass as bass
import concourse.mybir as mybir
from concourse.bass_utils import run_bass_kernel_spmd

F32 = mybir.dt.float32
BF16 = mybir.dt.bfloat16
AF = mybir.ActivationFunctionType
ALU = mybir.AluOpType
AX = mybir.AxisListType

D = 1024
FF = 2816
NFC = FF // 128
EPS = 1e-6
NCORES = 8


class Buf:
    __slots__ = ("name", "t", "lw", "rd", "excl")

    def __init__(self, name, t, excl=False):
        self.name = name
        self.t = t
        self.lw = None
        self.rd = {}
        self.excl = excl

    def __getitem__(self, idx):
        return self.t[idx]


class Sched:
    ENGS = ("pe", "act", "dve", "pool", "sp")

    def __init__(self, nc, ndma_sems=6):
        self.nc = nc
        self.ops = {e: [] for e in self.ENGS}
        self.cnt = {}
        self.sem = {}
        self.seen = {e: {} for e in self.ENGS}
        self._cm = []
        for e in self.ENGS:
            cm = nc.semaphore("prog_" + e)
            self.sem[e] = cm.__enter__()
            self._cm.append(cm)
            self.cnt[e] = 0
        self.dq = {}
        for q in ("sp", "pool", "act"):
            ring = []
            for i in range(ndma_sems):
                nm = "dma_%s_%d" % (q, i)
                cm = nc.semaphore(nm)
                self.sem[nm] = cm.__enter__()
                self._cm.append(cm)
                self.cnt[nm] = 0
                ring.append(nm)
            self.dq[q] = [ring, 0]
        self.ninstr = 0
        self.pe_rt = {}

    def close(self):
        for cm in reversed(self._cm):
            cm.__exit__(None, None, None)

    def _wait(self, engine, dep):
        if dep is None:
            return
        e, c = dep
        if self.seen[engine].get(e, 0) >= c:
            return
        self.seen[engine][e] = c
        sem = self.sem[e]
        self.ops[engine].append(lambda en, sem=sem, c=c: en.wait_ge(sem, c))
        self.ninstr += 1

    def _deps(self, engine, reads, writes):
        for b in reads:
            if b.lw is not None:
                if b.lw[0] == engine and engine == "pe":
                    continue
                self._wait(engine, b.lw)
        for b in writes:
            if b.lw is not None and not (b.lw[0] == engine and engine == "pe"):
                self._wait(engine, b.lw)
            for e, c in b.rd.items():
                if e == engine and engine == "pe":
                    continue
                self._wait(engine, (e, c))

    def op(self, engine, fn, reads=(), writes=()):
        ex = [b for b in reads if b.excl and engine != "pe"]
        if ex:
            writes = list(writes) + [b for b in ex if b not in writes]
        if engine == "pe":
            rt = getattr(fn, "rt", None)
            for b in writes:
                if b.lw is not None and b.lw[0] == "pe" and self.pe_rt.get(id(b)) != rt:
                    self._wait("pe", b.lw)
                self.pe_rt[id(b)] = rt
        self._deps(engine, reads, writes)
        self.cnt[engine] += 1
        c = self.cnt[engine]
        sem = self.sem[engine]
        self.ops[engine].append(lambda en, fn=fn, sem=sem: fn(en).then_inc(sem, 1))
        self.ninstr += 1
        for b in reads:
            b.rd[engine] = c
        for b in writes:
            b.lw = (engine, c)
            b.rd = {}
        return c

    def dma(self, q, out_ap, in_ap, reads=(), writes=(), **kw):
        ring, i = self.dq[q]
        nm = ring[i % len(ring)]
        self.dq[q][1] = i + 1
        if self.cnt[nm] > 0:
            self._wait(q, (nm, self.cnt[nm]))
        self._deps(q, reads, writes)
        self.cnt[nm] += 16
        c = self.cnt[nm]
        sem = self.sem[nm]
        self.ops[q].append(
            lambda en, o=out_ap, i_=in_ap, sem=sem, kw=kw: en.dma_start(out=o, in_=i_, **kw).then_inc(sem, 16))
        self.ninstr += 1
        for b in reads:
            b.rd[nm] = c
        for b in writes:
            b.lw = (nm, c)
            b.rd = {}

    def barrier(self):
        for e in self.ENGS:
            for s, c in self.cnt.items():
                if s != e and c > 0:
                    self._wait(e, (s, c))

    def emit(self):
        nc = self.nc
        ops = self.ops
        with nc.Block() as block:
            @block.tensor
            def _(en):
                for f in ops["pe"]:
                    f(en)

            @block.scalar
            def _(en):
                for f in ops["act"]:
                    f(en)

            @block.vector
            def _(en):
                for f in ops["dve"]:
                    f(en)

            @block.gpsimd
            def _(en):
                for f in ops["pool"]:
                    f(en)

            @block.sync
            def _(en):
                for f in ops["sp"]:
                    f(en)
        self.ops = {e: [] for e in self.ENGS}


class Ctx:
    N = [0]

    def __init__(self, nc):
        self.nc = nc
        self.es = contextlib.ExitStack()

    def sb(self, name, shape, dt):
        Ctx.N[0] += 1
        t = self.es.enter_context(self.nc.sbuf_tensor("%s_%d" % (name, Ctx.N[0]), list(shape), dt))
        return Buf(name, t)

    def ps(self, name, shape, dt=F32):
        Ctx.N[0] += 1
        nbytes = int(np.prod(shape[1:])) * (4 if dt == F32 else 2)
        assert nbytes == 2048, (name, shape)
        t = self.es.enter_context(self.nc.psum_tensor("%s_%d" % (name, Ctx.N[0]), list(shape), dt))
        return Buf(name, t, excl=True)

    def close(self):
        self.es.close()


def make_ident(sc, cx, dt):
    idf = cx.sb("identf", [128, 128], F32)
    idb = cx.sb("ident", [128, 128], dt)
    sc.op("pool", lambda en: en.memset(idf[:], 1.0), writes=[idf])
    sc.op("pool", lambda en: en.affine_select(out=idf[:], in_=idf[:], pattern=[[-1, 128]], compare_op=ALU.is_equal,
                                               fill=0.0, base=0, channel_multiplier=1), reads=[idf], writes=[idf])
    sc.op("dve", lambda en: en.tensor_copy(out=idb[:], in_=idf[:]), reads=[idf], writes=[idb])
    return idf, idb


def load_weight_bf16(sc, stg, dst, dst_idx_fn, w_ap, K, F, cw, scale_col=None, qi=[0]):
    nkc = K // 128
    for kc in range(nkc):
        for f0 in range(0, F, cw):
            fw = min(cw, F - f0)
            s = stg[qi[0] % len(stg)]
            q = "sp"
            sc.dma(q, s[:, 0:fw], w_ap[kc * 128:(kc + 1) * 128, f0:f0 + fw], writes=[s])
            eng = ("act", "dve", "pool")[qi[0] % 3]
            qi[0] += 1
            o = dst_idx_fn(kc, f0, fw)
            if scale_col is None:
                if eng == "act":
                    sc.op("act", lambda en, o=o, s=s, fw=fw: en.copy(out=o, in_=s[:, 0:fw]), reads=[s], writes=[dst])
                else:
                    sc.op(eng, lambda en, o=o, s=s, fw=fw: en.tensor_copy(out=o, in_=s[:, 0:fw]), reads=[s],
                          writes=[dst])
            else:
                sca = scale_col[:, kc:kc + 1]
                if eng == "act":
                    sc.op("act", lambda en, o=o, s=s, fw=fw, sca=sca: en.activation(out=o, in_=s[:, 0:fw],
                                                                                       func=AF.Copy, scale=sca),
                          reads=[s, scale_col], writes=[dst])
                else:
                    sc.op(eng, lambda en, o=o, s=s, fw=fw, sca=sca: en.tensor_scalar(
                        out=o, in0=s[:, 0:fw], scalar1=sca, scalar2=None, op0=ALU.mult),
                          reads=[s, scale_col], writes=[dst])


def rms_rstd(sc, x, junk, ss, rstd, eng_sq="act"):
    sc.op("act", lambda en: en.activation(out=junk[:], in_=x[:], func=AF.Square, accum_out=ss[:]),
          reads=[x], writes=[junk, ss])
    sc.op("act", lambda en: en.activation(out=rstd[:], in_=ss[:], func=AF.Sqrt, bias=EPS, scale=1.0 / D),
          reads=[ss], writes=[rstd])
    sc.op("dve", lambda en: en.reciprocal(out=rstd[:], in_=rstd[:]), reads=[rstd], writes=[rstd])


def phase_ffn(nc, sc, S, x_src, x_dst, gain, wg, wu, wd, final_gain=None):
    TF = 256
    NST = TF // 128
    cx = Ctx(nc)
    wgs = cx.sb("wg", [128, 8, FF], BF16)
    wus = cx.sb("wu", [128, 8, FF], BF16)
    wds = cx.sb("wd", [128, NFC, D], BF16)
    stg = [cx.sb("stg%d" % i, [128, 1408], F32) for i in range(2)]
    gcol = cx.sb("gcol", [128, 8], F32)
    idf, idb = make_ident(sc, cx, BF16)
    sc.dma("sp", gcol[:], gain.rearrange("(kc p) -> p kc", p=128), writes=[gcol], allow_slow_non_contiguous=True)
    load_weight_bf16(sc, stg, wgs, lambda kc, f0, fw: wgs[:, kc, f0:f0 + fw], wg, D, FF, 1408, gcol)
    load_weight_bf16(sc, stg, wus, lambda kc, f0, fw: wus[:, kc, f0:f0 + fw], wu, D, FF, 1408, gcol)
    load_weight_bf16(sc, stg, wds, lambda kc, f0, fw: wds[:, kc, f0:f0 + fw], wd, FF, D, 1024, None)
    fg = None
    if final_gain is not None:
        fg = cx.sb("fg", [128, D], F32)
        sc.dma("sp", fg[:], final_gain.partition_broadcast(128), writes=[fg])

    NB = 2
    xs = [[cx.sb("xs%d_%d" % (b, st), [128, D], F32) for st in range(NST)] for b in range(NB)]
    hb = [cx.sb("hb%d" % st, [128, D], BF16) for st in range(NST)]
    hT = [cx.sb("hT%d" % b, [128, 8, TF], BF16) for b in range(NB)]
    aT = [cx.sb("aT%d" % b, [128, NFC, TF], BF16) for b in range(NB)]
    junk = cx.sb("junk", [128, D], F32)
    ss = [cx.sb("ss%d" % i, [128, 1], F32) for i in range(2)]
    rstd = [cx.sb("rstd%d" % i, [128, 1], F32) for i in range(2)]
    sg = [cx.sb("sg%d" % i, [128, TF], F32) for i in range(2)]
    ptp = cx.ps("ptp", [128, D], BF16)
    pg = [cx.ps("pg%d" % i, [128, 512], F32) for i in range(2)]
    pu = [cx.ps("pu%d" % i, [128, 512], F32) for i in range(2)]
    pd = [cx.ps("pd%d" % i, [128, 512], F32) for i in range(2)]

    ntiles = S // TF
    k = 0
    for ti in range(ntiles):
        b = ti % NB
        t0 = ti * TF
        for st in range(NST):
            x = xs[b][st]
            sc.dma("sp", x[:], x_src[t0 + st * 128:t0 + (st + 1) * 128, :], writes=[x])
            r = rstd[st % 2]
            rms_rstd(sc, x, junk, ss[st % 2], r)
            h = hb[st]
            sc.op("dve", lambda en, h=h, x=x, r=r: en.tensor_scalar(out=h[:], in0=x[:], scalar1=r[:], scalar2=None,
                                                                    op0=ALU.mult), reads=[x, r], writes=[h])
            for kc in range(8):
                sc.op("pe", lambda en, kc=kc, h=h: en.transpose(out=ptp[:, kc * 128:(kc + 1) * 128],
                                                                 in_=h[:, kc * 128:(kc + 1) * 128], identity=idb[:]),
                      reads=[h, idb], writes=[ptp])
            sc.op("act", lambda en, b=b, st=st: en.copy(
                out=hT[b][:, :, st * 128:(st + 1) * 128],
                in_=ptp[:].rearrange("p (k t) -> p k t", k=8)), reads=[ptp], writes=[hT[b]])
        for fc in range(NFC):
            g = pg[fc % 2]
            u = pu[fc % 2]
            for kc in range(8):
                sc.op("pe", lambda en, g=g, kc=kc, fc=fc, b=b: en.matmul(
                    out=g[:, 0:TF], lhsT=wgs[:, kc, fc * 128:(fc + 1) * 128], rhs=hT[b][:, kc, :],
                    start=(kc == 0), stop=(kc == 7)), reads=[wgs, hT[b]], writes=[g])
            for kc in range(8):
                sc.op("pe", lambda en, u=u, kc=kc, fc=fc, b=b: en.matmul(
                    out=u[:, 0:TF], lhsT=wus[:, kc, fc * 128:(fc + 1) * 128], rhs=hT[b][:, kc, :],
                    start=(kc == 0), stop=(kc == 7)), reads=[wus, hT[b]], writes=[u])
            s_ = sg[fc % 2]
            sc.op("act", lambda en, s_=s_, g=g: en.activation(out=s_[:], in_=g[:, 0:TF], func=AF.Silu),
                  reads=[g], writes=[s_])
            sc.op("dve", lambda en, s_=s_, u=u, fc=fc, b=b: en.tensor_tensor(
                out=aT[b][:, fc, :], in0=u[:, 0:TF], in1=s_[:], op=ALU.mult), reads=[u, s_], writes=[aT[b]])
        for st in range(NST):
            x = xs[b][st]
            for half in range(2):
                p = pd[k % 2]
                k += 1
                for fc in range(NFC):
                    sc.op("pe", lambda en, p=p, fc=fc, st=st, half=half, b=b: en.matmul(
                        out=p[:], lhsT=aT[b][:, fc, st * 128:(st + 1) * 128],
                        rhs=wds[:, fc, half * 512:(half + 1) * 512], start=(fc == 0), stop=(fc == NFC - 1)),
                          reads=[aT[b], wds], writes=[p])
                sc.op("dve", lambda en, p=p, x=x, half=half: en.scalar_tensor_tensor(
                    out=x[:, half * 512:(half + 1) * 512], in0=p[:], scalar=0.5,
                    in1=x[:, half * 512:(half + 1) * 512], op0=ALU.mult, op1=ALU.add), reads=[p, x], writes=[x])
            if fg is not None:
                r = rstd[st % 2]
                rms_rstd(sc, x, junk, ss[st % 2], r)
                sc.op("dve", lambda en, x=x, r=r: en.scalar_tensor_tensor(
                    out=x[:], in0=x[:], scalar=r[:], in1=fg[:], op0=ALU.mult, op1=ALU.mult),
                      reads=[x, r, fg], writes=[x])
            sc.dma("pool", x_dst[t0 + st * 128:t0 + (st + 1) * 128, :], x[:], reads=[x])
    sc.barrier()
    sc.emit()
    cx.close()


I32 = mybir.dt.int32
NH_A = 12
AW = 768
IN_COLS = 7712
RS = 3360
OFF_R = 2304
OFF_G = 2304 + 3360
ROPE_THETA = 500000.0


def norm_transpose(sc, x, junk, ss, rstd, hb, ptp, idb, hT_out_ap, hT_buf):
    rms_rstd(sc, x, junk, ss, rstd)
    sc.op("dve", lambda en: en.tensor_scalar(out=hb[:], in0=x[:], scalar1=rstd[:], scalar2=None, op0=ALU.mult),
          reads=[x, rstd], writes=[hb])
    for kc in range(8):
        sc.op("pe", lambda en, kc=kc: en.transpose(out=ptp[:, kc * 128:(kc + 1) * 128],
                                                    in_=hb[:, kc * 128:(kc + 1) * 128], identity=idb[:]),
              reads=[hb, idb], writes=[ptp])
    sc.op("act", lambda en: en.copy(out=hT_out_ap, in_=ptp[:].rearrange("p (k t) -> p k t", k=8)),
          reads=[ptp], writes=[hT_buf])


def build_rope_tables(sc, cx, NT):
    half = 8
    inv_freq = np.power(np.float32(ROPE_THETA), -np.arange(half, dtype=np.float32) * np.float32(2.0 / 16)).astype(
        np.float32)
    pos = cx.sb("pos", [128, NT], F32)
    sc.op("pool", lambda en: en.iota(out=pos[:], pattern=[[128, NT]], base=0, channel_multiplier=1,
                                     allow_small_or_imprecise_dtypes=True), writes=[pos])
    ang = cx.sb("ang", [128, NT, 8], F32)
    for i in range(half):
        sc.op("dve", lambda en, i=i: en.tensor_scalar(out=ang[:, :, i], in0=pos[:], scalar1=float(inv_freq[i]),
                                                      scalar2=None, op0=ALU.mult), reads=[pos], writes=[ang])
    tabs = []
    for nm, shift in (("cos", math.pi / 2), ("sin", 0.0)):
        b = cx.sb("rb_" + nm, [128, NT, 8], F32)
        ki = cx.sb("rk_" + nm, [128, NT, 8], I32)
        kf = cx.sb("rf_" + nm, [128, NT, 8], F32)
        cr = cx.sb("rc_" + nm, [128, NT, 8], F32)
        tab = cx.sb("tab_" + nm, [128, NT, 8], F32)
        sc.op("dve", lambda en, b=b, shift=shift: en.tensor_scalar(out=b[:], in0=ang[:], scalar1=shift, scalar2=None,
                                                                   op0=ALU.add), reads=[ang], writes=[b])
        sc.op("dve", lambda en, b=b, ki=ki: en.tensor_scalar(out=ki[:], in0=b[:], scalar1=1.0 / (2 * math.pi),
                                                             scalar2=None, op0=ALU.mult), reads=[b], writes=[ki])
        sc.op("dve", lambda en, ki=ki, kf=kf: en.tensor_copy(out=kf[:], in_=ki[:]), reads=[ki], writes=[kf])
        sc.op("dve", lambda en, b=b, kf=kf: en.scalar_tensor_tensor(out=b[:], in0=kf[:], scalar=-2 * math.pi,
                                                                    in1=b[:], op0=ALU.mult, op1=ALU.add),
              reads=[kf, b], writes=[b])
        sc.op("dve", lambda en, b=b, cr=cr: en.tensor_scalar(out=cr[:], in0=b[:], scalar1=math.pi,
                                                             scalar2=-2 * math.pi, op0=ALU.is_gt, op1=ALU.mult),
              reads=[b], writes=[cr])
        sc.op("dve", lambda en, b=b, cr=cr: en.tensor_tensor(out=b[:], in0=b[:], in1=cr[:], op=ALU.add),
              reads=[b, cr], writes=[b])
        sc.op("dve", lambda en, b=b, cr=cr: en.tensor_scalar(out=cr[:], in0=b[:], scalar1=-math.pi,
                                                             scalar2=2 * math.pi, op0=ALU.is_lt, op1=ALU.mult),
              reads=[b], writes=[cr])
        sc.op("dve", lambda en, b=b, cr=cr: en.tensor_tensor(out=b[:], in0=b[:], in1=cr[:], op=ALU.add),
              reads=[b, cr], writes=[b])
        sc.op("act", lambda en, b=b, tab=tab: en.activation(out=tab[:], in_=b[:], func=AF.Sin), reads=[b],
              writes=[tab])
        tabs.append(tab)
    return tabs


def phase_proj(nc, sc, S, x1, qkv, gates, mix_norm, w_in, gate_bias):
    NT = S // 128
    cx = Ctx(nc)
    NC_ = 2304 + 2048
    ws = cx.sb("win", [128, 8, NC_], BF16)
    stg = [cx.sb("stg%d" % i, [128, 1152], F32) for i in range(2)]
    gcol = cx.sb("gcol", [128, 8], F32)
    idf, idb = make_ident(sc, cx, BF16)
    sc.dma("sp", gcol[:], mix_norm.rearrange("(kc p) -> p kc", p=128), writes=[gcol], allow_slow_non_contiguous=True)
    load_weight_bf16(sc, stg, ws, lambda kc, f0, fw: ws[:, kc, f0:f0 + fw], w_in[:, 0:2304], D, 2304, 1152, gcol)
    load_weight_bf16(sc, stg, ws, lambda kc, f0, fw: ws[:, kc, 2304 + f0:2304 + f0 + fw], w_in[:, OFF_G:OFF_G + 2048],
                     D, 2048, 1024, gcol)
    gb = cx.sb("gb", [128, 2048], F32)
    sc.dma("sp", gb[:], gate_bias.partition_broadcast(128), writes=[gb])
    cos_t, sin_t = build_rope_tables(sc, cx, NT)

    xs = [cx.sb("xs%d" % b, [128, D], F32) for b in range(2)]
    hb = cx.sb("hb", [128, D], BF16)
    hT = [cx.sb("hT%d" % b, [128, 8, 128], BF16) for b in range(2)]
    junk = cx.sb("junk", [128, D], F32)
    ss = cx.sb("ss", [128, 1], F32)
    rstd = cx.sb("rstd", [128, 1], F32)
    qs = [cx.sb("qs%d" % b, [128, 2304], BF16) for b in range(2)]
    gs = [cx.sb("gs%d" % b, [128, 2048], BF16) for b in range(2)]
    gf = cx.sb("gf", [128, 512], F32)
    rt = [cx.sb("rt%d" % i, [128, 24, 8], F32) for i in range(4)]
    ptp = cx.ps("ptp", [128, D], BF16)
    pp = [cx.ps("pp%d" % i, [128, 512], F32) for i in range(6)]
    pk = 0
    for ti in range(NT):
        b = ti % 2
        x = xs[b]
        sc.dma("sp", x[:], x1[ti * 128:(ti + 1) * 128, :], writes=[x])
        norm_transpose(sc, x, junk, ss, rstd, hb, ptp, idb, hT[b][:], hT[b])
        qk_ps = []
        for c0 in range(0, 2304, 512):
            cw = min(512, 2304 - c0)
            p = pp[pk % 6]
            pk += 1
            for kc in range(8):
                sc.op("pe", lambda en, p=p, kc=kc, c0=c0, cw=cw, b=b: en.matmul(
                    out=p[:, 0:cw], lhsT=hT[b][:, kc, :], rhs=ws[:, kc, c0:c0 + cw], start=(kc == 0), stop=(kc == 7)),
                      reads=[hT[b], ws], writes=[p])
            qk_ps.append((p, c0, cw))
        q_ = qs[b]
        for i, (p, c0, cw) in enumerate(qk_ps):
            eng = "act" if i % 2 == 0 else "dve"
            if eng == "act":
                sc.op("act", lambda en, p=p, c0=c0, cw=cw, q_=q_: en.copy(out=q_[:, c0:c0 + cw], in_=p[:, 0:cw]),
                      reads=[p], writes=[q_])
            else:
                sc.op("dve", lambda en, p=p, c0=c0, cw=cw, q_=q_: en.tensor_copy(out=q_[:, c0:c0 + cw],
                                                                                  in_=p[:, 0:cw]),
                      reads=[p], writes=[q_])
        cosb = cos_t[:, ti, :].unsqueeze(1).broadcast_to([128, 8, 8])
        sinb = sin_t[:, ti, :].unsqueeze(1).broadcast_to([128, 8, 8])
        for i in range(3):
            p = qk_ps[i][0]
            pv = p[:].rearrange("p (h d) -> p h d", h=8)
            qv = q_[:, i * 512:(i + 1) * 512].rearrange("p (h d) -> p h d", h=8)
            x1v = pv[:, :, 0:8]
            x2v = pv[:, :, 8:16]
            hs = slice(i * 8, (i + 1) * 8)
            sc.op("dve", lambda en, x1v=x1v, hs=hs, cosb=cosb: en.tensor_tensor(out=rt[0][:, hs, :], in0=x1v, in1=cosb,
                                                                      op=ALU.mult), reads=[p, cos_t], writes=[rt[0]])
            sc.op("dve", lambda en, x2v=x2v, hs=hs, sinb=sinb: en.tensor_tensor(out=rt[1][:, hs, :], in0=x2v, in1=sinb,
                                                                      op=ALU.mult), reads=[p, sin_t], writes=[rt[1]])
            sc.op("dve", lambda en, x2v=x2v, hs=hs, cosb=cosb: en.tensor_tensor(out=rt[2][:, hs, :], in0=x2v, in1=cosb,
                                                                      op=ALU.mult), reads=[p, cos_t], writes=[rt[2]])
            sc.op("dve", lambda en, x1v=x1v, hs=hs, sinb=sinb: en.tensor_tensor(out=rt[3][:, hs, :], in0=x1v, in1=sinb,
                                                                      op=ALU.mult), reads=[p, sin_t], writes=[rt[3]])
            sc.op("dve", lambda en, qv=qv, hs=hs: en.tensor_tensor(out=qv[:, :, 0:8], in0=rt[0][:, hs, :],
                                                                    in1=rt[1][:, hs, :], op=ALU.subtract),
                  reads=[rt[0], rt[1]], writes=[q_])
            sc.op("dve", lambda en, qv=qv, hs=hs: en.tensor_tensor(out=qv[:, :, 8:16], in0=rt[2][:, hs, :],
                                                                    in1=rt[3][:, hs, :], op=ALU.add),
                  reads=[rt[2], rt[3]], writes=[q_])
        sc.dma("pool", qkv[ti * 128:(ti + 1) * 128, :], q_[:], reads=[q_])
        g_ = gs[b]
        for c0 in range(0, 2048, 512):
            p = pp[pk % 6]
            pk += 1
            for kc in range(8):
                sc.op("pe", lambda en, p=p, kc=kc, c0=c0, b=b: en.matmul(
                    out=p[:], lhsT=hT[b][:, kc, :], rhs=ws[:, kc, 2304 + c0:2304 + c0 + 512], start=(kc == 0),
                    stop=(kc == 7)), reads=[hT[b], ws], writes=[p])
            sc.op("dve", lambda en, p=p, c0=c0: en.tensor_tensor(out=gf[:], in0=p[:], in1=gb[:, c0:c0 + 512],
                                                                  op=ALU.add), reads=[p, gb], writes=[gf])
            sc.op("act", lambda en, c0=c0, g_=g_: en.activation(out=g_[:, c0:c0 + 512], in_=gf[:], func=AF.Sigmoid),
                  reads=[gf], writes=[g_])
        sc.dma("pool", gates[ti * 128:(ti + 1) * 128, :], g_[:], reads=[g_])
    sc.barrier()
    sc.emit()
    cx.close()


ATT_GROUPS = ((128, 1), (512, 4), (2048, 16))
NEG = -30000.0


import os
ATT_LEVEL = int(os.environ.get('ATT_LEVEL', '9'))


def phase_attn(nc, sc, S, qkv, att):
    cx = Ctx(nc)
    idf, idb = make_ident(sc, cx, BF16)
    mask = cx.sb("mask", [128, 256], F32)
    mask0 = cx.sb("mask0", [128, 256], F32)
    sc.op("pool", lambda en: en.memset(mask[:], 0.0), writes=[mask])
    sc.op("pool", lambda en: en.affine_select(out=mask[:, 0:128], in_=mask[:, 0:128], pattern=[[1, 128]],
                                               compare_op=ALU.is_ge, fill=NEG, base=0, channel_multiplier=-1),
          reads=[mask], writes=[mask])
    sc.op("pool", lambda en: en.affine_select(out=mask[:, 128:256], in_=mask[:, 128:256], pattern=[[-1, 128]],
                                               compare_op=ALU.is_ge, fill=NEG, base=0, channel_multiplier=1),
          reads=[mask], writes=[mask])
    sc.op("pool", lambda en: en.tensor_copy(out=mask0[:], in_=mask[:]), reads=[mask], writes=[mask0])
    sc.op("pool", lambda en: en.memset(mask0[:, 0:128], NEG), reads=[mask0], writes=[mask0])

    qb = [cx.sb("qb%d" % i, [128, 256], BF16) for i in range(2)]
    kb = [cx.sb("kb%d" % i, [128, 256], BF16) for i in range(2)]
    vb = [cx.sb("vb%d" % i, [128, 256], BF16) for i in range(3)]
    QT = [cx.sb("QT%d" % i, [128, 2, 128], BF16) for i in range(2)]
    KT = [cx.sb("KT%d" % i, [128, 2, 128], BF16) for i in range(3)]
    for t in vb + KT:
        sc.op("pool", lambda en, t=t: en.memset(t[:], 0.0), writes=[t])
    sm = [cx.sb("sm%d" % i, [128, 2, 256], F32) for i in range(2)]
    m2 = [cx.sb("m2_%d" % i, [128, 2], F32) for i in range(2)]
    nb2 = [cx.sb("nb2_%d" % i, [128, 2], F32) for i in range(2)]
    pb = [cx.sb("pb%d" % i, [128, 256], BF16) for i in range(4)]
    PT = [cx.sb("PT%d" % i, [128, 2, 128], BF16) for i in range(4)]
    ob = [cx.sb("ob%d" % i, [128, 4, 66], F32) for i in range(2)]
    ptq = [cx.ps("ptq%d" % i, [128, 8, 128], BF16) for i in range(2)]
    sp = [cx.ps("sp%d" % i, [128, 2, 256], F32) for i in range(2)]
    ptp = [cx.ps("ptpp%d" % i, [128, 8, 128], BF16) for i in range(2)]
    po = [cx.ps("po%d" % i, [128, 8, 64], F32) for i in range(2)]

    blk = 0
    hk = 0
    for g, (window, d) in enumerate(ATT_GROUPS):
        L = S // d
        nb = L // 128
        qv = qkv.rearrange("(l d) c -> d l c", d=d)
        av = att.rearrange("(l d) g h e -> d l g (h e)", d=d)
        for r in range(d):
            for i in range(nb):
                rows = slice(i * 128, (i + 1) * 128)
                q_ = qb[blk % 2]
                k_ = kb[blk % 2]
                vcur = vb[blk % 3]
                vprev = vb[(blk - 1) % 3]
                kcur = KT[blk % 3]
                kprev = KT[(blk - 1) % 3]
                qt = QT[blk % 2]
                o_ = ob[blk % 2]
                po_ = po[blk % 2]
                sc.dma("sp", q_[:], qv[r, rows, 256 * g:256 * g + 256], writes=[q_])
                sc.dma("sp", k_[:], qv[r, rows, 768 + 256 * g:768 + 256 * g + 256], writes=[k_])
                sc.dma("sp", vcur[:], qv[r, rows, 1536 + 256 * g:1536 + 256 * g + 256], writes=[vcur])
                for src, dst in ((q_, qt), (k_, kcur)):
                    pt = ptq[hk % 2]
                    hk += 1
                    for pr in range(2):
                        sc.op("pe", lambda en, pt=pt, pr=pr, src=src: en.transpose(
                            out=pt[:, pr, :], in_=src[:, pr * 128:(pr + 1) * 128], identity=idb[:]),
                              reads=[src, idb], writes=[pt])
                    sc.op("act", lambda en, pt=pt, dst=dst: en.copy(out=dst[:], in_=pt[:, 0:2, :]), reads=[pt], writes=[dst])
                mk = mask0 if i == 0 else mask
                for pr in range(2):
                    s_ = sp[pr]
                    for hh in range(2):
                        ps_ = slice(64 * hh, 64 * hh + 64)
                        sc.op("pe", f_mm(s_[:, hh, 0:128], qt[ps_, pr, :], kprev[ps_, pr, :]), reads=[qt, kprev],
                              writes=[s_])
                        sc.op("pe", f_mm(s_[:, hh, 128:256], qt[ps_, pr, :], kcur[ps_, pr, :]), reads=[qt, kcur],
                              writes=[s_])
                for pr in range(2):
                    s_, sm_, m_, n_ = sp[pr], sm[pr], m2[pr], nb2[pr]
                    sc.op("dve", f_tt(sm_[:], s_[:], mk[:].unsqueeze(1).broadcast_to([128, 2, 256]), ALU.add),
                          reads=[s_, mk], writes=[sm_])
                    sc.op("dve", f_red(m_[:], sm_[:], ALU.max), reads=[sm_], writes=[m_])
                    sc.op("dve", f_ts(n_[:], m_[:], -0.125, ALU.mult), reads=[m_], writes=[n_])
                    sc.op("dve", f_ts(o_[:, 2 * pr:2 * pr + 2, 64], m_[:], 0.125, ALU.mult), reads=[m_], writes=[o_])
                for head in range(4):
                    pr, hh = head // 2, head % 2
                    sc.op("act", f_act(pb[head][:], sm[pr][:, hh, :], AF.Exp, bias=nb2[pr][:, hh:hh + 1], scale=0.125,
                                       accum=o_[:, head, 65:66]), reads=[sm[pr], nb2[pr]], writes=[pb[head], o_])
                for head in range(4):
                    tp = ptp[head % 2]
                    for half in range(2):
                        sc.op("pe", f_tr(tp[:, half, :], pb[head][:, half * 128:(half + 1) * 128], idb[:]),
                              reads=[pb[head], idb], writes=[tp])
                    sc.op("dve", f_copy(PT[head][:], tp[:, 0:2, :]), reads=[tp], writes=[PT[head]])
                for head in range(4):
                    pT = PT[head]
                    sc.op("pe", f_mm(po_[:, head, :], pT[:, 0, :], vprev[:, head * 64:(head + 1) * 64], True, False),
                          reads=[pT, vprev], writes=[po_])
                    sc.op("pe", f_mm(po_[:, head, :], pT[:, 1, :], vcur[:, head * 64:(head + 1) * 64], False, True),
                          reads=[pT, vcur], writes=[po_])
                sc.op("act", f_acopy(o_[:, :, 0:64], po_[:, 0:4, :]), reads=[po_], writes=[o_])
                sc.dma("pool", av[r, rows, g, :], o_[:].rearrange("p h e -> p (h e)"), reads=[o_])
                blk += 1
    sc.barrier()
    sc.emit()
    cx.close()


def f_tt(o, a, b, op):
    return lambda en: en.tensor_tensor(out=o, in0=a, in1=b, op=op)


def f_ts(o, a, s1, op0, s2=None, op1=None):
    if op1 is None:
        return lambda en: en.tensor_scalar(out=o, in0=a, scalar1=s1, scalar2=None, op0=op0)
    return lambda en: en.tensor_scalar(out=o, in0=a, scalar1=s1, scalar2=s2, op0=op0, op1=op1)


def f_stt(o, a, sca, b, op0, op1):
    return lambda en: en.scalar_tensor_tensor(out=o, in0=a, scalar=sca, in1=b, op0=op0, op1=op1)


def f_act(o, a, func, bias=None, scale=None, accum=None):
    kw = {}
    if bias is not None:
        kw["bias"] = bias
    if scale is not None:
        kw["scale"] = scale
    if accum is not None:
        kw["accum_out"] = accum
    return lambda en: en.activation(out=o, in_=a, func=func, **kw)


def f_acopy(o, a):
    return lambda en: en.copy(out=o, in_=a)


def f_copy(o, a):
    return lambda en: en.tensor_copy(out=o, in_=a)


def f_red(o, a, op):
    return lambda en: en.tensor_reduce(out=o, in_=a, axis=AX.X, op=op)


def f_mm(o, l, r, st=True, sp=True):
    fn = lambda en: en.matmul(out=o, lhsT=l, rhs=r, start=st, stop=sp)
    fn.rt = (l.base_partition(), l.partition_size())
    return fn


def f_tr(o, a, ident):
    fn = lambda en: en.transpose(out=o, in_=a, identity=ident)
    fn.rt = (a.base_partition(), a.partition_size())
    return fn


def f_memset(o, v):
    return lambda en: en.memset(o, v)


GN_EPS = 64e-5
VEC_NAMES = ["rwkv_w0", "rwkv_a0", "rwkv_k_k", "rwkv_k_a", "rwkv_r_k", "rwkv_ln_w", "rwkv_ln_b"]


def phase_rwkv(nc, sc, S, x1, outr, W, pd):
    NT = S // 128
    cx = Ctx(nc)
    op = sc.op
    idf, idb = make_ident(sc, cx, BF16)
    wr = cx.sb("wr", [128, 8, RS], BF16)
    stg = [cx.sb("stg%d" % i, [128, 1120], F32) for i in range(1)]
    gcol = cx.sb("gcol", [128, 8], F32)
    sc.dma("sp", gcol[:], W["mix_norm"].rearrange("(kc p) -> p kc", p=128), writes=[gcol],
           allow_slow_non_contiguous=True)
    load_weight_bf16(sc, stg, wr, lambda kc, f0, fw: wr[:, kc, f0:f0 + fw], W["w_in"][:, OFF_R:OFF_R + RS], D, RS,
                     1120, gcol)
    w2a2 = cx.sb("w2a2", [128, D], BF16)
    g2s = cx.sb("g2s", [128, 2, D], BF16)
    for (dst_ap, src, rows) in ((w2a2[0:64, :], W["rwkv_w2"], 64), (w2a2[64:128, :], W["rwkv_a2"], 64),
                                (g2s[:, 0, :], W["rwkv_g2"][0:128, :], 128), (g2s[0:32, 1, :], W["rwkv_g2"][128:160, :], 32)):
        s_ = stg[0]
        base = 64 if dst_ap is not None and rows == 64 and src is W["rwkv_a2"] else 0
        sc.dma("sp", s_[base:base + rows, 0:D], src, writes=[s_])
        op("dve", f_copy(dst_ap, s_[base:base + rows, 0:D]), reads=[s_], writes=[w2a2, g2s])
    vecs = cx.sb("vecs", [8, D], F32)
    op("dve", f_memset(vecs[:], 0.0), writes=[vecs])
    for j, nm in enumerate(VEC_NAMES):
        sc.dma("sp", vecs[j:j + 1, :], W[nm].rearrange("(o n) -> o n", o=1), writes=[vecs])
    muv = cx.sb("muv", [8, 512], F32)
    op("dve", f_memset(muv[:], 0.0), writes=[muv])
    sc.dma("sp", muv[0:6, :], W["rwkv_mu"][0:3072].rearrange("(j c) -> j c", c=512), reads=[], writes=[muv])
    sc.dma("sp", muv[6:7, 0:288], W["rwkv_mu"][3072:3360].rearrange("(o n) -> o n", o=1), writes=[muv])
    selm = cx.sb("selm", [8, 8, 128], F32)
    op("pool", f_memset(selm[:], 1.0), writes=[selm])
    op("pool", lambda en: en.affine_select(out=selm[:], in_=selm[:], pattern=[[-1, 8], [0, 128]],
                                           compare_op=ALU.is_equal, fill=0.0, base=0, channel_multiplier=1),
       reads=[selm], writes=[selm])
    selb = cx.sb("selb", [8, 8, 128], BF16)
    vhi = cx.sb("vhi", [8, D], BF16)
    vlo = cx.sb("vlo", [8, D], BF16)
    mhi = cx.sb("mhi", [8, 512], BF16)
    mlo = cx.sb("mlo", [8, 512], BF16)
    op("dve", f_copy(selb[:], selm[:]), reads=[selm], writes=[selb])
    for (src, hi, lo) in ((vecs, vhi, vlo), (muv, mhi, mlo)):
        op("dve", f_copy(hi[:], src[:]), reads=[src], writes=[hi])
        op("dve", f_tt(lo[:], src[:], hi[:], ALU.subtract), reads=[src, hi], writes=[lo])
    m_ui = cx.sb("m_ui", [128, 128], F32)
    m_su = cx.sb("m_su", [128, 128], F32)
    m_sl = cx.sb("m_sl", [128, 128], F32)
    mask4 = cx.sb("mask4", [128, 512], F32)
    ech = cx.sb("ech", [128, 2], F32)
    for m, pat, cm, cmp_ in ((m_ui, [[1, 128]], -1, ALU.is_ge), (m_su, [[1, 128]], -1, ALU.is_gt),
                             (m_sl, [[-1, 128]], 1, ALU.is_gt)):
        op("pool", f_memset(m[:], 1.0), writes=[m])
        op("pool", lambda en, m=m, pat=pat, cm=cm, cmp_=cmp_: en.affine_select(
            out=m[:], in_=m[:], pattern=pat, compare_op=cmp_, fill=0.0, base=0, channel_multiplier=cm),
           reads=[m], writes=[m])
        op("pool", f_memset(m[0:64, 64:128], 0.0), reads=[m], writes=[m])
        op("pool", f_memset(m[64:128, 0:64], 0.0), reads=[m], writes=[m])
    for i, m in enumerate((m_su, m_ui, m_su, m_ui)):
        op("pool", f_copy(mask4[:, i * 128:(i + 1) * 128], m[:]), reads=[m], writes=[mask4])
    op("pool", f_memset(ech[:], 0.0), writes=[ech])
    op("pool", f_memset(ech[0:64, 0:1], 1.0), reads=[ech], writes=[ech])
    op("pool", f_memset(ech[64:128, 1:2], 1.0), reads=[ech], writes=[ech])

    xs = [cx.sb("xs%d" % b, [128, D], F32) for b in range(1)]
    hb = cx.sb("hb", [128, D], BF16)
    hTc = cx.sb("hTc", [128, 8, 128], BF16)
    ss = cx.sb("ss", [128, 1], F32)
    rstd = cx.sb("rstd", [128, 1], F32)
    z = cx.sb("z", [128, RS], F32)
    junkb = cx.sb("junkb", [128, D], BF16)
    bon = cx.sb("bon", [128, D], BF16)
    zq = [cx.sb("zq%d" % i, [128, 512], F32) for i in range(2)]
    zblk = [Buf("zblk%d" % i, z.t) for i in range(7)]
    pdb = [Buf("pd%d" % i, None) for i in range(7)]
    op("pool", f_memset(zq[0][:], 0.0), writes=[zq[0]])
    for cb in range(7):
        c0 = cb * 512
        cw = min(512, RS - c0)
        sc.dma("sp", pd[0:1, c0:c0 + cw], zq[0][0:1, 0:cw], reads=[zq[0]], writes=[pdb[cb]])
    lor = cx.sb("lor", [128, 288], BF16)
    lorT = cx.sb("lorT", [128, 3, 128], BF16)
    TM = {n: cx.sb("tm_" + n, [128, D], BF16 if n in ("At", "Bt", "Kt", "Rt", "vb") else F32)
          for n in ("lw", "a", "g", "kk", "kp", "E", "At", "Bt", "Kt", "Rt", "t1", "vb")}
    arT = cx.sb("arT", [128, 8, 256], BF16)
    bkT = cx.sb("bkT", [128, 8, 256], BF16)
    Hs = cx.sb("Hs", [64, 16, 64], F32)
    op("pool", f_memset(Hs[:], 0.0), writes=[Hs])
    Hhi = cx.sb("Hhi", [64, 16, 64], BF16)
    Hlo = cx.sb("Hlo", [64, 16, 64], BF16)
    op("pool", f_memset(Hhi[:], 0.0), writes=[Hhi])
    op("pool", f_memset(Hlo[:], 0.0), writes=[Hlo])
    Wc = cx.sb("Wc", [64, 32], F32)
    sm16 = {n: cx.sb("s16_" + n, [128, 16], F32) for n in ("ssq", "rn", "rks", "mu", "vs", "rs")}
    HG = 8
    M4 = [cx.sb("M4_%d" % i, [128, 512], BF16) for i in range(HG)]
    PP = [[cx.sb("PP%d_%d" % (i, j), [128, 256], BF16) for j in range(2)] for i in range(HG)]
    TTb = [[cx.sb("TT%d_%d" % (i, j), [128, 128], BF16) for j in range(2)] for i in range(HG)]
    Zs = [cx.sb("Zs%d" % i, [128, 64], BF16) for i in range(HG)]
    AU = [cx.sb("AU%d" % i, [128, 128], BF16) for i in range(HG)]
    QT = [cx.sb("QTs%d" % i, [64, 128], BF16) for i in range(HG)]
    GTs = cx.sb("GTs", [64, HG, 2, 64], BF16)
    NW = cx.sb("NW", [64, HG, 2, 64], F32)
    Y0s = cx.sb("Y0s", [64, HG, 128], F32)
    orb = [cx.sb("orb%d" % i, [128, D], BF16) for i in range(1)]
    ptp = cx.ps("ptp", [128, D], BF16)
    ptp2 = cx.ps("ptp2", [128, D], BF16)
    ring = [cx.ps("pr%d" % i, [128, 512], F32) for i in range(4)]
    py = [cx.ps("py%d" % i, [128, 512], F32) for i in range(2)]
    rk = [0]

    def nxt():
        p = ring[rk[0] % len(ring)]
        rk[0] += 1
        return p

    def bc_vec(j, hf):
        p = nxt()
        op("pe", f_mm(p[:], selb[0:8, j, :], vhi[0:8, hf * 512:(hf + 1) * 512], True, False), reads=[selb, vhi],
           writes=[p])
        op("pe", f_mm(p[:], selb[0:8, j, :], vlo[0:8, hf * 512:(hf + 1) * 512], False, True), reads=[selb, vlo],
           writes=[p])
        return p

    zr, zk, zv = z[:, 0:D], z[:, D:2 * D], z[:, 2 * D:3 * D]

    def v3(ap):
        return ap.rearrange("p (h d) -> p h d", h=16)

    def b16(buf):
        return buf[:].unsqueeze(2).broadcast_to([128, 16, 64])

    def front1(ti):
        x = xs[0]
        sc.dma("sp", x[:], x1[ti * 128:(ti + 1) * 128, :], writes=[x])
        norm_transpose(sc, x, junkb, ss, rstd, hb, ptp, idb, hTc[:], hTc)
        for zb in zblk:
            zb.lw, zb.rd = z.lw, dict(z.rd)
        yield
        pend = None

        def mix(cb, c0, cw, q, pM):
            op("dve", f_tt(q[:, 0:cw], q[:, 0:cw], z[:, c0:c0 + cw], ALU.subtract), reads=[q, zblk[cb]], writes=[q])
            op("dve", f_tt(q[:, 0:cw], q[:, 0:cw], pM[:, 0:cw], ALU.mult), reads=[q, pM], writes=[q])
            op("pool", f_tt(z[:, c0:c0 + cw], z[:, c0:c0 + cw], q[:, 0:cw], ALU.add), reads=[zblk[cb], q],
               writes=[zblk[cb]])

        for cb in range(7):
            c0 = cb * 512
            cw = min(512, RS - c0)
            pP = nxt()
            pM = nxt()
            for kc in range(8):
                op("pe", f_mm(pP[:, 0:cw], hTc[:, kc, :], wr[:, kc, c0:c0 + cw], kc == 0, kc == 7), reads=[hTc, wr],
                   writes=[pP])
            op("pe", f_mm(pM[:, 0:cw], selb[0:8, cb, :], mhi[0:8, 0:cw], True, False), reads=[selb, mhi], writes=[pM])
            op("pe", f_mm(pM[:, 0:cw], selb[0:8, cb, :], mlo[0:8, 0:cw], False, True), reads=[selb, mlo], writes=[pM])
            op("act", f_acopy(z[:, c0:c0 + cw], pP[:, 0:cw]), reads=[pP], writes=[zblk[cb]])
            sc.dma("pool", pd[1 + ti * 128:1 + (ti + 1) * 128, c0:c0 + cw], z[:, c0:c0 + cw], reads=[zblk[cb]],
                   writes=[pdb[cb]])
            q = zq[cb % 2]
            sc.dma("sp", q[:, 0:cw], pd[ti * 128:(ti + 1) * 128, c0:c0 + cw], reads=[pdb[cb]], writes=[q])
            if pend is not None:
                mix(*pend)
            pend = (cb, c0, cw, q, pM)
            yield
        mix(*pend)
        z.lw, z.rd = ("pool", sc.cnt["pool"]), {}
        yield

    nxt_front = front1(0)
    for _ in nxt_front:
        pass
    for ti in range(NT):
        nxt_front = front1(ti + 1) if ti + 1 < NT else iter(())
        op("act", f_act(lor[:, 0:64], z[:, 3072:3136], AF.Tanh), reads=[z], writes=[lor])
        op("act", f_act(lor[:, 128:288], z[:, 3200:3360], AF.Sigmoid), reads=[z], writes=[lor])
        op("dve", f_copy(lor[:, 64:128], z[:, 3136:3200]), reads=[z], writes=[lor])
        op("pe", f_tr(ptp[:, 0:128], lor[:, 0:128], idb[:]), reads=[lor, idb], writes=[ptp])
        op("pe", f_tr(ptp[:, 128:256], lor[:, 128:256], idb[:]), reads=[lor, idb], writes=[ptp])
        op("pe", f_tr(ptp[0:32, 256:384], lor[:, 256:288], idb[:]), reads=[lor, idb], writes=[ptp])
        op("act", f_acopy(lorT[:, 0:2, :], ptp[:, 0:256].rearrange("p (a t) -> p a t", a=2)), reads=[ptp], writes=[lorT])
        op("act", f_acopy(lorT[0:32, 2, :], ptp[0:32, 256:384]), reads=[ptp], writes=[lorT])
        ysb = TM["a"]
        lw, a_, g_, kk, kp, E, At, Bt, Kt, Rt, t1 = (TM[n] for n in ("lw", "a", "g", "kk", "kp", "E", "At", "Bt", "Kt",
                                                                      "Rt", "t1"))
        for hf in range(2):
            cs_ = slice(hf * 512, (hf + 1) * 512)
            p = nxt()
            op("pe", f_mm(p[:], lorT[0:64, 0, :], w2a2[0:64, cs_], True, False), reads=[lorT, w2a2], writes=[p])
            op("pe", f_mm(p[:], selb[0:8, 0, :], vhi[0:8, cs_], False, False), reads=[selb, vhi], writes=[p])
            op("pe", f_mm(p[:], selb[0:8, 0, :], vlo[0:8, cs_], False, True), reads=[selb, vlo], writes=[p])
            op("act", f_act(lw[:, cs_], p[:], AF.Sigmoid), reads=[p], writes=[lw])
            p = nxt()
            op("pe", f_mm(p[:], lorT[64:128, 0, :], w2a2[64:128, cs_], True, False), reads=[lorT, w2a2], writes=[p])
            op("pe", f_mm(p[:], selb[0:8, 1, :], vhi[0:8, cs_], False, False), reads=[selb, vhi], writes=[p])
            op("pe", f_mm(p[:], selb[0:8, 1, :], vlo[0:8, cs_], False, True), reads=[selb, vlo], writes=[p])
            op("act", f_act(a_[:, cs_], p[:], AF.Sigmoid), reads=[p], writes=[a_])
            p = nxt()
            op("pe", f_mm(p[:], lorT[:, 1, :], g2s[:, 0, cs_], True, False), reads=[lorT, g2s], writes=[p])
            op("pe", f_mm(p[:], lorT[0:32, 2, :], g2s[0:32, 1, cs_], False, True), reads=[lorT, g2s], writes=[p])
            op("act", f_acopy(g_[:, cs_], p[:]), reads=[p], writes=[g_])
        op("pool", f_ts(lw[:], lw[:], -math.exp(-0.5), ALU.mult), reads=[lw], writes=[lw])
        for hf in range(2):
            cs_ = slice(hf * 512, (hf + 1) * 512)
            p = bc_vec(2, hf)
            op("dve", f_tt(kk[:, cs_], zk[:, cs_], p[:], ALU.mult), reads=[z, p], writes=[kk])
            p = bc_vec(3, hf)
            op("dve", f_stt(t1[:, cs_], a_[:, cs_], -1.0, p[:], ALU.add, ALU.mult), reads=[a_, p], writes=[t1])
            p = bc_vec(4, hf)
            op("dve", f_tt(E[:, cs_], zr[:, cs_], p[:], ALU.mult), reads=[z, p], writes=[E])
        op("dve", f_stt(kp[:], t1[:], 1.0, zk, ALU.add, ALU.mult), reads=[t1, z], writes=[kp])
        op("pool", f_tt(t1[:], kk[:], kk[:], ALU.mult), reads=[kk], writes=[t1])
        op("dve", f_red(sm16["ssq"][:], v3(t1[:]), ALU.add), reads=[t1], writes=[sm16["ssq"]])
        op("act", f_act(sm16["rn"][:], sm16["ssq"][:], AF.Sqrt), reads=[sm16["ssq"]], writes=[sm16["rn"]])
        op("dve", f_ts(sm16["rn"][:], sm16["rn"][:], 1e-12, ALU.max), reads=[sm16["rn"]], writes=[sm16["rn"]])
        op("dve", lambda en: en.reciprocal(out=sm16["rn"][:], in_=sm16["rn"][:]), reads=[sm16["rn"]],
           writes=[sm16["rn"]])
        op("dve", f_tt(v3(kk[:]), v3(kk[:]), b16(sm16["rn"]), ALU.mult), reads=[kk, sm16["rn"]], writes=[kk])
        op("pool", f_tt(E[:], E[:], kp[:], ALU.mult), reads=[E, kp], writes=[E])
        op("dve", f_red(sm16["rks"][:], v3(E[:]), ALU.add), reads=[E], writes=[sm16["rks"]])
        op("dve", f_tt(v3(bon[:]), v3(zv), b16(sm16["rks"]), ALU.mult), reads=[z, sm16["rks"]], writes=[bon])
        pcs = []
        for hf in range(2):
            cs_ = slice(hf * 512, (hf + 1) * 512)
            p = nxt()
            op("pe", f_mm(p[:], m_ui[:], lw[:, cs_]), reads=[m_ui, lw], writes=[p])
            pcs.append(p)
            op("act", f_act(E[:, cs_], p[:], AF.Exp), reads=[p], writes=[E])
            op("dve", f_tt(Rt[:, cs_], zr[:, cs_], E[:, cs_], ALU.mult), reads=[z, E], writes=[Rt])
        op("pool", f_tt(t1[:], kk[:], a_[:], ALU.mult), reads=[kk, a_], writes=[t1])
        for hf in range(2):
            cs_ = slice(hf * 512, (hf + 1) * 512)
            p = pcs[hf]
            op("act", f_act(E[:, cs_], p[:], AF.Exp, scale=-1.0), reads=[p], writes=[E])
            op("dve", f_tt(Bt[:, cs_], t1[:, cs_], E[:, cs_], ALU.mult), reads=[t1, E], writes=[Bt])
            op("pool", f_tt(Kt[:, cs_], kp[:, cs_], E[:, cs_], ALU.mult), reads=[kp, E], writes=[Kt])
            op("dve", f_tt(At[:, cs_], p[:], lw[:, cs_], ALU.subtract), reads=[p, lw], writes=[At])
        op("act", f_act(E[:], At[:], AF.Exp), reads=[At], writes=[E])
        op("dve", f_stt(At[:], kk[:], -1.0, E[:], ALU.mult, ALU.mult), reads=[kk, E], writes=[At])
        p = nxt()
        for h in range(16):
            op("pe", f_mm(p[0:64, 2 * h:2 * h + 2], lw[:, 64 * h:64 * h + 64], ech[:]), reads=[lw, ech], writes=[p])
        op("act", f_act(Wc[:], p[0:64, 0:32], AF.Exp), reads=[p], writes=[Wc])
        op("pool", f_copy(TM["vb"][:], zv), reads=[z], writes=[TM["vb"]])
        for pr in range(8):
            p = (ptp, ptp2)[pr % 2]
            cs_ = slice(pr * 128, (pr + 1) * 128)
            for i, src in enumerate((At, Rt, Bt, Kt)):
                op("pe", f_tr(p[:, i * 128:(i + 1) * 128], src[:, cs_], idb[:]), reads=[src, idb], writes=[p])
            op("act", f_acopy(arT[:, pr, :], p[:, 0:256]), reads=[p], writes=[arT])
            op("dve", f_copy(bkT[:, pr, :], p[:, 256:512]), reads=[p], writes=[bkT])
        for _ in nxt_front:
            pass
        for hg in range(16 // HG):
            heads = [hg * HG + i for i in range(HG)]
            info = []
            for i, h in enumerate(heads):
                pr, hb_ = h // 2, 64 * (h % 2)
                ps_ = slice(hb_, hb_ + 64)
                hc = slice(64 * h, 64 * h + 64)
                info.append((i, h, pr, ps_, hc))
            for (i, h, pr, ps_, hc) in info:
                pa = nxt()
                pb = nxt()
                op("pe", f_mm(pa[:, 0:128], arT[ps_, pr, 0:128], bkT[ps_, pr, 0:128]), reads=[arT, bkT], writes=[pa])
                op("pe", f_mm(pb[:, 0:256], bkT[ps_, pr, 0:128], arT[ps_, pr, 0:256]), reads=[arT, bkT], writes=[pb])
                op("pe", f_mm(pb[:, 256:512], bkT[ps_, pr, 128:256], arT[ps_, pr, 0:256]), reads=[arT, bkT],
                   writes=[pb])
                op("dve", f_tt(PP[i][0][:, 0:128], pa[:, 0:128], m_sl[:], ALU.mult), reads=[pa, m_sl],
                   writes=[PP[i][0]])
                op("dve", f_tt(M4[i][:], pb[:], mask4[:], ALU.mult), reads=[pb, mask4], writes=[M4[i]])
                op("pool", f_copy(PP[i][0][:, 128:256], M4[i][:, 0:128]), reads=[M4[i]], writes=[PP[i][0]])
                op("pool", f_tt(TTb[i][0][:], M4[i][:, 0:128], idb[:], ALU.add), reads=[M4[i], idb],
                   writes=[TTb[i][0]])
            for j in range(5):
                cur, nx = j % 2, (j + 1) % 2
                pcl = {}
                for (i, h, pr, ps_, hc) in info:
                    pc = nxt()
                    pcl[i] = pc
                    op("pe", f_mm(pc[:, 0:128], PP[i][cur][:, 128:256], PP[i][cur][:, 0:128]), reads=[PP[i][cur]],
                       writes=[pc])
                    if j < 4:
                        op("pe", f_mm(pc[:, 128:256], PP[i][cur][:, 0:128], PP[i][cur][:, 128:256]),
                           reads=[PP[i][cur]], writes=[pc])
                    wdt = 256 if j < 4 else 128
                    op("act", f_acopy(PP[i][nx][:, 0:wdt], pc[:, 0:wdt]), reads=[pc], writes=[PP[i][nx]])
                for (i, h, pr, ps_, hc) in info:
                    pc = pcl[i]
                    op("pe", f_mm(pc[:, 256:384], PP[i][nx][:, 0:128], TTb[i][cur][:]), reads=[PP[i][nx], TTb[i][cur]],
                       writes=[pc])
                    op("dve", f_tt(TTb[i][nx][:], pc[:, 256:384], TTb[i][cur][:], ALU.add),
                       reads=[pc, TTb[i][cur]], writes=[TTb[i][nx]])
            pzl = {}
            for (i, h, pr, ps_, hc) in info:
                TT = TTb[i][1]
                pz = nxt()
                pzl[i] = pz
                vh = TM["vb"][:, 64 * h:64 * h + 64]
                op("pe", f_mm(pz[:, 0:64], M4[i][:, 256:384], vh), reads=[M4[i], TM["vb"]], writes=[pz])
                op("act", f_acopy(Zs[i][:], pz[:, 0:64]), reads=[pz], writes=[Zs[i]])
            for (i, h, pr, ps_, hc) in info:
                TT = TTb[i][1]
                pz = pzl[i]
                op("pe", f_mm(pz[:, 128:192], TT[:], At[:, hc]), reads=[TT, At], writes=[pz])
                op("pe", f_mm(pz[:, 192:256], TT[:], Zs[i][:]), reads=[TT, Zs[i]], writes=[pz])
                op("act", f_acopy(AU[i][:], pz[:, 128:256]), reads=[pz], writes=[AU[i]])
            for (i, h, pr, ps_, hc) in info:
                pz = pzl[i]
                op("pe", f_mm(pz[0:64, 256:384], AU[i][:, 0:64], M4[i][:, 128:256], True, False),
                   reads=[AU[i], M4[i]], writes=[pz])
                op("pe", f_mm(pz[0:64, 256:384], Rt[:, hc], idb[:], False, True), reads=[Rt, idb], writes=[pz])
                op("act", f_acopy(QT[i][:], pz[0:64, 256:384]), reads=[pz], writes=[QT[i]])
            for q0 in range(0, HG, 4):
                quad = info[q0:q0 + 4]
                pN, pG, pY = nxt(), nxt(), nxt()
                for c in range(2):
                    cs_ = slice(64 * c, 64 * c + 64)
                    for (i, h, pr, ps_, hc) in quad:
                        j = i - q0
                        vh = TM["vb"][:, 64 * h:64 * h + 64]
                        col = 64 * (2 * j + c)
                        op("pe", f_mm(pN[0:64, col:col + 64], Bt[cs_, hc], AU[i][cs_, 64:128], True, False),
                           reads=[Bt, AU[i]], writes=[pN])
                        op("pe", f_mm(pN[0:64, col:col + 64], Kt[cs_, hc], vh[cs_, :], False, True),
                           reads=[Kt, TM["vb"]], writes=[pN])
                        op("pe", f_mm(pG[0:64, col:col + 64], AU[i][cs_, 0:64], Bt[cs_, hc]), reads=[AU[i], Bt],
                           writes=[pG])
                for (i, h, pr, ps_, hc) in quad:
                    j = i - q0
                    vh = TM["vb"][:, 64 * h:64 * h + 64]
                    yo = pY[0:64, 128 * j:128 * j + 128]
                    op("pe", f_mm(yo, AU[i][:, 64:128], M4[i][:, 128:256], True, False), reads=[AU[i], M4[i]],
                       writes=[pY])
                    op("pe", f_mm(yo, vh, M4[i][:, 384:512], False, True), reads=[TM["vb"], M4[i]], writes=[pY])
                for (i, h, pr, ps_, hc) in quad:
                    j = i - q0
                    for c in range(2):
                        col = 64 * (2 * j + c)
                        op("act", f_act(NW[0:64, i, c, :], pN[0:64, col:col + 64], AF.Copy,
                                        scale=Wc[0:64, 2 * h + c:2 * h + c + 1]), reads=[pN, Wc], writes=[NW])
                op("dve", f_tt(GTs[0:64, q0:q0 + 4, :, :].rearrange("p a c k -> p (a c) k"),
                               pG[0:64, :].rearrange("p (a k) -> p a k", a=8),
                               idf[0:64, 0:64].unsqueeze(1).broadcast_to([64, 8, 64]), ALU.add), reads=[pG, idf],
                   writes=[GTs])
                op("act", f_acopy(Y0s[0:64, q0:q0 + 4, :], pY[0:64, :].rearrange("p (a t) -> p a t", a=4)),
                   reads=[pY], writes=[Y0s])
            for c in range(2):
                cs_ = slice(64 * c, 64 * c + 64)
                pS = py[c]
                pH = nxt()
                for (i, h, pr, ps_, hc) in info:
                    op("pe", f_mm(pS[0:64, 64 * i:64 * i + 64], Hhi[0:64, h, :], QT[i][:, cs_], True, False),
                       reads=[Hhi, QT[i]], writes=[pS])
                    op("pe", f_mm(pS[0:64, 64 * i:64 * i + 64], Hlo[0:64, h, :], QT[i][:, cs_], False, True),
                       reads=[Hlo, QT[i]], writes=[pS])
                for (i, h, pr, ps_, hc) in info:
                    op("pe", f_mm(pH[0:64, 64 * i:64 * i + 64], GTs[0:64, i, c, :], Hhi[0:64, h, :], True, False),
                       reads=[GTs, Hhi], writes=[pH])
                    op("pe", f_mm(pH[0:64, 64 * i:64 * i + 64], GTs[0:64, i, c, :], Hlo[0:64, h, :], False, True),
                       reads=[GTs, Hlo], writes=[pH])
                for (i, h, pr, ps_, hc) in info:
                    op("dve", f_stt(Hs[0:64, h, :], pH[0:64, 64 * i:64 * i + 64], Wc[0:64, 2 * h + c:2 * h + c + 1],
                                    NW[0:64, i, c, :], ALU.mult, ALU.add), reads=[pH, Wc, NW], writes=[Hs])
                hsl = slice(HG * hg, HG * hg + HG)
                op("pool", f_copy(Hhi[0:64, hsl, :], Hs[0:64, hsl, :]), reads=[Hs], writes=[Hhi])
                op("pool", f_tt(Hlo[0:64, hsl, :], Hs[0:64, hsl, :], Hhi[0:64, hsl, :], ALU.subtract), reads=[Hs, Hhi],
                   writes=[Hlo])
                op("dve", f_tt(Y0s[0:64, :, cs_], Y0s[0:64, :, cs_],
                               pS[0:64, :].rearrange("p (a t) -> p a t", a=8), ALU.add), reads=[Y0s, pS],
                   writes=[Y0s])
            pT = nxt()
            for (i, h, pr, ps_, hc) in info:
                op("pe", f_tr(pT[:, 64 * i:64 * i + 64], Y0s[0:64, i, :], idf[0:64, 0:64]), reads=[Y0s, idf],
                   writes=[pT])
            op("act", f_acopy(ysb[:, 512 * hg:512 * hg + 512], pT[:]), reads=[pT], writes=[ysb])
            for _ in range(4):
                next(nxt_front, None)
        for _ in nxt_front:
            pass
        q16 = sm16
        op("dve", f_red(q16["mu"][:], v3(ysb[:]), ALU.add), reads=[ysb], writes=[q16["mu"]])
        op("dve", f_ts(q16["mu"][:], q16["mu"][:], -1.0 / 64, ALU.mult), reads=[q16["mu"]], writes=[q16["mu"]])
        op("dve", f_tt(v3(ysb[:]), v3(ysb[:]), b16(q16["mu"]), ALU.add), reads=[ysb, q16["mu"]], writes=[ysb])
        op("pool", f_tt(t1[:], ysb[:], ysb[:], ALU.mult), reads=[ysb], writes=[t1])
        op("dve", f_red(q16["vs"][:], v3(t1[:]), ALU.add), reads=[t1], writes=[q16["vs"]])
        op("act", f_act(q16["rs"][:], q16["vs"][:], AF.Sqrt, bias=GN_EPS, scale=1.0 / 64), reads=[q16["vs"]],
           writes=[q16["rs"]])
        op("dve", lambda en: en.reciprocal(out=q16["rs"][:], in_=q16["rs"][:]), reads=[q16["rs"]],
           writes=[q16["rs"]])
        op("dve", f_tt(v3(ysb[:]), v3(ysb[:]), b16(q16["rs"]), ALU.mult), reads=[ysb, q16["rs"]], writes=[ysb])
        for hf in range(2):
            cs_ = slice(hf * 512, (hf + 1) * 512)
            p = bc_vec(5, hf)
            op("dve", f_tt(ysb[:, cs_], ysb[:, cs_], p[:], ALU.mult), reads=[ysb, p], writes=[ysb])
            p = bc_vec(6, hf)
            op("dve", f_tt(ysb[:, cs_], ysb[:, cs_], p[:], ALU.add), reads=[ysb, p], writes=[ysb])
        op("pool", f_tt(ysb[:], ysb[:], bon[:], ALU.add), reads=[ysb, bon], writes=[ysb])
        o_ = orb[0]
        op("dve", f_tt(o_[:], ysb[:], g_[:], ALU.mult), reads=[ysb, g_], writes=[o_])
        sc.dma("pool", outr[ti * 128:(ti + 1) * 128, :], o_[:], reads=[o_])
    sc.barrier()
    sc.emit()
    cx.close()


def phase_merge(nc, sc, S, x1, x2, gates, att, outr, W):
    NT = S // 128
    cx = Ctx(nc)
    op = sc.op
    idf, idb = make_ident(sc, cx, BF16)
    wout = cx.sb("wout", [128, 8, D], BF16)
    wo = cx.sb("wo", [128, 8, D], BF16)
    wup = cx.sb("wup", [128, 2, D], BF16)
    stg = [cx.sb("stg%d" % i, [128, D], F32) for i in range(2)]
    load_weight_bf16(sc, stg, wout, lambda kc, f0, fw: wout[:, kc, f0:f0 + fw], W["rwkv_w_out"], D, D, D)
    load_weight_bf16(sc, stg, wo, lambda kc, f0, fw: wo[:, kc, f0:f0 + fw], W["w_o"], D, D, D)
    load_weight_bf16(sc, stg, wup, lambda kc, f0, fw: wup[:, kc, f0:f0 + fw], W["attn_w_up"], 256, D, D)
    xs = [cx.sb("xs%d" % b, [128, D], F32) for b in range(2)]
    gt = [cx.sb("gt%d" % b, [128, 2 * D], BF16) for b in range(2)]
    at = [cx.sb("at%d" % b, [128, 3, 4, 66], F32) for b in range(2)]
    orr = [cx.sb("orr%d" % b, [128, D], BF16) for b in range(2)]
    orT = cx.sb("orT", [128, 8, 128], BF16)
    mgT = cx.sb("mgT", [128, 8, 128], BF16)
    oaT = cx.sb("oaT", [128, 2, 128], BF16)
    mx = cx.sb("mx", [128, 4], F32)
    cc = cx.sb("cc", [128, 3, 4], F32)
    den = cx.sb("den", [128, 4], F32)
    dtmp = cx.sb("dtmp", [128, 4], F32)
    num = cx.sb("num", [128, 4, 64], F32)
    ntmp = cx.sb("ntmp", [128, 4, 64], F32)
    oab = cx.sb("oab", [128, 256], BF16)
    ta = cx.sb("ta", [128, D], F32)
    tb = cx.sb("tb", [128, D], F32)
    mgb = cx.sb("mgb", [128, D], BF16)
    ptp = cx.ps("ptp", [128, D], BF16)
    pya = [cx.ps("pya%d" % i, [128, 512], F32) for i in range(2)]
    pyr = [cx.ps("pyr%d" % i, [128, 512], F32) for i in range(2)]
    pout = [cx.ps("pout%d" % i, [128, 512], F32) for i in range(2)]
    for ti in range(NT):
        b = ti % 2
        rows = slice(ti * 128, (ti + 1) * 128)
        x, g_, a_, o_ = xs[b], gt[b], at[b], orr[b]
        sc.dma("sp", x[:], x1[rows, :], writes=[x])
        sc.dma("sp", g_[:], gates[rows, :], writes=[g_])
        sc.dma("sp", a_[:], att[rows], writes=[a_])
        sc.dma("sp", o_[:], outr[rows, :], writes=[o_])
        for kc in range(8):
            op("pe", f_tr(ptp[:, kc * 128:(kc + 1) * 128], o_[:, kc * 128:(kc + 1) * 128], idb[:]), reads=[o_, idb],
               writes=[ptp])
        op("act", f_acopy(orT[:], ptp[:].rearrange("p (k t) -> p k t", k=8)), reads=[ptp], writes=[orT])
        for hf in range(2):
            for kc in range(8):
                op("pe", f_mm(pyr[hf][:], orT[:, kc, :], wout[:, kc, hf * 512:(hf + 1) * 512], kc == 0, kc == 7),
                   reads=[orT, wout], writes=[pyr[hf]])
        m0, m1, m2_ = (a_[:, g, :, 64] for g in range(3))
        op("dve", f_tt(mx[:], m0, m1, ALU.max), reads=[a_], writes=[mx])
        op("dve", f_tt(mx[:], mx[:], m2_, ALU.max), reads=[a_, mx], writes=[mx])
        for g in range(3):
            op("dve", f_tt(cc[:, g, :], a_[:, g, :, 64], mx[:], ALU.subtract), reads=[a_, mx], writes=[cc])
        op("act", f_act(cc[:], cc[:], AF.Exp), reads=[cc], writes=[cc])
        for g in range(3):
            if g == 0:
                op("dve", f_tt(den[:], cc[:, 0, :], a_[:, 0, :, 65], ALU.mult), reads=[cc, a_], writes=[den])
                op("dve", f_tt(num[:], a_[:, 0, :, 0:64], cc[:, 0, :].unsqueeze(2).broadcast_to([128, 4, 64]),
                               ALU.mult), reads=[cc, a_], writes=[num])
            else:
                op("dve", f_tt(dtmp[:], cc[:, g, :], a_[:, g, :, 65], ALU.mult), reads=[cc, a_], writes=[dtmp])
                op("dve", f_tt(den[:], den[:], dtmp[:], ALU.add), reads=[den, dtmp], writes=[den])
                op("dve", f_tt(ntmp[:], a_[:, g, :, 0:64], cc[:, g, :].unsqueeze(2).broadcast_to([128, 4, 64]),
                               ALU.mult), reads=[cc, a_], writes=[ntmp])
                op("pool", f_tt(num[:], num[:], ntmp[:], ALU.add), reads=[num, ntmp], writes=[num])
        op("dve", lambda en: en.reciprocal(out=den[:], in_=den[:]), reads=[den], writes=[den])
        op("dve", f_tt(oab[:].rearrange("p (h d) -> p h d", h=4), num[:],
                       den[:].unsqueeze(2).broadcast_to([128, 4, 64]), ALU.mult), reads=[num, den], writes=[oab])
        for kc in range(2):
            op("pe", f_tr(ptp[:, kc * 128:(kc + 1) * 128], oab[:, kc * 128:(kc + 1) * 128], idb[:]),
               reads=[oab, idb], writes=[ptp])
        op("act", f_acopy(oaT[:], ptp[:, 0:256].rearrange("p (k t) -> p k t", k=2)), reads=[ptp], writes=[oaT])
        for hf in range(2):
            for kc in range(2):
                op("pe", f_mm(pya[hf][:], oaT[:, kc, :], wup[:, kc, hf * 512:(hf + 1) * 512], kc == 0, kc == 1),
                   reads=[oaT, wup], writes=[pya[hf]])
        for hf in range(2):
            cs_ = slice(hf * 512, (hf + 1) * 512)
            op("dve", f_tt(ta[:, cs_], pya[hf][:], g_[:, hf * 512:(hf + 1) * 512], ALU.mult), reads=[pya[hf], g_],
               writes=[ta])
            op("dve", f_tt(tb[:, cs_], pyr[hf][:], g_[:, D + hf * 512:D + (hf + 1) * 512], ALU.mult),
               reads=[pyr[hf], g_], writes=[tb])
        op("pool", f_tt(mgb[:], ta[:], tb[:], ALU.add), reads=[ta, tb], writes=[mgb])
        for kc in range(8):
            op("pe", f_tr(ptp[:, kc * 128:(kc + 1) * 128], mgb[:, kc * 128:(kc + 1) * 128], idb[:]),
               reads=[mgb, idb], writes=[ptp])
        op("act", f_acopy(mgT[:], ptp[:].rearrange("p (k t) -> p k t", k=8)), reads=[ptp], writes=[mgT])
        for hf in range(2):
            cs_ = slice(hf * 512, (hf + 1) * 512)
            for kc in range(8):
                op("pe", f_mm(pout[hf][:], mgT[:, kc, :], wo[:, kc, cs_], kc == 0, kc == 7), reads=[mgT, wo],
                   writes=[pout[hf]])
            op("dve", f_tt(x[:, cs_], x[:, cs_], pout[hf][:], ALU.add), reads=[x, pout[hf]], writes=[x])
        sc.dma("pool", x2[rows, :], x[:], reads=[x])
    sc.barrier()
    sc.emit()
    cx.close()


def build_nc(S, stages="ABCDE", debug=False):
    nc = bass.Bass("TRN2", target_bir_lowering=False)

    def inp(name, shape):
        return nc.dram_tensor(name, list(shape), F32, kind="ExternalInput").ap()

    def scr(name, shape, dt):
        return nc.dram_tensor(name, list(shape), dt, kind="ExternalOutput" if debug else "Internal").ap()

    W = {}
    for name, shape in WEIGHT_SHAPES:
        W[name] = inp(name, shape)
    x = inp("x", [S, D])
    out = nc.dram_tensor("out", [S, D], F32, kind="ExternalOutput").ap()
    x1 = scr("x1_scr", [S, D], F32)
    x2 = scr("x2_scr", [S, D], F32)
    qkv = scr("qkv_scr", [S, 2304], BF16)
    gates = scr("gates_scr", [S, 2048], BF16)
    att = scr("att_scr", [S, 3, 4, 66], F32)
    outr = scr("outr_scr", [S, D], BF16)
    pd = scr("pd_scr", [S + 1, RS], F32)
    sc = Sched(nc)
    if "A" in stages:
        phase_ffn(nc, sc, S, x, x1, W["ffn1_norm"], W["ffn1_w_gate"], W["ffn1_w_up"], W["ffn1_w_down"])
    if "B" in stages:
        phase_proj(nc, sc, S, x1, qkv, gates, W["mix_norm"], W["w_in"], W["gate_bias"])
    if "C" in stages:
        phase_attn(nc, sc, S, qkv, att)
    if "D" in stages:
        phase_rwkv(nc, sc, S, x1, outr, W, pd)
        phase_merge(nc, sc, S, x1, x2, gates, att, outr, W)
    if "E" in stages:
        phase_ffn(nc, sc, S, x2 if "D" in stages else x1, out, W["ffn2_norm"], W["ffn2_w_gate"], W["ffn2_w_up"],
                  W["ffn2_w_down"], final_gain=W["final_norm"])
    sc.close()
    return nc


WEIGHT_SHAPES = [
    ("ffn1_norm", [D]), ("ffn1_w_gate", [D, FF]), ("ffn1_w_up", [D, FF]), ("ffn1_w_down", [FF, D]),
    ("mix_norm", [D]), ("w_in", [D, IN_COLS]), ("gate_bias", [2 * D]), ("attn_w_up", [256, D]),
    ("rwkv_mu", [RS]), ("rwkv_w0", [D]), ("rwkv_w2", [64, D]), ("rwkv_a0", [D]), ("rwkv_a2", [64, D]),
    ("rwkv_g2", [160, D]), ("rwkv_k_k", [D]), ("rwkv_k_a", [D]), ("rwkv_r_k", [D]), ("rwkv_ln_w", [D]),
    ("rwkv_ln_b", [D]), ("rwkv_w_out", [D, D]), ("w_o", [D, D]),
    ("ffn2_norm", [D]), ("ffn2_w_gate", [D, FF]), ("ffn2_w_up", [D, FF]), ("ffn2_w_down", [FF, D]),
    ("final_norm", [D]),
]


def make_in_maps(inputs):
    B = inputs["x"].shape[0]
    shared = {}
    for name, shape in WEIGHT_SHAPES:
        a = np.asarray(inputs[name], dtype=np.float32)
        shared[name] = np.ascontiguousarray(a.reshape(shape))
    xin = np.asarray(inputs["x"], dtype=np.float32)
    in_maps = []
    for c in range(B):
        m = dict(shared)
        m["x"] = np.ascontiguousarray(xin[c])
        in_maps.append(m)
    return in_maps


def kernel(**inputs):
    B, S, _ = inputs["x"].shape
    nc = build_nc(S)
    in_maps = make_in_maps(inputs)
    res = run_bass_kernel_spmd(nc, in_maps, core_ids=list(range(B)))
    return np.stack([np.asarray(r["out"]) for r in res.results], axis=0).astype(np.float32)
```

```python
import contextlib
import math
import numpy as np
import concourse.bass as bass
import concourse.mybir as mybir
from concourse.bass_utils import run_bass_kernel_spmd

F32 = mybir.dt.float32
BF16 = mybir.dt.bfloat16
AF = mybir.ActivationFunctionType
ALU = mybir.AluOpType
AX = mybir.AxisListType

D = 1024
FF = 2816
NFC = FF // 128
EPS = 1e-6
NCORES = 8


class Buf:
    __slots__ = ("name", "t", "lw", "rd", "excl")

    def __init__(self, name, t, excl=False):
        self.name = name
        self.t = t
        self.lw = None
        self.rd = {}
        self.excl = excl

    def __getitem__(self, idx):
        return self.t[idx]


class Sched:
    ENGS = ("pe", "act", "dve", "pool", "sp")

    def __init__(self, nc, ndma_sems=6):
        self.nc = nc
        self.ops = {e: [] for e in self.ENGS}
        self.cnt = {}
        self.sem = {}
        self.seen = {e: {} for e in self.ENGS}
        self._cm = []
        for e in self.ENGS:
            cm = nc.semaphore("prog_" + e)
            self.sem[e] = cm.__enter__()
            self._cm.append(cm)
            self.cnt[e] = 0
        self.dq = {}
        for q in ("sp", "pool", "act"):
            ring = []
            for i in range(ndma_sems):
                nm = "dma_%s_%d" % (q, i)
                cm = nc.semaphore(nm)
                self.sem[nm] = cm.__enter__()
                self._cm.append(cm)
                self.cnt[nm] = 0
                ring.append(nm)
            self.dq[q] = [ring, 0]
        self.ninstr = 0
        self.pe_rt = {}

    def close(self):
        for cm in reversed(self._cm):
            cm.__exit__(None, None, None)

    def _wait(self, engine, dep):
        if dep is None:
            return
        e, c = dep
        if self.seen[engine].get(e, 0) >= c:
            return
        self.seen[engine][e] = c
        sem = self.sem[e]
        self.ops[engine].append(lambda en, sem=sem, c=c: en.wait_ge(sem, c))
        self.ninstr += 1

    def _deps(self, engine, reads, writes):
        for b in reads:
            if b.lw is not None:
                if b.lw[0] == engine and engine == "pe":
                    continue
                self._wait(engine, b.lw)
        for b in writes:
            if b.lw is not None and not (b.lw[0] == engine and engine == "pe"):
                self._wait(engine, b.lw)
            for e, c in b.rd.items():
                if e == engine and engine == "pe":
                    continue
                self._wait(engine, (e, c))

    def op(self, engine, fn, reads=(), writes=()):
        ex = [b for b in reads if b.excl and engine != "pe"]
        if ex:
            writes = list(writes) + [b for b in ex if b not in writes]
        if engine == "pe":
            rt = getattr(fn, "rt", None)
            for b in writes:
                if b.lw is not None and b.lw[0] == "pe" and self.pe_rt.get(id(b)) != rt:
                    self._wait("pe", b.lw)
                self.pe_rt[id(b)] = rt
        self._deps(engine, reads, writes)
        self.cnt[engine] += 1
        c = self.cnt[engine]
        sem = self.sem[engine]
        self.ops[engine].append(lambda en, fn=fn, sem=sem: fn(en).then_inc(sem, 1))
        self.ninstr += 1
        for b in reads:
            b.rd[engine] = c
        for b in writes:
            b.lw = (engine, c)
            b.rd = {}
        return c

    def dma(self, q, out_ap, in_ap, reads=(), writes=(), **kw):
        ring, i = self.dq[q]
        nm = ring[i % len(ring)]
        self.dq[q][1] = i + 1
        if self.cnt[nm] > 0:
            self._wait(q, (nm, self.cnt[nm]))
        self._deps(q, reads, writes)
        self.cnt[nm] += 16
        c = self.cnt[nm]
        sem = self.sem[nm]
        self.ops[q].append(
            lambda en, o=out_ap, i_=in_ap, sem=sem, kw=kw: en.dma_start(out=o, in_=i_, **kw).then_inc(sem, 16))
        self.ninstr += 1
        for b in reads:
            b.rd[nm] = c
        for b in writes:
            b.lw = (nm, c)
            b.rd = {}

    def barrier(self):
        for e in self.ENGS:
            for s, c in self.cnt.items():
                if s != e and c > 0:
                    self._wait(e, (s, c))

    def emit(self):
        nc = self.nc
        ops = self.ops
        with nc.Block() as block:
            @block.tensor
            def _(en):
                for f in ops["pe"]:
                    f(en)

            @block.scalar
            def _(en):
                for f in ops["act"]:
                    f(en)

            @block.vector
            def _(en):
                for f in ops["dve"]:
                    f(en)

            @block.gpsimd
            def _(en):
                for f in ops["pool"]:
                    f(en)

            @block.sync
            def _(en):
                for f in ops["sp"]:
                    f(en)
        self.ops = {e: [] for e in self.ENGS}


class Ctx:
    N = [0]

    def __init__(self, nc):
        self.nc = nc
        self.es = contextlib.ExitStack()

    def sb(self, name, shape, dt):
        Ctx.N[0] += 1
        t = self.es.enter_context(self.nc.sbuf_tensor("%s_%d" % (name, Ctx.N[0]), list(shape), dt))
        return Buf(name, t)

    def ps(self, name, shape, dt=F32):
        Ctx.N[0] += 1
        nbytes = int(np.prod(shape[1:])) * (4 if dt == F32 else 2)
        assert nbytes == 2048, (name, shape)
        t = self.es.enter_context(self.nc.psum_tensor("%s_%d" % (name, Ctx.N[0]), list(shape), dt))
        return Buf(name, t, excl=True)

    def close(self):
        self.es.close()


def make_ident(sc, cx, dt):
    idf = cx.sb("identf", [128, 128], F32)
    idb = cx.sb("ident", [128, 128], dt)
    sc.op("pool", lambda en: en.memset(idf[:], 1.0), writes=[idf])
    sc.op("pool", lambda en: en.affine_select(out=idf[:], in_=idf[:], pattern=[[-1, 128]], compare_op=ALU.is_equal,
                                               fill=0.0, base=0, channel_multiplier=1), reads=[idf], writes=[idf])
    sc.op("dve", lambda en: en.tensor_copy(out=idb[:], in_=idf[:]), reads=[idf], writes=[idb])
    return idf, idb


def load_weight_bf16(sc, stg, dst, dst_idx_fn, w_ap, K, F, cw, scale_col=None, qi=[0]):
    nkc = K // 128
    for kc in range(nkc):
        for f0 in range(0, F, cw):
            fw = min(cw, F - f0)
            s = stg[qi[0] % len(stg)]
            q = "sp"
            sc.dma(q, s[:, 0:fw], w_ap[kc * 128:(kc + 1) * 128, f0:f0 + fw], writes=[s])
            eng = ("act", "dve", "pool")[qi[0] % 3]
            qi[0] += 1
            o = dst_idx_fn(kc, f0, fw)
            if scale_col is None:
                if eng == "act":
                    sc.op("act", lambda en, o=o, s=s, fw=fw: en.copy(out=o, in_=s[:, 0:fw]), reads=[s], writes=[dst])
                else:
                    sc.op(eng, lambda en, o=o, s=s, fw=fw: en.tensor_copy(out=o, in_=s[:, 0:fw]), reads=[s],
                          writes=[dst])
            else:
                sca = scale_col[:, kc:kc + 1]
                if eng == "act":
                    sc.op("act", lambda en, o=o, s=s, fw=fw, sca=sca: en.activation(out=o, in_=s[:, 0:fw],
                                                                                       func=AF.Copy, scale=sca),
                          reads=[s, scale_col], writes=[dst])
                else:
                    sc.op(eng, lambda en, o=o, s=s, fw=fw, sca=sca: en.tensor_scalar(
                        out=o, in0=s[:, 0:fw], scalar1=sca, scalar2=None, op0=ALU.mult),
                          reads=[s, scale_col], writes=[dst])


def rms_rstd(sc, x, junk, ss, rstd, eng_sq="act"):
    sc.op("act", lambda en: en.activation(out=junk[:], in_=x[:], func=AF.Square, accum_out=ss[:]),
          reads=[x], writes=[junk, ss])
    sc.op("act", lambda en: en.activation(out=rstd[:], in_=ss[:], func=AF.Sqrt, bias=EPS, scale=1.0 / D),
          reads=[ss], writes=[rstd])
    sc.op("dve", lambda en: en.reciprocal(out=rstd[:], in_=rstd[:]), reads=[rstd], writes=[rstd])


def phase_ffn(nc, sc, S, x_src, x_dst, gain, wg, wu, wd, final_gain=None):
    TF = 256
    NST = TF // 128
    cx = Ctx(nc)
    wgs = cx.sb("wg", [128, 8, FF], BF16)
    wus = cx.sb("wu", [128, 8, FF], BF16)
    wds = cx.sb("wd", [128, NFC, D], BF16)
    stg = [cx.sb("stg%d" % i, [128, 1408], F32) for i in range(2)]
    gcol = cx.sb("gcol", [128, 8], F32)
    idf, idb = make_ident(sc, cx, BF16)
    sc.dma("sp", gcol[:], gain.rearrange("(kc p) -> p kc", p=128), writes=[gcol], allow_slow_non_contiguous=True)
    load_weight_bf16(sc, stg, wgs, lambda kc, f0, fw: wgs[:, kc, f0:f0 + fw], wg, D, FF, 1408, gcol)
    load_weight_bf16(sc, stg, wus, lambda kc, f0, fw: wus[:, kc, f0:f0 + fw], wu, D, FF, 1408, gcol)
    load_weight_bf16(sc, stg, wds, lambda kc, f0, fw: wds[:, kc, f0:f0 + fw], wd, FF, D, 1024, None)
    fg = None
    if final_gain is not None:
        fg = cx.sb("fg", [128, D], F32)
        sc.dma("sp", fg[:], final_gain.partition_broadcast(128), writes=[fg])

    NB = 2
    xs = [[cx.sb("xs%d_%d" % (b, st), [128, D], F32) for st in range(NST)] for b in range(NB)]
    hb = [cx.sb("hb%d" % st, [128, D], BF16) for st in range(NST)]
    hT = [cx.sb("hT%d" % b, [128, 8, TF], BF16) for b in range(NB)]
    aT = [cx.sb("aT%d" % b, [128, NFC, TF], BF16) for b in range(NB)]
    junk = cx.sb("junk", [128, D], F32)
    ss = [cx.sb("ss%d" % i, [128, 1], F32) for i in range(2)]
    rstd = [cx.sb("rstd%d" % i, [128, 1], F32) for i in range(2)]
    sg = [cx.sb("sg%d" % i, [128, TF], F32) for i in range(2)]
    ptp = cx.ps("ptp", [128, D], BF16)
    pg = [cx.ps("pg%d" % i, [128, 512], F32) for i in range(2)]
    pu = [cx.ps("pu%d" % i, [128, 512], F32) for i in range(2)]
    pd = [cx.ps("pd%d" % i, [128, 512], F32) for i in range(2)]

    ntiles = S // TF
    k = 0
    for ti in range(ntiles):
        b = ti % NB
        t0 = ti * TF
        for st in range(NST):
            x = xs[b][st]
            sc.dma("sp", x[:], x_src[t0 + st * 128:t0 + (st + 1) * 128, :], writes=[x])
            r = rstd[st % 2]
            rms_rstd(sc, x, junk, ss[st % 2], r)
            h = hb[st]
            sc.op("dve", lambda en, h=h, x=x, r=r: en.tensor_scalar(out=h[:], in0=x[:], scalar1=r[:], scalar2=None,
                                                                    op0=ALU.mult), reads=[x, r], writes=[h])
            for kc in range(8):
                sc.op("pe", lambda en, kc=kc, h=h: en.transpose(out=ptp[:, kc * 128:(kc + 1) * 128],
                                                                 in_=h[:, kc * 128:(kc + 1) * 128], identity=idb[:]),
                      reads=[h, idb], writes=[ptp])
            sc.op("act", lambda en, b=b, st=st: en.copy(
                out=hT[b][:, :, st * 128:(st + 1) * 128],
                in_=ptp[:].rearrange("p (k t) -> p k t", k=8)), reads=[ptp], writes=[hT[b]])
        for fc in range(NFC):
            g = pg[fc % 2]
            u = pu[fc % 2]
            for kc in range(8):
                sc.op("pe", lambda en, g=g, kc=kc, fc=fc, b=b: en.matmul(
                    out=g[:, 0:TF], lhsT=wgs[:, kc, fc * 128:(fc + 1) * 128], rhs=hT[b][:, kc, :],
                    start=(kc == 0), stop=(kc == 7)), reads=[wgs, hT[b]], writes=[g])
            for kc in range(8):
                sc.op("pe", lambda en, u=u, kc=kc, fc=fc, b=b: en.matmul(
                    out=u[:, 0:TF], lhsT=wus[:, kc, fc * 128:(fc + 1) * 128], rhs=hT[b][:, kc, :],
                    start=(kc == 0), stop=(kc == 7)), reads=[wus, hT[b]], writes=[u])
            s_ = sg[fc % 2]
            sc.op("act", lambda en, s_=s_, g=g: en.activation(out=s_[:], in_=g[:, 0:TF], func=AF.Silu),
                  reads=[g], writes=[s_])
            sc.op("dve", lambda en, s_=s_, u=u, fc=fc, b=b: en.tensor_tensor(
                out=aT[b][:, fc, :], in0=u[:, 0:TF], in1=s_[:], op=ALU.mult), reads=[u, s_], writes=[aT[b]])
        for st in range(NST):
            x = xs[b][st]
            for half in range(2):
                p = pd[k % 2]
                k += 1
                for fc in range(NFC):
                    sc.op("pe", lambda en, p=p, fc=fc, st=st, half=half, b=b: en.matmul(
                        out=p[:], lhsT=aT[b][:, fc, st * 128:(st + 1) * 128],
                        rhs=wds[:, fc, half * 512:(half + 1) * 512], start=(fc == 0), stop=(fc == NFC - 1)),
                          reads=[aT[b], wds], writes=[p])
                sc.op("dve", lambda en, p=p, x=x, half=half: en.scalar_tensor_tensor(
                    out=x[:, half * 512:(half + 1) * 512], in0=p[:], scalar=0.5,
                    in1=x[:, half * 512:(half + 1) * 512], op0=ALU.mult, op1=ALU.add), reads=[p, x], writes=[x])
            if fg is not None:
                r = rstd[st % 2]
                rms_rstd(sc, x, junk, ss[st % 2], r)
                sc.op("dve", lambda en, x=x, r=r: en.scalar_tensor_tensor(
                    out=x[:], in0=x[:], scalar=r[:], in1=fg[:], op0=ALU.mult, op1=ALU.mult),
                      reads=[x, r, fg], writes=[x])
            sc.dma("pool", x_dst[t0 + st * 128:t0 + (st + 1) * 128, :], x[:], reads=[x])
    sc.barrier()
    sc.emit()
    cx.close()


I32 = mybir.dt.int32
NH_A = 12
AW = 768
IN_COLS = 7712
RS = 3360
OFF_R = 2304
OFF_G = 2304 + 3360
ROPE_THETA = 500000.0


def norm_transpose(sc, x, junk, ss, rstd, hb, ptp, idb, hT_out_ap, hT_buf):
    rms_rstd(sc, x, junk, ss, rstd)
    sc.op("dve", lambda en: en.tensor_scalar(out=hb[:], in0=x[:], scalar1=rstd[:], scalar2=None, op0=ALU.mult),
          reads=[x, rstd], writes=[hb])
    for kc in range(8):
        sc.op("pe", lambda en, kc=kc: en.transpose(out=ptp[:, kc * 128:(kc + 1) * 128],
                                                    in_=hb[:, kc * 128:(kc + 1) * 128], identity=idb[:]),
              reads=[hb, idb], writes=[ptp])
    sc.op("act", lambda en: en.copy(out=hT_out_ap, in_=ptp[:].rearrange("p (k t) -> p k t", k=8)),
          reads=[ptp], writes=[hT_buf])


def build_rope_tables(sc, cx, NT):
    half = 8
    inv_freq = np.power(np.float32(ROPE_THETA), -np.arange(half, dtype=np.float32) * np.float32(2.0 / 16)).astype(
        np.float32)
    pos = cx.sb("pos", [128, NT], F32)
    sc.op("pool", lambda en: en.iota(out=pos[:], pattern=[[128, NT]], base=0, channel_multiplier=1,
                                     allow_small_or_imprecise_dtypes=True), writes=[pos])
    ang = cx.sb("ang", [128, NT, 8], F32)
    for i in range(half):
        sc.op("dve", lambda en, i=i: en.tensor_scalar(out=ang[:, :, i], in0=pos[:], scalar1=float(inv_freq[i]),
                                                      scalar2=None, op0=ALU.mult), reads=[pos], writes=[ang])
    tabs = []
    for nm, shift in (("cos", math.pi / 2), ("sin", 0.0)):
        b = cx.sb("rb_" + nm, [128, NT, 8], F32)
        ki = cx.sb("rk_" + nm, [128, NT, 8], I32)
        kf = cx.sb("rf_" + nm, [128, NT, 8], F32)
        cr = cx.sb("rc_" + nm, [128, NT, 8], F32)
        tab = cx.sb("tab_" + nm, [128, NT, 8], F32)
        sc.op("dve", lambda en, b=b, shift=shift: en.tensor_scalar(out=b[:], in0=ang[:], scalar1=shift, scalar2=None,
                                                                   op0=ALU.add), reads=[ang], writes=[b])
        sc.op("dve", lambda en, b=b, ki=ki: en.tensor_scalar(out=ki[:], in0=b[:], scalar1=1.0 / (2 * math.pi),
                                                             scalar2=None, op0=ALU.mult), reads=[b], writes=[ki])
        sc.op("dve", lambda en, ki=ki, kf=kf: en.tensor_copy(out=kf[:], in_=ki[:]), reads=[ki], writes=[kf])
        sc.op("dve", lambda en, b=b, kf=kf: en.scalar_tensor_tensor(out=b[:], in0=kf[:], scalar=-2 * math.pi,
                                                                    in1=b[:], op0=ALU.mult, op1=ALU.add),
              reads=[kf, b], writes=[b])
        sc.op("dve", lambda en, b=b, cr=cr: en.tensor_scalar(out=cr[:], in0=b[:], scalar1=math.pi,
                                                             scalar2=-2 * math.pi, op0=ALU.is_gt, op1=ALU.mult),
              reads=[b], writes=[cr])
        sc.op("dve", lambda en, b=b, cr=cr: en.tensor_tensor(out=b[:], in0=b[:], in1=cr[:], op=ALU.add),
              reads=[b, cr], writes=[b])
        sc.op("dve", lambda en, b=b, cr=cr: en.tensor_scalar(out=cr[:], in0=b[:], scalar1=-math.pi,
                                                             scalar2=2 * math.pi, op0=ALU.is_lt, op1=ALU.mult),
              reads=[b], writes=[cr])
        sc.op("dve", lambda en, b=b, cr=cr: en.tensor_tensor(out=b[:], in0=b[:], in1=cr[:], op=ALU.add),
              reads=[b, cr], writes=[b])
        sc.op("act", lambda en, b=b, tab=tab: en.activation(out=tab[:], in_=b[:], func=AF.Sin), reads=[b],
              writes=[tab])
        tabs.append(tab)
    return tabs


def phase_proj(nc, sc, S, x1, qkv, gates, mix_norm, w_in, gate_bias):
    NT = S // 128
    cx = Ctx(nc)
    NC_ = 2304 + 2048
    ws = cx.sb("win", [128, 8, NC_], BF16)
    stg = [cx.sb("stg%d" % i, [128, 1152], F32) for i in range(2)]
    gcol = cx.sb("gcol", [128, 8], F32)
    idf, idb = make_ident(sc, cx, BF16)
    sc.dma("sp", gcol[:], mix_norm.rearrange("(kc p) -> p kc", p=128), writes=[gcol], allow_slow_non_contiguous=True)
    load_weight_bf16(sc, stg, ws, lambda kc, f0, fw: ws[:, kc, f0:f0 + fw], w_in[:, 0:2304], D, 2304, 1152, gcol)
    load_weight_bf16(sc, stg, ws, lambda kc, f0, fw: ws[:, kc, 2304 + f0:2304 + f0 + fw], w_in[:, OFF_G:OFF_G + 2048],
                     D, 2048, 1024, gcol)
    gb = cx.sb("gb", [128, 2048], F32)
    sc.dma("sp", gb[:], gate_bias.partition_broadcast(128), writes=[gb])
    cos_t, sin_t = build_rope_tables(sc, cx, NT)

    xs = [cx.sb("xs%d" % b, [128, D], F32) for b in range(2)]
    hb = cx.sb("hb", [128, D], BF16)
    hT = [cx.sb("hT%d" % b, [128, 8, 128], BF16) for b in range(2)]
    junk = cx.sb("junk", [128, D], F32)
    ss = cx.sb("ss", [128, 1], F32)
    rstd = cx.sb("rstd", [128, 1], F32)
    qs = [cx.sb("qs%d" % b, [128, 2304], BF16) for b in range(2)]
    gs = [cx.sb("gs%d" % b, [128, 2048], BF16) for b in range(2)]
    gf = cx.sb("gf", [128, 512], F32)
    rt = [cx.sb("rt%d" % i, [128, 24, 8], F32) for i in range(4)]
    ptp = cx.ps("ptp", [128, D], BF16)
    pp = [cx.ps("pp%d" % i, [128, 512], F32) for i in range(6)]
    pk = 0
    def front(ti):
        x = xs[ti % 2]
        sc.dma("sp", x[:], x1[ti * 128:(ti + 1) * 128, :], writes=[x])
        norm_transpose(sc, x, junk, ss, rstd, hb, ptp, idb, hT[ti % 2][:], hT[ti % 2])

    front(0)
    for ti in range(NT):
        b = ti % 2
        qk_ps = []
        for c0 in range(0, 2304, 512):
            cw = min(512, 2304 - c0)
            p = pp[pk % 6]
            pk += 1
            for kc in range(8):
                sc.op("pe", lambda en, p=p, kc=kc, c0=c0, cw=cw, b=b: en.matmul(
                    out=p[:, 0:cw], lhsT=hT[b][:, kc, :], rhs=ws[:, kc, c0:c0 + cw], start=(kc == 0), stop=(kc == 7)),
                      reads=[hT[b], ws], writes=[p])
            qk_ps.append((p, c0, cw))
        if ti + 1 < NT:
            front(ti + 1)
        q_ = qs[b]
        for i, (p, c0, cw) in enumerate(qk_ps):
            eng = "act" if i % 2 == 0 else "dve"
            if eng == "act":
                sc.op("act", lambda en, p=p, c0=c0, cw=cw, q_=q_: en.copy(out=q_[:, c0:c0 + cw], in_=p[:, 0:cw]),
                      reads=[p], writes=[q_])
            else:
                sc.op("dve", lambda en, p=p, c0=c0, cw=cw, q_=q_: en.tensor_copy(out=q_[:, c0:c0 + cw],
                                                                                  in_=p[:, 0:cw]),
                      reads=[p], writes=[q_])
        cosb = cos_t[:, ti, :].unsqueeze(1).broadcast_to([128, 8, 8])
        sinb = sin_t[:, ti, :].unsqueeze(1).broadcast_to([128, 8, 8])
        for i in range(3):
            p = qk_ps[i][0]
            pv = p[:].rearrange("p (h d) -> p h d", h=8)
            qv = q_[:, i * 512:(i + 1) * 512].rearrange("p (h d) -> p h d", h=8)
            x1v = pv[:, :, 0:8]
            x2v = pv[:, :, 8:16]
            hs = slice(i * 8, (i + 1) * 8)
            sc.op("dve", lambda en, x1v=x1v, hs=hs, cosb=cosb: en.tensor_tensor(out=rt[0][:, hs, :], in0=x1v, in1=cosb,
                                                                      op=ALU.mult), reads=[p, cos_t], writes=[rt[0]])
            sc.op("dve", lambda en, x2v=x2v, hs=hs, sinb=sinb: en.tensor_tensor(out=rt[1][:, hs, :], in0=x2v, in1=sinb,
                                                                      op=ALU.mult), reads=[p, sin_t], writes=[rt[1]])
            sc.op("dve", lambda en, x2v=x2v, hs=hs, cosb=cosb: en.tensor_tensor(out=rt[2][:, hs, :], in0=x2v, in1=cosb,
                                                                      op=ALU.mult), reads=[p, cos_t], writes=[rt[2]])
            sc.op("dve", lambda en, x1v=x1v, hs=hs, sinb=sinb: en.tensor_tensor(out=rt[3][:, hs, :], in0=x1v, in1=sinb,
                                                                      op=ALU.mult), reads=[p, sin_t], writes=[rt[3]])
            sc.op("dve", lambda en, qv=qv, hs=hs: en.tensor_tensor(out=qv[:, :, 0:8], in0=rt[0][:, hs, :],
                                                                    in1=rt[1][:, hs, :], op=ALU.subtract),
                  reads=[rt[0], rt[1]], writes=[q_])
            sc.op("dve", lambda en, qv=qv, hs=hs: en.tensor_tensor(out=qv[:, :, 8:16], in0=rt[2][:, hs, :],
                                                                    in1=rt[3][:, hs, :], op=ALU.add),
                  reads=[rt[2], rt[3]], writes=[q_])
        sc.dma("pool", qkv[ti * 128:(ti + 1) * 128, :], q_[:], reads=[q_])
        g_ = gs[b]
        for c0 in range(0, 2048, 512):
            p = pp[pk % 6]
            pk += 1
            for kc in range(8):
                sc.op("pe", lambda en, p=p, kc=kc, c0=c0, b=b: en.matmul(
                    out=p[:], lhsT=hT[b][:, kc, :], rhs=ws[:, kc, 2304 + c0:2304 + c0 + 512], start=(kc == 0),
                    stop=(kc == 7)), reads=[hT[b], ws], writes=[p])
            sc.op("dve", lambda en, p=p, c0=c0: en.tensor_tensor(out=gf[:], in0=p[:], in1=gb[:, c0:c0 + 512],
                                                                  op=ALU.add), reads=[p, gb], writes=[gf])
            sc.op("act", lambda en, c0=c0, g_=g_: en.activation(out=g_[:, c0:c0 + 512], in_=gf[:], func=AF.Sigmoid),
                  reads=[gf], writes=[g_])
        sc.dma("pool", gates[ti * 128:(ti + 1) * 128, :], g_[:], reads=[g_])
    sc.barrier()
    sc.emit()
    cx.close()


ATT_GROUPS = ((128, 1), (512, 4), (2048, 16))
NEG = -30000.0


import os
ATT_LEVEL = int(os.environ.get('ATT_LEVEL', '9'))


def phase_attn(nc, sc, S, qkv, att):
    cx = Ctx(nc)
    idf, idb = make_ident(sc, cx, BF16)
    mask = cx.sb("mask", [128, 256], F32)
    mask0 = cx.sb("mask0", [128, 256], F32)
    sc.op("pool", lambda en: en.memset(mask[:], 0.0), writes=[mask])
    sc.op("pool", lambda en: en.affine_select(out=mask[:, 0:128], in_=mask[:, 0:128], pattern=[[1, 128]],
                                               compare_op=ALU.is_ge, fill=NEG, base=0, channel_multiplier=-1),
          reads=[mask], writes=[mask])
    sc.op("pool", lambda en: en.affine_select(out=mask[:, 128:256], in_=mask[:, 128:256], pattern=[[-1, 128]],
                                               compare_op=ALU.is_ge, fill=NEG, base=0, channel_multiplier=1),
          reads=[mask], writes=[mask])
    sc.op("pool", lambda en: en.tensor_copy(out=mask0[:], in_=mask[:]), reads=[mask], writes=[mask0])
    sc.op("pool", lambda en: en.memset(mask0[:, 0:128], NEG), reads=[mask0], writes=[mask0])

    qb = [cx.sb("qb%d" % i, [128, 256], BF16) for i in range(2)]
    kb = [cx.sb("kb%d" % i, [128, 256], BF16) for i in range(2)]
    vb = [cx.sb("vb%d" % i, [128, 256], BF16) for i in range(3)]
    QT = [cx.sb("QT%d" % i, [128, 2, 128], BF16) for i in range(2)]
    KT = [cx.sb("KT%d" % i, [128, 2, 128], BF16) for i in range(3)]
    for t in vb + KT:
        sc.op("pool", lambda en, t=t: en.memset(t[:], 0.0), writes=[t])
    sm = [cx.sb("sm%d" % i, [128, 2, 256], F32) for i in range(2)]
    m2 = [cx.sb("m2_%d" % i, [128, 2], F32) for i in range(2)]
    nb2 = [cx.sb("nb2_%d" % i, [128, 2], F32) for i in range(2)]
    pb = [cx.sb("pb%d" % i, [128, 256], BF16) for i in range(4)]
    PT = [cx.sb("PT%d" % i, [128, 2, 128], BF16) for i in range(4)]
    ob = [cx.sb("ob%d" % i, [128, 4, 66], F32) for i in range(2)]
    ptq = [cx.ps("ptq%d" % i, [128, 8, 128], BF16) for i in range(2)]
    sp = [cx.ps("sp%d" % i, [128, 2, 256], F32) for i in range(2)]
    ptp = [cx.ps("ptpp%d" % i, [128, 8, 128], BF16) for i in range(2)]
    po = [cx.ps("po%d" % i, [128, 8, 64], F32) for i in range(2)]

    blk = 0
    hk = 0
    for g, (window, d) in enumerate(ATT_GROUPS):
        L = S // d
        nb = L // 128
        qv = qkv.rearrange("(l d) c -> d l c", d=d)
        av = att.rearrange("(l d) g h e -> d l g (h e)", d=d)
        for r in range(d):
            for i in range(nb):
                rows = slice(i * 128, (i + 1) * 128)
                q_ = qb[blk % 2]
                k_ = kb[blk % 2]
                vcur = vb[blk % 3]
                vprev = vb[(blk - 1) % 3]
                kcur = KT[blk % 3]
                kprev = KT[(blk - 1) % 3]
                qt = QT[blk % 2]
                o_ = ob[blk % 2]
                po_ = po[blk % 2]
                sc.dma("sp", q_[:], qv[r, rows, 256 * g:256 * g + 256], writes=[q_])
                sc.dma("sp", k_[:], qv[r, rows, 768 + 256 * g:768 + 256 * g + 256], writes=[k_])
                sc.dma("sp", vcur[:], qv[r, rows, 1536 + 256 * g:1536 + 256 * g + 256], writes=[vcur])
                for src, dst in ((q_, qt), (k_, kcur)):
                    pt = ptq[hk % 2]
                    hk += 1
                    for pr in range(2):
                        sc.op("pe", lambda en, pt=pt, pr=pr, src=src: en.transpose(
                            out=pt[:, pr, :], in_=src[:, pr * 128:(pr + 1) * 128], identity=idb[:]),
                              reads=[src, idb], writes=[pt])
                    sc.op("act", lambda en, pt=pt, dst=dst: en.copy(out=dst[:], in_=pt[:, 0:2, :]), reads=[pt], writes=[dst])
                mk = mask0 if i == 0 else mask
                for pr in range(2):
                    s_ = sp[pr]
                    for hh in range(2):
                        ps_ = slice(64 * hh, 64 * hh + 64)
                        sc.op("pe", f_mm(s_[:, hh, 0:128], qt[ps_, pr, :], kprev[ps_, pr, :]), reads=[qt, kprev],
                              writes=[s_])
                        sc.op("pe", f_mm(s_[:, hh, 128:256], qt[ps_, pr, :], kcur[ps_, pr, :]), reads=[qt, kcur],
                              writes=[s_])
                for pr in range(2):
                    s_, sm_, m_, n_ = sp[pr], sm[pr], m2[pr], nb2[pr]
                    sc.op("dve", f_tt(sm_[:], s_[:], mk[:].unsqueeze(1).broadcast_to([128, 2, 256]), ALU.add),
                          reads=[s_, mk], writes=[sm_])
                    sc.op("dve", f_red(m_[:], sm_[:], ALU.max), reads=[sm_], writes=[m_])
                    sc.op("dve", f_ts(n_[:], m_[:], -0.125, ALU.mult), reads=[m_], writes=[n_])
                    sc.op("dve", f_ts(o_[:, 2 * pr:2 * pr + 2, 64], m_[:], 0.125, ALU.mult), reads=[m_], writes=[o_])
                for head in range(4):
                    pr, hh = head // 2, head % 2
                    sc.op("act", f_act(pb[head][:], sm[pr][:, hh, :], AF.Exp, bias=nb2[pr][:, hh:hh + 1], scale=0.125,
                                       accum=o_[:, head, 65:66]), reads=[sm[pr], nb2[pr]], writes=[pb[head], o_])
                for head in range(4):
                    tp = ptp[head % 2]
                    for half in range(2):
                        sc.op("pe", f_tr(tp[:, half, :], pb[head][:, half * 128:(half + 1) * 128], idb[:]),
                              reads=[pb[head], idb], writes=[tp])
                    sc.op("dve", f_copy(PT[head][:], tp[:, 0:2, :]), reads=[tp], writes=[PT[head]])
                for head in range(4):
                    pT = PT[head]
                    sc.op("pe", f_mm(po_[:, head, :], pT[:, 0, :], vprev[:, head * 64:(head + 1) * 64], True, False),
                          reads=[pT, vprev], writes=[po_])
                    sc.op("pe", f_mm(po_[:, head, :], pT[:, 1, :], vcur[:, head * 64:(head + 1) * 64], False, True),
                          reads=[pT, vcur], writes=[po_])
                sc.op("act", f_acopy(o_[:, :, 0:64], po_[:, 0:4, :]), reads=[po_], writes=[o_])
                sc.dma("pool", av[r, rows, g, :], o_[:].rearrange("p h e -> p (h e)"), reads=[o_])
                blk += 1
    sc.barrier()
    sc.emit()
    cx.close()


def f_tt(o, a, b, op):
    return lambda en: en.tensor_tensor(out=o, in0=a, in1=b, op=op)


def f_ts(o, a, s1, op0, s2=None, op1=None):
    if op1 is None:
        return lambda en: en.tensor_scalar(out=o, in0=a, scalar1=s1, scalar2=None, op0=op0)
    return lambda en: en.tensor_scalar(out=o, in0=a, scalar1=s1, scalar2=s2, op0=op0, op1=op1)


def f_stt(o, a, sca, b, op0, op1):
    return lambda en: en.scalar_tensor_tensor(out=o, in0=a, scalar=sca, in1=b, op0=op0, op1=op1)


def f_act(o, a, func, bias=None, scale=None, accum=None):
    kw = {}
    if bias is not None:
        kw["bias"] = bias
    if scale is not None:
        kw["scale"] = scale
    if accum is not None:
        kw["accum_out"] = accum
    return lambda en: en.activation(out=o, in_=a, func=func, **kw)


def f_acopy(o, a):
    return lambda en: en.copy(out=o, in_=a)


def f_copy(o, a):
    return lambda en: en.tensor_copy(out=o, in_=a)


def f_red(o, a, op):
    return lambda en: en.tensor_reduce(out=o, in_=a, axis=AX.X, op=op)


def f_mm(o, l, r, st=True, sp=True):
    fn = lambda en: en.matmul(out=o, lhsT=l, rhs=r, start=st, stop=sp)
    fn.rt = (l.base_partition(), l.partition_size())
    return fn


def f_tr(o, a, ident):
    fn = lambda en: en.transpose(out=o, in_=a, identity=ident)
    fn.rt = (a.base_partition(), a.partition_size())
    return fn


def f_memset(o, v):
    return lambda en: en.memset(o, v)


GN_EPS = 64e-5
VEC_NAMES = ["rwkv_w0", "rwkv_a0", "rwkv_k_k", "rwkv_k_a", "rwkv_r_k", "rwkv_ln_w", "rwkv_ln_b"]


def phase_rwkv(nc, sc, S, x1, outr, W):
    NT = S // 128
    cx = Ctx(nc)
    op = sc.op
    idf, idb = make_ident(sc, cx, BF16)
    wr = cx.sb("wr", [128, 8, RS], BF16)
    stg = [cx.sb("stg%d" % i, [128, 1120], F32) for i in range(1)]
    gcol = cx.sb("gcol", [128, 8], F32)
    sc.dma("sp", gcol[:], W["mix_norm"].rearrange("(kc p) -> p kc", p=128), writes=[gcol],
           allow_slow_non_contiguous=True)
    load_weight_bf16(sc, stg, wr, lambda kc, f0, fw: wr[:, kc, f0:f0 + fw], W["w_in"][:, OFF_R:OFF_R + RS], D, RS,
                     1120, gcol)
    w2a2 = cx.sb("w2a2", [128, D], BF16)
    g2s = cx.sb("g2s", [128, 2, D], BF16)
    for (dst_ap, src, rows) in ((w2a2[0:64, :], W["rwkv_w2"], 64), (w2a2[64:128, :], W["rwkv_a2"], 64),
                                (g2s[:, 0, :], W["rwkv_g2"][0:128, :], 128), (g2s[0:32, 1, :], W["rwkv_g2"][128:160, :], 32)):
        s_ = stg[0]
        base = 64 if dst_ap is not None and rows == 64 and src is W["rwkv_a2"] else 0
        sc.dma("sp", s_[base:base + rows, 0:D], src, writes=[s_])
        op("dve", f_copy(dst_ap, s_[base:base + rows, 0:D]), reads=[s_], writes=[w2a2, g2s])
    vecs = cx.sb("vecs", [8, D], F32)
    op("dve", f_memset(vecs[:], 0.0), writes=[vecs])
    for j, nm in enumerate(VEC_NAMES):
        sc.dma("sp", vecs[j:j + 1, :], W[nm].rearrange("(o n) -> o n", o=1), writes=[vecs])
    muv = cx.sb("muv", [8, 512], F32)
    op("dve", f_memset(muv[:], 0.0), writes=[muv])
    sc.dma("sp", muv[0:6, :], W["rwkv_mu"][0:3072].rearrange("(j c) -> j c", c=512), reads=[], writes=[muv])
    sc.dma("sp", muv[6:7, 0:288], W["rwkv_mu"][3072:3360].rearrange("(o n) -> o n", o=1), writes=[muv])
    selm = cx.sb("selm", [8, 8, 128], F32)
    op("pool", f_memset(selm[:], 1.0), writes=[selm])
    op("pool", lambda en: en.affine_select(out=selm[:], in_=selm[:], pattern=[[-1, 8], [0, 128]],
                                           compare_op=ALU.is_equal, fill=0.0, base=0, channel_multiplier=1),
       reads=[selm], writes=[selm])
    selb = cx.sb("selb", [8, 8, 128], BF16)
    vhi = cx.sb("vhi", [8, D], BF16)
    vlo = cx.sb("vlo", [8, D], BF16)
    mhi = cx.sb("mhi", [8, 512], BF16)
    mlo = cx.sb("mlo", [8, 512], BF16)
    op("dve", f_copy(selb[:], selm[:]), reads=[selm], writes=[selb])
    for (src, hi, lo) in ((vecs, vhi, vlo), (muv, mhi, mlo)):
        op("dve", f_copy(hi[:], src[:]), reads=[src], writes=[hi])
        op("dve", f_tt(lo[:], src[:], hi[:], ALU.subtract), reads=[src, hi], writes=[lo])
    m_ui = cx.sb("m_ui", [128, 128], F32)
    m_su = cx.sb("m_su", [128, 128], F32)
    m_sl = cx.sb("m_sl", [128, 128], F32)
    mask4 = cx.sb("mask4", [128, 512], F32)
    ech = cx.sb("ech", [128, 2], F32)
    for m, pat, cm, cmp_ in ((m_ui, [[1, 128]], -1, ALU.is_ge), (m_su, [[1, 128]], -1, ALU.is_gt),
                             (m_sl, [[-1, 128]], 1, ALU.is_gt)):
        op("pool", f_memset(m[:], 1.0), writes=[m])
        op("pool", lambda en, m=m, pat=pat, cm=cm, cmp_=cmp_: en.affine_select(
            out=m[:], in_=m[:], pattern=pat, compare_op=cmp_, fill=0.0, base=0, channel_multiplier=cm),
           reads=[m], writes=[m])
        op("pool", f_memset(m[0:64, 64:128], 0.0), reads=[m], writes=[m])
        op("pool", f_memset(m[64:128, 0:64], 0.0), reads=[m], writes=[m])
    for i, m in enumerate((m_su, m_ui, m_su, m_ui)):
        op("pool", f_copy(mask4[:, i * 128:(i + 1) * 128], m[:]), reads=[m], writes=[mask4])
    op("pool", f_memset(ech[:], 0.0), writes=[ech])
    op("pool", f_memset(ech[0:64, 0:1], 1.0), reads=[ech], writes=[ech])
    op("pool", f_memset(ech[64:128, 1:2], 1.0), reads=[ech], writes=[ech])

    xs = [cx.sb("xs%d" % b, [128, D], F32) for b in range(1)]
    hb = cx.sb("hb", [128, D], BF16)
    hTc = cx.sb("hTc", [128, 8, 128], BF16)
    hTp = cx.sb("hTp", [128, 8, 128], BF16)
    carry = cx.sb("carry", [128, 8, 1], BF16)
    op("pool", f_memset(carry[:], 0.0), writes=[carry])
    ss = cx.sb("ss", [128, 1], F32)
    rstd = cx.sb("rstd", [128, 1], F32)
    z = cx.sb("z", [128, RS], F32)
    junkb = cx.sb("junkb", [128, D], BF16)
    bon = cx.sb("bon", [128, D], BF16)
    ztmp = cx.sb("ztmp", [128, 512], F32)
    lor = cx.sb("lor", [128, 288], BF16)
    lorT = cx.sb("lorT", [128, 3, 128], BF16)
    TM = {n: cx.sb("tm_" + n, [128, D], BF16 if n in ("At", "Bt", "Kt", "Rt", "vb") else F32)
          for n in ("lw", "a", "g", "kk", "kp", "E", "At", "Bt", "Kt", "Rt", "t1", "vb")}
    arT = cx.sb("arT", [128, 8, 256], BF16)
    bkT = cx.sb("bkT", [128, 8, 256], BF16)
    Hs = cx.sb("Hs", [64, 16, 64], F32)
    op("pool", f_memset(Hs[:], 0.0), writes=[Hs])
    Hhi = cx.sb("Hhi", [64, 16, 64], BF16)
    Hlo = cx.sb("Hlo", [64, 16, 64], BF16)
    op("pool", f_memset(Hhi[:], 0.0), writes=[Hhi])
    op("pool", f_memset(Hlo[:], 0.0), writes=[Hlo])
    Wc = cx.sb("Wc", [64, 32], F32)
    sm16 = {n: cx.sb("s16_" + n, [128, 16], F32) for n in ("ssq", "rn", "rks", "mu", "vs", "rs")}
    HG = 8
    M4 = [cx.sb("M4_%d" % i, [128, 512], BF16) for i in range(HG)]
    PP = [[cx.sb("PP%d_%d" % (i, j), [128, 256], BF16) for j in range(2)] for i in range(HG)]
    TTb = [[cx.sb("TT%d_%d" % (i, j), [128, 128], BF16) for j in range(2)] for i in range(HG)]
    Zs = [cx.sb("Zs%d" % i, [128, 64], BF16) for i in range(HG)]
    AU = [cx.sb("AU%d" % i, [128, 128], BF16) for i in range(HG)]
    QT = [cx.sb("QTs%d" % i, [64, 128], BF16) for i in range(HG)]
    GTs = cx.sb("GTs", [64, HG, 2, 64], BF16)
    NW = cx.sb("NW", [64, HG, 2, 64], F32)
    Y0s = cx.sb("Y0s", [64, HG, 128], F32)
    orb = [cx.sb("orb%d" % i, [128, D], BF16) for i in range(1)]
    ptp = cx.ps("ptp", [128, D], BF16)
    ptp2 = cx.ps("ptp2", [128, D], BF16)
    ring = [cx.ps("pr%d" % i, [128, 512], F32) for i in range(4)]
    py = [cx.ps("py%d" % i, [128, 512], F32) for i in range(2)]
    rk = [0]

    def nxt():
        p = ring[rk[0] % len(ring)]
        rk[0] += 1
        return p

    def bc_vec(j, hf):
        p = nxt()
        op("pe", f_mm(p[:], selb[0:8, j, :], vhi[0:8, hf * 512:(hf + 1) * 512], True, False), reads=[selb, vhi],
           writes=[p])
        op("pe", f_mm(p[:], selb[0:8, j, :], vlo[0:8, hf * 512:(hf + 1) * 512], False, True), reads=[selb, vlo],
           writes=[p])
        return p

    zr, zk, zv = z[:, 0:D], z[:, D:2 * D], z[:, 2 * D:3 * D]

    def v3(ap):
        return ap.rearrange("p (h d) -> p h d", h=16)

    def b16(buf):
        return buf[:].unsqueeze(2).broadcast_to([128, 16, 64])

    def front1(ti):
        x = xs[0]
        sc.dma("sp", x[:], x1[ti * 128:(ti + 1) * 128, :], writes=[x])
        norm_transpose(sc, x, junkb, ss, rstd, hb, ptp, idb, hTc[:], hTc)
        op("dve", f_copy(hTp[:, :, 0:1], carry[:]), reads=[carry], writes=[hTp])
        op("dve", f_copy(hTp[:, :, 1:128], hTc[:, :, 0:127]), reads=[hTc], writes=[hTp])
        op("dve", f_copy(carry[:], hTc[:, :, 127:128]), reads=[hTc], writes=[carry])
        yield
        for cb in range(7):
            c0 = cb * 512
            cw = min(512, RS - c0)
            pP = nxt()
            pQ = nxt()
            pM = nxt()
            for kc in range(8):
                op("pe", f_mm(pP[:, 0:cw], hTc[:, kc, :], wr[:, kc, c0:c0 + cw], kc == 0, kc == 7), reads=[hTc, wr],
                   writes=[pP])
            for kc in range(8):
                op("pe", f_mm(pQ[:, 0:cw], hTp[:, kc, :], wr[:, kc, c0:c0 + cw], kc == 0, kc == 7), reads=[hTp, wr],
                   writes=[pQ])
            op("pe", f_mm(pM[:, 0:cw], selb[0:8, cb, :], mhi[0:8, 0:cw], True, False), reads=[selb, mhi], writes=[pM])
            op("pe", f_mm(pM[:, 0:cw], selb[0:8, cb, :], mlo[0:8, 0:cw], False, True), reads=[selb, mlo], writes=[pM])
            op("act", f_acopy(z[:, c0:c0 + cw], pP[:, 0:cw]), reads=[pP], writes=[z])
            op("dve", f_tt(ztmp[:, 0:cw], pQ[:, 0:cw], z[:, c0:c0 + cw], ALU.subtract), reads=[pQ, z], writes=[ztmp])
            op("dve", f_tt(ztmp[:, 0:cw], ztmp[:, 0:cw], pM[:, 0:cw], ALU.mult), reads=[ztmp, pM], writes=[ztmp])
            op("pool", f_tt(z[:, c0:c0 + cw], z[:, c0:c0 + cw], ztmp[:, 0:cw], ALU.add), reads=[z, ztmp], writes=[z])
            yield

    nxt_front = front1(0)
    for _ in nxt_front:
        pass
    for ti in range(NT):
        nxt_front = front1(ti + 1) if ti + 1 < NT else iter(())
        op("act", f_act(lor[:, 0:64], z[:, 3072:3136], AF.Tanh), reads=[z], writes=[lor])
        op("act", f_act(lor[:, 128:288], z[:, 3200:3360], AF.Sigmoid), reads=[z], writes=[lor])
        op("dve", f_copy(lor[:, 64:128], z[:, 3136:3200]), reads=[z], writes=[lor])
        op("pe", f_tr(ptp[:, 0:128], lor[:, 0:128], idb[:]), reads=[lor, idb], writes=[ptp])
        op("pe", f_tr(ptp[:, 128:256], lor[:, 128:256], idb[:]), reads=[lor, idb], writes=[ptp])
        op("pe", f_tr(ptp[0:32, 256:384], lor[:, 256:288], idb[:]), reads=[lor, idb], writes=[ptp])
        op("act", f_acopy(lorT[:, 0:2, :], ptp[:, 0:256].rearrange("p (a t) -> p a t", a=2)), reads=[ptp], writes=[lorT])
        op("act", f_acopy(lorT[0:32, 2, :], ptp[0:32, 256:384]), reads=[ptp], writes=[lorT])
        ysb = TM["a"]
        lw, a_, g_, kk, kp, E, At, Bt, Kt, Rt, t1 = (TM[n] for n in ("lw", "a", "g", "kk", "kp", "E", "At", "Bt", "Kt",
                                                                      "Rt", "t1"))
        for hf in range(2):
            cs_ = slice(hf * 512, (hf + 1) * 512)
            p = nxt()
            op("pe", f_mm(p[:], lorT[0:64, 0, :], w2a2[0:64, cs_], True, False), reads=[lorT, w2a2], writes=[p])
            op("pe", f_mm(p[:], selb[0:8, 0, :], vhi[0:8, cs_], False, False), reads=[selb, vhi], writes=[p])
            op("pe", f_mm(p[:], selb[0:8, 0, :], vlo[0:8, cs_], False, True), reads=[selb, vlo], writes=[p])
            op("act", f_act(lw[:, cs_], p[:], AF.Sigmoid), reads=[p], writes=[lw])
            p = nxt()
            op("pe", f_mm(p[:], lorT[64:128, 0, :], w2a2[64:128, cs_], True, False), reads=[lorT, w2a2], writes=[p])
            op("pe", f_mm(p[:], selb[0:8, 1, :], vhi[0:8, cs_], False, False), reads=[selb, vhi], writes=[p])
            op("pe", f_mm(p[:], selb[0:8, 1, :], vlo[0:8, cs_], False, True), reads=[selb, vlo], writes=[p])
            op("act", f_act(a_[:, cs_], p[:], AF.Sigmoid), reads=[p], writes=[a_])
            p = nxt()
            op("pe", f_mm(p[:], lorT[:, 1, :], g2s[:, 0, cs_], True, False), reads=[lorT, g2s], writes=[p])
            op("pe", f_mm(p[:], lorT[0:32, 2, :], g2s[0:32, 1, cs_], False, True), reads=[lorT, g2s], writes=[p])
            op("act", f_acopy(g_[:, cs_], p[:]), reads=[p], writes=[g_])
        op("pool", f_ts(lw[:], lw[:], -math.exp(-0.5), ALU.mult), reads=[lw], writes=[lw])
        for hf in range(2):
            cs_ = slice(hf * 512, (hf + 1) * 512)
            p = bc_vec(2, hf)
            op("dve", f_tt(kk[:, cs_], zk[:, cs_], p[:], ALU.mult), reads=[z, p], writes=[kk])
            p = bc_vec(3, hf)
            op("dve", f_stt(t1[:, cs_], a_[:, cs_], -1.0, p[:], ALU.add, ALU.mult), reads=[a_, p], writes=[t1])
            p = bc_vec(4, hf)
            op("dve", f_tt(E[:, cs_], zr[:, cs_], p[:], ALU.mult), reads=[z, p], writes=[E])
        op("dve", f_stt(kp[:], t1[:], 1.0, zk, ALU.add, ALU.mult), reads=[t1, z], writes=[kp])
        op("pool", f_tt(t1[:], kk[:], kk[:], ALU.mult), reads=[kk], writes=[t1])
        op("dve", f_red(sm16["ssq"][:], v3(t1[:]), ALU.add), reads=[t1], writes=[sm16["ssq"]])
        op("act", f_act(sm16["rn"][:], sm16["ssq"][:], AF.Sqrt), reads=[sm16["ssq"]], writes=[sm16["rn"]])
        op("dve", f_ts(sm16["rn"][:], sm16["rn"][:], 1e-12, ALU.max), reads=[sm16["rn"]], writes=[sm16["rn"]])
        op("dve", lambda en: en.reciprocal(out=sm16["rn"][:], in_=sm16["rn"][:]), reads=[sm16["rn"]],
           writes=[sm16["rn"]])
        op("dve", f_tt(v3(kk[:]), v3(kk[:]), b16(sm16["rn"]), ALU.mult), reads=[kk, sm16["rn"]], writes=[kk])
        op("pool", f_tt(E[:], E[:], kp[:], ALU.mult), reads=[E, kp], writes=[E])
        op("dve", f_red(sm16["rks"][:], v3(E[:]), ALU.add), reads=[E], writes=[sm16["rks"]])
        op("dve", f_tt(v3(bon[:]), v3(zv), b16(sm16["rks"]), ALU.mult), reads=[z, sm16["rks"]], writes=[bon])
        pcs = []
        for hf in range(2):
            cs_ = slice(hf * 512, (hf + 1) * 512)
            p = nxt()
            op("pe", f_mm(p[:], m_ui[:], lw[:, cs_]), reads=[m_ui, lw], writes=[p])
            pcs.append(p)
            op("act", f_act(E[:, cs_], p[:], AF.Exp), reads=[p], writes=[E])
            op("dve", f_tt(Rt[:, cs_], zr[:, cs_], E[:, cs_], ALU.mult), reads=[z, E], writes=[Rt])
        op("pool", f_tt(t1[:], kk[:], a_[:], ALU.mult), reads=[kk, a_], writes=[t1])
        for hf in range(2):
            cs_ = slice(hf * 512, (hf + 1) * 512)
            p = pcs[hf]
            op("act", f_act(E[:, cs_], p[:], AF.Exp, scale=-1.0), reads=[p], writes=[E])
            op("dve", f_tt(Bt[:, cs_], t1[:, cs_], E[:, cs_], ALU.mult), reads=[t1, E], writes=[Bt])
            op("pool", f_tt(Kt[:, cs_], kp[:, cs_], E[:, cs_], ALU.mult), reads=[kp, E], writes=[Kt])
            op("dve", f_tt(At[:, cs_], p[:], lw[:, cs_], ALU.subtract), reads=[p, lw], writes=[At])
        op("act", f_act(E[:], At[:], AF.Exp), reads=[At], writes=[E])
        op("dve", f_stt(At[:], kk[:], -1.0, E[:], ALU.mult, ALU.mult), reads=[kk, E], writes=[At])
        p = nxt()
        for h in range(16):
            op("pe", f_mm(p[0:64, 2 * h:2 * h + 2], lw[:, 64 * h:64 * h + 64], ech[:]), reads=[lw, ech], writes=[p])
        op("act", f_act(Wc[:], p[0:64, 0:32], AF.Exp), reads=[p], writes=[Wc])
        op("pool", f_copy(TM["vb"][:], zv), reads=[z], writes=[TM["vb"]])
        for pr in range(8):
            p = (ptp, ptp2)[pr % 2]
            cs_ = slice(pr * 128, (pr + 1) * 128)
            for i, src in enumerate((At, Rt, Bt, Kt)):
                op("pe", f_tr(p[:, i * 128:(i + 1) * 128], src[:, cs_], idb[:]), reads=[src, idb], writes=[p])
            op("act", f_acopy(arT[:, pr, :], p[:, 0:256]), reads=[p], writes=[arT])
            op("dve", f_copy(bkT[:, pr, :], p[:, 256:512]), reads=[p], writes=[bkT])
        for _ in nxt_front:
            pass
        for hg in range(16 // HG):
            heads = [hg * HG + i for i in range(HG)]
            info = []
            for i, h in enumerate(heads):
                pr, hb_ = h // 2, 64 * (h % 2)
                ps_ = slice(hb_, hb_ + 64)
                hc = slice(64 * h, 64 * h + 64)
                info.append((i, h, pr, ps_, hc))
            for (i, h, pr, ps_, hc) in info:
                pa = nxt()
                pb = nxt()
                op("pe", f_mm(pa[:, 0:128], arT[ps_, pr, 0:128], bkT[ps_, pr, 0:128]), reads=[arT, bkT], writes=[pa])
                op("pe", f_mm(pb[:, 0:256], bkT[ps_, pr, 0:128], arT[ps_, pr, 0:256]), reads=[arT, bkT], writes=[pb])
                op("pe", f_mm(pb[:, 256:512], bkT[ps_, pr, 128:256], arT[ps_, pr, 0:256]), reads=[arT, bkT],
                   writes=[pb])
                op("dve", f_tt(PP[i][0][:, 0:128], pa[:, 0:128], m_sl[:], ALU.mult), reads=[pa, m_sl],
                   writes=[PP[i][0]])
                op("dve", f_tt(M4[i][:], pb[:], mask4[:], ALU.mult), reads=[pb, mask4], writes=[M4[i]])
                op("pool", f_copy(PP[i][0][:, 128:256], M4[i][:, 0:128]), reads=[M4[i]], writes=[PP[i][0]])
                op("pool", f_tt(TTb[i][0][:], M4[i][:, 0:128], idb[:], ALU.add), reads=[M4[i], idb],
                   writes=[TTb[i][0]])
            for j in range(5):
                cur, nx = j % 2, (j + 1) % 2
                pcl = {}
                for (i, h, pr, ps_, hc) in info:
                    pc = nxt()
                    pcl[i] = pc
                    op("pe", f_mm(pc[:, 0:128], PP[i][cur][:, 128:256], PP[i][cur][:, 0:128]), reads=[PP[i][cur]],
                       writes=[pc])
                    if j < 4:
                        op("pe", f_mm(pc[:, 128:256], PP[i][cur][:, 0:128], PP[i][cur][:, 128:256]),
                           reads=[PP[i][cur]], writes=[pc])
                    wdt = 256 if j < 4 else 128
                    op("act", f_acopy(PP[i][nx][:, 0:wdt], pc[:, 0:wdt]), reads=[pc], writes=[PP[i][nx]])
                for (i, h, pr, ps_, hc) in info:
                    pc = pcl[i]
                    op("pe", f_mm(pc[:, 256:384], PP[i][nx][:, 0:128], TTb[i][cur][:]), reads=[PP[i][nx], TTb[i][cur]],
                       writes=[pc])
                    op("dve", f_tt(TTb[i][nx][:], pc[:, 256:384], TTb[i][cur][:], ALU.add),
                       reads=[pc, TTb[i][cur]], writes=[TTb[i][nx]])
            pzl = {}
            for (i, h, pr, ps_, hc) in info:
                TT = TTb[i][1]
                pz = nxt()
                pzl[i] = pz
                vh = TM["vb"][:, 64 * h:64 * h + 64]
                op("pe", f_mm(pz[:, 0:64], M4[i][:, 256:384], vh), reads=[M4[i], TM["vb"]], writes=[pz])
                op("act", f_acopy(Zs[i][:], pz[:, 0:64]), reads=[pz], writes=[Zs[i]])
            for (i, h, pr, ps_, hc) in info:
                TT = TTb[i][1]
                pz = pzl[i]
                op("pe", f_mm(pz[:, 128:192], TT[:], At[:, hc]), reads=[TT, At], writes=[pz])
                op("pe", f_mm(pz[:, 192:256], TT[:], Zs[i][:]), reads=[TT, Zs[i]], writes=[pz])
                op("act", f_acopy(AU[i][:], pz[:, 128:256]), reads=[pz], writes=[AU[i]])
            for (i, h, pr, ps_, hc) in info:
                pz = pzl[i]
                op("pe", f_mm(pz[0:64, 256:384], AU[i][:, 0:64], M4[i][:, 128:256], True, False),
                   reads=[AU[i], M4[i]], writes=[pz])
                op("pe", f_mm(pz[0:64, 256:384], Rt[:, hc], idb[:], False, True), reads=[Rt, idb], writes=[pz])
                op("act", f_acopy(QT[i][:], pz[0:64, 256:384]), reads=[pz], writes=[QT[i]])
            for q0 in range(0, HG, 4):
                quad = info[q0:q0 + 4]
                pN, pG, pY = nxt(), nxt(), nxt()
                for c in range(2):
                    cs_ = slice(64 * c, 64 * c + 64)
                    for (i, h, pr, ps_, hc) in quad:
                        j = i - q0
                        vh = TM["vb"][:, 64 * h:64 * h + 64]
                        col = 64 * (2 * j + c)
                        op("pe", f_mm(pN[0:64, col:col + 64], Bt[cs_, hc], AU[i][cs_, 64:128], True, False),
                           reads=[Bt, AU[i]], writes=[pN])
                        op("pe", f_mm(pN[0:64, col:col + 64], Kt[cs_, hc], vh[cs_, :], False, True),
                           reads=[Kt, TM["vb"]], writes=[pN])
                        op("pe", f_mm(pG[0:64, col:col + 64], AU[i][cs_, 0:64], Bt[cs_, hc]), reads=[AU[i], Bt],
                           writes=[pG])
                for (i, h, pr, ps_, hc) in quad:
                    j = i - q0
                    vh = TM["vb"][:, 64 * h:64 * h + 64]
                    yo = pY[0:64, 128 * j:128 * j + 128]
                    op("pe", f_mm(yo, AU[i][:, 64:128], M4[i][:, 128:256], True, False), reads=[AU[i], M4[i]],
                       writes=[pY])
                    op("pe", f_mm(yo, vh, M4[i][:, 384:512], False, True), reads=[TM["vb"], M4[i]], writes=[pY])
                for (i, h, pr, ps_, hc) in quad:
                    j = i - q0
                    for c in range(2):
                        col = 64 * (2 * j + c)
                        op("act", f_act(NW[0:64, i, c, :], pN[0:64, col:col + 64], AF.Copy,
                                        scale=Wc[0:64, 2 * h + c:2 * h + c + 1]), reads=[pN, Wc], writes=[NW])
                op("dve", f_tt(GTs[0:64, q0:q0 + 4, :, :].rearrange("p a c k -> p (a c) k"),
                               pG[0:64, :].rearrange("p (a k) -> p a k", a=8),
                               idf[0:64, 0:64].unsqueeze(1).broadcast_to([64, 8, 64]), ALU.add), reads=[pG, idf],
                   writes=[GTs])
                op("act", f_acopy(Y0s[0:64, q0:q0 + 4, :], pY[0:64, :].rearrange("p (a t) -> p a t", a=4)),
                   reads=[pY], writes=[Y0s])
            for c in range(2):
                cs_ = slice(64 * c, 64 * c + 64)
                pS = py[c]
                pH = nxt()
                for (i, h, pr, ps_, hc) in info:
                    op("pe", f_mm(pS[0:64, 64 * i:64 * i + 64], Hhi[0:64, h, :], QT[i][:, cs_], True, False),
                       reads=[Hhi, QT[i]], writes=[pS])
                    op("pe", f_mm(pS[0:64, 64 * i:64 * i + 64], Hlo[0:64, h, :], QT[i][:, cs_], False, True),
                       reads=[Hlo, QT[i]], writes=[pS])
                for (i, h, pr, ps_, hc) in info:
                    op("pe", f_mm(pH[0:64, 64 * i:64 * i + 64], GTs[0:64, i, c, :], Hhi[0:64, h, :], True, False),
                       reads=[GTs, Hhi], writes=[pH])
                    op("pe", f_mm(pH[0:64, 64 * i:64 * i + 64], GTs[0:64, i, c, :], Hlo[0:64, h, :], False, True),
                       reads=[GTs, Hlo], writes=[pH])
                for (i, h, pr, ps_, hc) in info:
                    op("dve", f_stt(Hs[0:64, h, :], pH[0:64, 64 * i:64 * i + 64], Wc[0:64, 2 * h + c:2 * h + c + 1],
                                    NW[0:64, i, c, :], ALU.mult, ALU.add), reads=[pH, Wc, NW], writes=[Hs])
                hsl = slice(HG * hg, HG * hg + HG)
                op("pool", f_copy(Hhi[0:64, hsl, :], Hs[0:64, hsl, :]), reads=[Hs], writes=[Hhi])
                op("pool", f_tt(Hlo[0:64, hsl, :], Hs[0:64, hsl, :], Hhi[0:64, hsl, :], ALU.subtract), reads=[Hs, Hhi],
                   writes=[Hlo])
                op("dve", f_tt(Y0s[0:64, :, cs_], Y0s[0:64, :, cs_],
                               pS[0:64, :].rearrange("p (a t) -> p a t", a=8), ALU.add), reads=[Y0s, pS],
                   writes=[Y0s])
            pT = nxt()
            for (i, h, pr, ps_, hc) in info:
                op("pe", f_tr(pT[:, 64 * i:64 * i + 64], Y0s[0:64, i, :], idf[0:64, 0:64]), reads=[Y0s, idf],
                   writes=[pT])
            op("act", f_acopy(ysb[:, 512 * hg:512 * hg + 512], pT[:]), reads=[pT], writes=[ysb])
            for _ in range(4):
                next(nxt_front, None)
        for _ in nxt_front:
            pass
        q16 = sm16
        op("dve", f_red(q16["mu"][:], v3(ysb[:]), ALU.add), reads=[ysb], writes=[q16["mu"]])
        op("dve", f_ts(q16["mu"][:], q16["mu"][:], -1.0 / 64, ALU.mult), reads=[q16["mu"]], writes=[q16["mu"]])
        op("dve", f_tt(v3(ysb[:]), v3(ysb[:]), b16(q16["mu"]), ALU.add), reads=[ysb, q16["mu"]], writes=[ysb])
        op("pool", f_tt(t1[:], ysb[:], ysb[:], ALU.mult), reads=[ysb], writes=[t1])
        op("dve", f_red(q16["vs"][:], v3(t1[:]), ALU.add), reads=[t1], writes=[q16["vs"]])
        op("act", f_act(q16["rs"][:], q16["vs"][:], AF.Sqrt, bias=GN_EPS, scale=1.0 / 64), reads=[q16["vs"]],
           writes=[q16["rs"]])
        op("dve", lambda en: en.reciprocal(out=q16["rs"][:], in_=q16["rs"][:]), reads=[q16["rs"]],
           writes=[q16["rs"]])
        op("dve", f_tt(v3(ysb[:]), v3(ysb[:]), b16(q16["rs"]), ALU.mult), reads=[ysb, q16["rs"]], writes=[ysb])
        for hf in range(2):
            cs_ = slice(hf * 512, (hf + 1) * 512)
            p = bc_vec(5, hf)
            op("dve", f_tt(ysb[:, cs_], ysb[:, cs_], p[:], ALU.mult), reads=[ysb, p], writes=[ysb])
            p = bc_vec(6, hf)
            op("dve", f_tt(ysb[:, cs_], ysb[:, cs_], p[:], ALU.add), reads=[ysb, p], writes=[ysb])
        op("pool", f_tt(ysb[:], ysb[:], bon[:], ALU.add), reads=[ysb, bon], writes=[ysb])
        o_ = orb[0]
        op("dve", f_tt(o_[:], ysb[:], g_[:], ALU.mult), reads=[ysb, g_], writes=[o_])
        sc.dma("pool", outr[ti * 128:(ti + 1) * 128, :], o_[:], reads=[o_])
    sc.barrier()
    sc.emit()
    cx.close()


def phase_merge(nc, sc, S, x1, x2, gates, att, outr, W):
    NT = S // 128
    cx = Ctx(nc)
    op = sc.op
    idf, idb = make_ident(sc, cx, BF16)
    wout = cx.sb("wout", [128, 8, D], BF16)
    wo = cx.sb("wo", [128, 8, D], BF16)
    wup = cx.sb("wup", [128, 2, D], BF16)
    stg = [cx.sb("stg%d" % i, [128, D], F32) for i in range(2)]
    load_weight_bf16(sc, stg, wout, lambda kc, f0, fw: wout[:, kc, f0:f0 + fw], W["rwkv_w_out"], D, D, D)
    load_weight_bf16(sc, stg, wo, lambda kc, f0, fw: wo[:, kc, f0:f0 + fw], W["w_o"], D, D, D)
    load_weight_bf16(sc, stg, wup, lambda kc, f0, fw: wup[:, kc, f0:f0 + fw], W["attn_w_up"], 256, D, D)
    xs = [cx.sb("xs%d" % b, [128, D], F32) for b in range(2)]
    gt = [cx.sb("gt%d" % b, [128, 2 * D], BF16) for b in range(2)]
    at = [cx.sb("at%d" % b, [128, 3, 4, 66], F32) for b in range(2)]
    orr = [cx.sb("orr%d" % b, [128, D], BF16) for b in range(2)]
    orT = cx.sb("orT", [128, 8, 128], BF16)
    mgT = cx.sb("mgT", [128, 8, 128], BF16)
    oaT = cx.sb("oaT", [128, 2, 128], BF16)
    mx = cx.sb("mx", [128, 4], F32)
    cc = cx.sb("cc", [128, 3, 4], F32)
    den = cx.sb("den", [128, 4], F32)
    dtmp = cx.sb("dtmp", [128, 4], F32)
    num = cx.sb("num", [128, 4, 64], F32)
    ntmp = cx.sb("ntmp", [128, 4, 64], F32)
    oab = cx.sb("oab", [128, 256], BF16)
    ta = cx.sb("ta", [128, D], F32)
    tb = cx.sb("tb", [128, D], F32)
    mgb = cx.sb("mgb", [128, D], BF16)
    ptp = cx.ps("ptp", [128, D], BF16)
    pya = [cx.ps("pya%d" % i, [128, 512], F32) for i in range(2)]
    pyr = [cx.ps("pyr%d" % i, [128, 512], F32) for i in range(2)]
    pout = [cx.ps("pout%d" % i, [128, 512], F32) for i in range(2)]
    for ti in range(NT):
        b = ti % 2
        rows = slice(ti * 128, (ti + 1) * 128)
        x, g_, a_, o_ = xs[b], gt[b], at[b], orr[b]
        sc.dma("sp", x[:], x1[rows, :], writes=[x])
        sc.dma("sp", g_[:], gates[rows, :], writes=[g_])
        sc.dma("sp", a_[:], att[rows], writes=[a_])
        sc.dma("sp", o_[:], outr[rows, :], writes=[o_])
        for kc in range(8):
            op("pe", f_tr(ptp[:, kc * 128:(kc + 1) * 128], o_[:, kc * 128:(kc + 1) * 128], idb[:]), reads=[o_, idb],
               writes=[ptp])
        op("act", f_acopy(orT[:], ptp[:].rearrange("p (k t) -> p k t", k=8)), reads=[ptp], writes=[orT])
        for hf in range(2):
            for kc in range(8):
                op("pe", f_mm(pyr[hf][:], orT[:, kc, :], wout[:, kc, hf * 512:(hf + 1) * 512], kc == 0, kc == 7),
                   reads=[orT, wout], writes=[pyr[hf]])
        m0, m1, m2_ = (a_[:, g, :, 64] for g in range(3))
        op("dve", f_tt(mx[:], m0, m1, ALU.max), reads=[a_], writes=[mx])
        op("dve", f_tt(mx[:], mx[:], m2_, ALU.max), reads=[a_, mx], writes=[mx])
        for g in range(3):
            op("dve", f_tt(cc[:, g, :], a_[:, g, :, 64], mx[:], ALU.subtract), reads=[a_, mx], writes=[cc])
        op("act", f_act(cc[:], cc[:], AF.Exp), reads=[cc], writes=[cc])
        for g in range(3):
            if g == 0:
                op("dve", f_tt(den[:], cc[:, 0, :], a_[:, 0, :, 65], ALU.mult), reads=[cc, a_], writes=[den])
                op("dve", f_tt(num[:], a_[:, 0, :, 0:64], cc[:, 0, :].unsqueeze(2).broadcast_to([128, 4, 64]),
                               ALU.mult), reads=[cc, a_], writes=[num])
            else:
                op("dve", f_tt(dtmp[:], cc[:, g, :], a_[:, g, :, 65], ALU.mult), reads=[cc, a_], writes=[dtmp])
                op("dve", f_tt(den[:], den[:], dtmp[:], ALU.add), reads=[den, dtmp], writes=[den])
                op("dve", f_tt(ntmp[:], a_[:, g, :, 0:64], cc[:, g, :].unsqueeze(2).broadcast_to([128, 4, 64]),
                               ALU.mult), reads=[cc, a_], writes=[ntmp])
                op("pool", f_tt(num[:], num[:], ntmp[:], ALU.add), reads=[num, ntmp], writes=[num])
        op("dve", lambda en: en.reciprocal(out=den[:], in_=den[:]), reads=[den], writes=[den])
        op("dve", f_tt(oab[:].rearrange("p (h d) -> p h d", h=4), num[:],
                       den[:].unsqueeze(2).broadcast_to([128, 4, 64]), ALU.mult), reads=[num, den], writes=[oab])
        for kc in range(2):
            op("pe", f_tr(ptp[:, kc * 128:(kc + 1) * 128], oab[:, kc * 128:(kc + 1) * 128], idb[:]),
               reads=[oab, idb], writes=[ptp])
        op("act", f_acopy(oaT[:], ptp[:, 0:256].rearrange("p (k t) -> p k t", k=2)), reads=[ptp], writes=[oaT])
        for hf in range(2):
            for kc in range(2):
                op("pe", f_mm(pya[hf][:], oaT[:, kc, :], wup[:, kc, hf * 512:(hf + 1) * 512], kc == 0, kc == 1),
                   reads=[oaT, wup], writes=[pya[hf]])
        for hf in range(2):
            cs_ = slice(hf * 512, (hf + 1) * 512)
            op("dve", f_tt(ta[:, cs_], pya[hf][:], g_[:, hf * 512:(hf + 1) * 512], ALU.mult), reads=[pya[hf], g_],
               writes=[ta])
            op("dve", f_tt(tb[:, cs_], pyr[hf][:], g_[:, D + hf * 512:D + (hf + 1) * 512], ALU.mult),
               reads=[pyr[hf], g_], writes=[tb])
        op("pool", f_tt(mgb[:], ta[:], tb[:], ALU.add), reads=[ta, tb], writes=[mgb])
        for kc in range(8):
            op("pe", f_tr(ptp[:, kc * 128:(kc + 1) * 128], mgb[:, kc * 128:(kc + 1) * 128], idb[:]),
               reads=[mgb, idb], writes=[ptp])
        op("act", f_acopy(mgT[:], ptp[:].rearrange("p (k t) -> p k t", k=8)), reads=[ptp], writes=[mgT])
        for hf in range(2):
            cs_ = slice(hf * 512, (hf + 1) * 512)
            for kc in range(8):
                op("pe", f_mm(pout[hf][:], mgT[:, kc, :], wo[:, kc, cs_], kc == 0, kc == 7), reads=[mgT, wo],
                   writes=[pout[hf]])
            op("dve", f_tt(x[:, cs_], x[:, cs_], pout[hf][:], ALU.add), reads=[x, pout[hf]], writes=[x])
        sc.dma("pool", x2[rows, :], x[:], reads=[x])
    sc.barrier()
    sc.emit()
    cx.close()


def build_nc(S, stages="ABCDE", debug=False):
    nc = bass.Bass("TRN2", target_bir_lowering=False)

    def inp(name, shape):
        return nc.dram_tensor(name, list(shape), F32, kind="ExternalInput").ap()

    def scr(name, shape, dt):
        return nc.dram_tensor(name, list(shape), dt, kind="ExternalOutput" if debug else "Internal").ap()

    W = {}
    for name, shape in WEIGHT_SHAPES:
        W[name] = inp(name, shape)
    x = inp("x", [S, D])
    out = nc.dram_tensor("out", [S, D], F32, kind="ExternalOutput").ap()
    x1 = scr("x1_scr", [S, D], F32)
    x2 = scr("x2_scr", [S, D], F32)
    qkv = scr("qkv_scr", [S, 2304], BF16)
    gates = scr("gates_scr", [S, 2048], BF16)
    att = scr("att_scr", [S, 3, 4, 66], F32)
    outr = scr("outr_scr", [S, D], BF16)
    sc = Sched(nc)
    if "A" in stages:
        phase_ffn(nc, sc, S, x, x1, W["ffn1_norm"], W["ffn1_w_gate"], W["ffn1_w_up"], W["ffn1_w_down"])
    if "B" in stages:
        phase_proj(nc, sc, S, x1, qkv, gates, W["mix_norm"], W["w_in"], W["gate_bias"])
    if "C" in stages:
        phase_attn(nc, sc, S, qkv, att)
    if "D" in stages:
        phase_rwkv(nc, sc, S, x1, outr, W)
        phase_merge(nc, sc, S, x1, x2, gates, att, outr, W)
    if "E" in stages:
        phase_ffn(nc, sc, S, x2 if "D" in stages else x1, out, W["ffn2_norm"], W["ffn2_w_gate"], W["ffn2_w_up"],
                  W["ffn2_w_down"], final_gain=W["final_norm"])
    sc.close()
    return nc


WEIGHT_SHAPES = [
    ("ffn1_norm", [D]), ("ffn1_w_gate", [D, FF]), ("ffn1_w_up", [D, FF]), ("ffn1_w_down", [FF, D]),
    ("mix_norm", [D]), ("w_in", [D, IN_COLS]), ("gate_bias", [2 * D]), ("attn_w_up", [256, D]),
    ("rwkv_mu", [RS]), ("rwkv_w0", [D]), ("rwkv_w2", [64, D]), ("rwkv_a0", [D]), ("rwkv_a2", [64, D]),
    ("rwkv_g2", [160, D]), ("rwkv_k_k", [D]), ("rwkv_k_a", [D]), ("rwkv_r_k", [D]), ("rwkv_ln_w", [D]),
    ("rwkv_ln_b", [D]), ("rwkv_w_out", [D, D]), ("w_o", [D, D]),
    ("ffn2_norm", [D]), ("ffn2_w_gate", [D, FF]), ("ffn2_w_up", [D, FF]), ("ffn2_w_down", [FF, D]),
    ("final_norm", [D]),
]


def make_in_maps(inputs):
    B = inputs["x"].shape[0]
    shared = {}
    for name, shape in WEIGHT_SHAPES:
        a = np.asarray(inputs[name], dtype=np.float32)
        shared[name] = np.ascontiguousarray(a.reshape(shape))
    xin = np.asarray(inputs["x"], dtype=np.float32)
    in_maps = []
    for c in range(B):
        m = dict(shared)
        m["x"] = np.ascontiguousarray(xin[c])
        in_maps.append(m)
    return in_maps


def kernel(**inputs):
    B, S, _ = inputs["x"].shape
    nc = build_nc(S)
    in_maps = make_in_maps(inputs)
    res = run_bass_kernel_spmd(nc, in_maps, core_ids=list(range(B)))
    return np.stack([np.asarray(r["out"]) for r in res.results], axis=0).astype(np.float32)
```

```python
import contextlib
import math
import numpy as np
import concourse.bass as bass
import concourse.mybir as mybir
from concourse.bass_utils import run_bass_kernel_spmd

F32 = mybir.dt.float32
BF16 = mybir.dt.bfloat16
AF = mybir.ActivationFunctionType
ALU = mybir.AluOpType
AX = mybir.AxisListType

D = 1024
FF = 2816
NFC = FF // 128
EPS = 1e-6
NCORES = 8


class Buf:
    __slots__ = ("name", "t", "lw", "rd", "excl")

    def __init__(self, name, t, excl=False):
        self.name = name
        self.t = t
        self.lw = None
        self.rd = {}
        self.excl = excl

    def __getitem__(self, idx):
        return self.t[idx]


class Sched:
    ENGS = ("pe", "act", "dve", "pool", "sp")

    def __init__(self, nc, ndma_sems=6):
        self.nc = nc
        self.ops = {e: [] for e in self.ENGS}
        self.cnt = {}
        self.sem = {}
        self.seen = {e: {} for e in self.ENGS}
        self._cm = []
        for e in self.ENGS:
            cm = nc.semaphore("prog_" + e)
            self.sem[e] = cm.__enter__()
            self._cm.append(cm)
            self.cnt[e] = 0
        self.dq = {}
        for q in ("sp", "pool", "act"):
            ring = []
            for i in range(ndma_sems):
                nm = "dma_%s_%d" % (q, i)
                cm = nc.semaphore(nm)
                self.sem[nm] = cm.__enter__()
                self._cm.append(cm)
                self.cnt[nm] = 0
                ring.append(nm)
            self.dq[q] = [ring, 0]
        self.ninstr = 0
        self.pe_rt = {}

    def close(self):
        for cm in reversed(self._cm):
            cm.__exit__(None, None, None)

    def _wait(self, engine, dep):
        if dep is None:
            return
        e, c = dep
        if self.seen[engine].get(e, 0) >= c:
            return
        self.seen[engine][e] = c
        sem = self.sem[e]
        self.ops[engine].append(lambda en, sem=sem, c=c: en.wait_ge(sem, c))
        self.ninstr += 1

    def _deps(self, engine, reads, writes):
        for b in reads:
            if b.lw is not None:
                if b.lw[0] == engine and engine == "pe":
                    continue
                self._wait(engine, b.lw)
        for b in writes:
            if b.lw is not None and not (b.lw[0] == engine and engine == "pe"):
                self._wait(engine, b.lw)
            for e, c in b.rd.items():
                if e == engine and engine == "pe":
                    continue
                self._wait(engine, (e, c))

    def op(self, engine, fn, reads=(), writes=()):
        ex = [b for b in reads if b.excl and engine != "pe"]
        if ex:
            writes = list(writes) + [b for b in ex if b not in writes]
        if engine == "pe":
            rt = getattr(fn, "rt", None)
            for b in writes:
                if b.lw is not None and b.lw[0] == "pe" and self.pe_rt.get(id(b)) != rt:
                    self._wait("pe", b.lw)
                self.pe_rt[id(b)] = rt
        self._deps(engine, reads, writes)
        self.cnt[engine] += 1
        c = self.cnt[engine]
        sem = self.sem[engine]
        self.ops[engine].append(lambda en, fn=fn, sem=sem: fn(en).then_inc(sem, 1))
        self.ninstr += 1
        for b in reads:
            b.rd[engine] = c
        for b in writes:
            b.lw = (engine, c)
            b.rd = {}
        return c

    def dma(self, q, out_ap, in_ap, reads=(), writes=(), **kw):
        ring, i = self.dq[q]
        nm = ring[i % len(ring)]
        self.dq[q][1] = i + 1
        if self.cnt[nm] > 0:
            self._wait(q, (nm, self.cnt[nm]))
        self._deps(q, reads, writes)
        self.cnt[nm] += 16
        c = self.cnt[nm]
        sem = self.sem[nm]
        self.ops[q].append(
            lambda en, o=out_ap, i_=in_ap, sem=sem, kw=kw: en.dma_start(out=o, in_=i_, **kw).then_inc(sem, 16))
        self.ninstr += 1
        for b in reads:
            b.rd[nm] = c
        for b in writes:
            b.lw = (nm, c)
            b.rd = {}

    def barrier(self):
        for e in self.ENGS:
            for s, c in self.cnt.items():
                if s != e and c > 0:
                    self._wait(e, (s, c))

    def emit(self):
        nc = self.nc
        ops = self.ops
        with nc.Block() as block:
            @block.tensor
            def _(en):
                for f in ops["pe"]:
                    f(en)

            @block.scalar
            def _(en):
                for f in ops["act"]:
                    f(en)

            @block.vector
            def _(en):
                for f in ops["dve"]:
                    f(en)

            @block.gpsimd
            def _(en):
                for f in ops["pool"]:
                    f(en)

            @block.sync
            def _(en):
                for f in ops["sp"]:
                    f(en)
        self.ops = {e: [] for e in self.ENGS}


class Ctx:
    N = [0]

    def __init__(self, nc):
        self.nc = nc
        self.es = contextlib.ExitStack()

    def sb(self, name, shape, dt):
        Ctx.N[0] += 1
        t = self.es.enter_context(self.nc.sbuf_tensor("%s_%d" % (name, Ctx.N[0]), list(shape), dt))
        return Buf(name, t)

    def ps(self, name, shape, dt=F32):
        Ctx.N[0] += 1
        nbytes = int(np.prod(shape[1:])) * (4 if dt == F32 else 2)
        assert nbytes == 2048, (name, shape)
        t = self.es.enter_context(self.nc.psum_tensor("%s_%d" % (name, Ctx.N[0]), list(shape), dt))
        return Buf(name, t, excl=True)

    def close(self):
        self.es.close()


def make_ident(sc, cx, dt):
    idf = cx.sb("identf", [128, 128], F32)
    idb = cx.sb("ident", [128, 128], dt)
    sc.op("pool", lambda en: en.memset(idf[:], 1.0), writes=[idf])
    sc.op("pool", lambda en: en.affine_select(out=idf[:], in_=idf[:], pattern=[[-1, 128]], compare_op=ALU.is_equal,
                                               fill=0.0, base=0, channel_multiplier=1), reads=[idf], writes=[idf])
    sc.op("dve", lambda en: en.tensor_copy(out=idb[:], in_=idf[:]), reads=[idf], writes=[idb])
    return idf, idb


def load_weight_bf16(sc, stg, dst, dst_idx_fn, w_ap, K, F, cw, scale_col=None, qi=[0]):
    nkc = K // 128
    for kc in range(nkc):
        for f0 in range(0, F, cw):
            fw = min(cw, F - f0)
            s = stg[qi[0] % len(stg)]
            q = "sp"
            sc.dma(q, s[:, 0:fw], w_ap[kc * 128:(kc + 1) * 128, f0:f0 + fw], writes=[s])
            eng = ("act", "dve", "pool")[qi[0] % 3]
            qi[0] += 1
            o = dst_idx_fn(kc, f0, fw)
            if scale_col is None:
                if eng == "act":
                    sc.op("act", lambda en, o=o, s=s, fw=fw: en.copy(out=o, in_=s[:, 0:fw]), reads=[s], writes=[dst])
                else:
                    sc.op(eng, lambda en, o=o, s=s, fw=fw: en.tensor_copy(out=o, in_=s[:, 0:fw]), reads=[s],
                          writes=[dst])
            else:
                sca = scale_col[:, kc:kc + 1]
                if eng == "act":
                    sc.op("act", lambda en, o=o, s=s, fw=fw, sca=sca: en.activation(out=o, in_=s[:, 0:fw],
                                                                                       func=AF.Copy, scale=sca),
                          reads=[s, scale_col], writes=[dst])
                else:
                    sc.op(eng, lambda en, o=o, s=s, fw=fw, sca=sca: en.tensor_scalar(
                        out=o, in0=s[:, 0:fw], scalar1=sca, scalar2=None, op0=ALU.mult),
                          reads=[s, scale_col], writes=[dst])


def rms_rstd(sc, x, junk, ss, rstd, eng_sq="act"):
    sc.op("act", lambda en: en.activation(out=junk[:], in_=x[:], func=AF.Square, accum_out=ss[:]),
          reads=[x], writes=[junk, ss])
    sc.op("act", lambda en: en.activation(out=rstd[:], in_=ss[:], func=AF.Sqrt, bias=EPS, scale=1.0 / D),
          reads=[ss], writes=[rstd])
    sc.op("dve", lambda en: en.reciprocal(out=rstd[:], in_=rstd[:]), reads=[rstd], writes=[rstd])


def phase_ffn(nc, sc, S, x_src, x_dst, gain, wg, wu, wd, final_gain=None):
    TF = 256
    NST = TF // 128
    cx = Ctx(nc)
    wgs = cx.sb("wg", [128, 8, FF], BF16)
    wus = cx.sb("wu", [128, 8, FF], BF16)
    wds = cx.sb("wd", [128, NFC, D], BF16)
    stg = [cx.sb("stg%d" % i, [128, 1408], F32) for i in range(2)]
    gcol = cx.sb("gcol", [128, 8], F32)
    idf, idb = make_ident(sc, cx, BF16)
    sc.dma("sp", gcol[:], gain.rearrange("(kc p) -> p kc", p=128), writes=[gcol], allow_slow_non_contiguous=True)
    load_weight_bf16(sc, stg, wgs, lambda kc, f0, fw: wgs[:, kc, f0:f0 + fw], wg, D, FF, 1408, gcol)
    load_weight_bf16(sc, stg, wus, lambda kc, f0, fw: wus[:, kc, f0:f0 + fw], wu, D, FF, 1408, gcol)
    load_weight_bf16(sc, stg, wds, lambda kc, f0, fw: wds[:, kc, f0:f0 + fw], wd, FF, D, 1024, None)
    fg = None
    if final_gain is not None:
        fg = cx.sb("fg", [128, D], F32)
        sc.dma("sp", fg[:], final_gain.partition_broadcast(128), writes=[fg])

    NB = 2
    xs = [[cx.sb("xs%d_%d" % (b, st), [128, D], F32) for st in range(NST)] for b in range(NB)]
    hb = [cx.sb("hb%d" % st, [128, D], BF16) for st in range(NST)]
    hT = [cx.sb("hT%d" % b, [128, 8, TF], BF16) for b in range(NB)]
    aT = [cx.sb("aT%d" % b, [128, NFC, TF], BF16) for b in range(NB)]
    junk = cx.sb("junk", [128, D], F32)
    ss = [cx.sb("ss%d" % i, [128, 1], F32) for i in range(2)]
    rstd = [cx.sb("rstd%d" % i, [128, 1], F32) for i in range(2)]
    sg = [cx.sb("sg%d" % i, [128, TF], F32) for i in range(2)]
    ptp = cx.ps("ptp", [128, D], BF16)
    pg = [cx.ps("pg%d" % i, [128, 512], F32) for i in range(2)]
    pu = [cx.ps("pu%d" % i, [128, 512], F32) for i in range(2)]
    pd = [cx.ps("pd%d" % i, [128, 512], F32) for i in range(2)]

    ntiles = S // TF
    k = 0
    def front(ti):
        b = ti % NB
        t0 = ti * TF
        for st in range(NST):
            x = xs[b][st]
            sc.dma("sp", x[:], x_src[t0 + st * 128:t0 + (st + 1) * 128, :], writes=[x])
            r = rstd[st % 2]
            rms_rstd(sc, x, junk, ss[st % 2], r)
            h = hb[st]
            sc.op("dve", lambda en, h=h, x=x, r=r: en.tensor_scalar(out=h[:], in0=x[:], scalar1=r[:], scalar2=None,
                                                                    op0=ALU.mult), reads=[x, r], writes=[h])
            for kc in range(8):
                sc.op("pe", lambda en, kc=kc, h=h: en.transpose(out=ptp[:, kc * 128:(kc + 1) * 128],
                                                                 in_=h[:, kc * 128:(kc + 1) * 128], identity=idb[:]),
                      reads=[h, idb], writes=[ptp])
            sc.op("act", lambda en, b=b, st=st: en.copy(
                out=hT[b][:, :, st * 128:(st + 1) * 128],
                in_=ptp[:].rearrange("p (k t) -> p k t", k=8)), reads=[ptp], writes=[hT[b]])
    front(0)
    for ti in range(ntiles):
        b = ti % NB
        t0 = ti * TF
        for fc in range(NFC):
            g = pg[fc % 2]
            u = pu[fc % 2]
            for kc in range(8):
                sc.op("pe", lambda en, g=g, kc=kc, fc=fc, b=b: en.matmul(
                    out=g[:, 0:TF], lhsT=wgs[:, kc, fc * 128:(fc + 1) * 128], rhs=hT[b][:, kc, :],
                    start=(kc == 0), stop=(kc == 7)), reads=[wgs, hT[b]], writes=[g])
            for kc in range(8):
                sc.op("pe", lambda en, u=u, kc=kc, fc=fc, b=b: en.matmul(
                    out=u[:, 0:TF], lhsT=wus[:, kc, fc * 128:(fc + 1) * 128], rhs=hT[b][:, kc, :],
                    start=(kc == 0), stop=(kc == 7)), reads=[wus, hT[b]], writes=[u])
            s_ = sg[fc % 2]
            sc.op("act", lambda en, s_=s_, g=g: en.activation(out=s_[:], in_=g[:, 0:TF], func=AF.Silu),
                  reads=[g], writes=[s_])
            sc.op("dve", lambda en, s_=s_, u=u, fc=fc, b=b: en.tensor_tensor(
                out=aT[b][:, fc, :], in0=u[:, 0:TF], in1=s_[:], op=ALU.mult), reads=[u, s_], writes=[aT[b]])
        if ti + 1 < ntiles:
            front(ti + 1)
        for st in range(NST):
            x = xs[b][st]
            for half in range(2):
                p = pd[k % 2]
                k += 1
                for fc in range(NFC):
                    sc.op("pe", lambda en, p=p, fc=fc, st=st, half=half, b=b: en.matmul(
                        out=p[:], lhsT=aT[b][:, fc, st * 128:(st + 1) * 128],
                        rhs=wds[:, fc, half * 512:(half + 1) * 512], start=(fc == 0), stop=(fc == NFC - 1)),
                          reads=[aT[b], wds], writes=[p])
                sc.op("dve", lambda en, p=p, x=x, half=half: en.scalar_tensor_tensor(
                    out=x[:, half * 512:(half + 1) * 512], in0=p[:], scalar=0.5,
                    in1=x[:, half * 512:(half + 1) * 512], op0=ALU.mult, op1=ALU.add), reads=[p, x], writes=[x])
            if fg is not None:
                r = rstd[st % 2]
                rms_rstd(sc, x, junk, ss[st % 2], r)
                sc.op("dve", lambda en, x=x, r=r: en.scalar_tensor_tensor(
                    out=x[:], in0=x[:], scalar=r[:], in1=fg[:], op0=ALU.mult, op1=ALU.mult),
                      reads=[x, r, fg], writes=[x])
            sc.dma("pool", x_dst[t0 + st * 128:t0 + (st + 1) * 128, :], x[:], reads=[x])
    sc.barrier()
    sc.emit()
    cx.close()


I32 = mybir.dt.int32
NH_A = 12
AW = 768
IN_COLS = 7712
RS = 3360
OFF_R = 2304
OFF_G = 2304 + 3360
ROPE_THETA = 500000.0


def norm_transpose(sc, x, junk, ss, rstd, hb, ptp, idb, hT_out_ap, hT_buf):
    rms_rstd(sc, x, junk, ss, rstd)
    sc.op("dve", lambda en: en.tensor_scalar(out=hb[:], in0=x[:], scalar1=rstd[:], scalar2=None, op0=ALU.mult),
          reads=[x, rstd], writes=[hb])
    for kc in range(8):
        sc.op("pe", lambda en, kc=kc: en.transpose(out=ptp[:, kc * 128:(kc + 1) * 128],
                                                    in_=hb[:, kc * 128:(kc + 1) * 128], identity=idb[:]),
              reads=[hb, idb], writes=[ptp])
    sc.op("act", lambda en: en.copy(out=hT_out_ap, in_=ptp[:].rearrange("p (k t) -> p k t", k=8)),
          reads=[ptp], writes=[hT_buf])


def build_rope_tables(sc, cx, NT):
    half = 8
    inv_freq = np.power(np.float32(ROPE_THETA), -np.arange(half, dtype=np.float32) * np.float32(2.0 / 16)).astype(
        np.float32)
    pos = cx.sb("pos", [128, NT], F32)
    sc.op("pool", lambda en: en.iota(out=pos[:], pattern=[[128, NT]], base=0, channel_multiplier=1,
                                     allow_small_or_imprecise_dtypes=True), writes=[pos])
    ang = cx.sb("ang", [128, NT, 8], F32)
    for i in range(half):
        sc.op("dve", lambda en, i=i: en.tensor_scalar(out=ang[:, :, i], in0=pos[:], scalar1=float(inv_freq[i]),
                                                      scalar2=None, op0=ALU.mult), reads=[pos], writes=[ang])
    tabs = []
    for nm, shift in (("cos", math.pi / 2), ("sin", 0.0)):
        b = cx.sb("rb_" + nm, [128, NT, 8], F32)
        ki = cx.sb("rk_" + nm, [128, NT, 8], I32)
        kf = cx.sb("rf_" + nm, [128, NT, 8], F32)
        cr = cx.sb("rc_" + nm, [128, NT, 8], F32)
        tab = cx.sb("tab_" + nm, [128, NT, 8], F32)
        sc.op("dve", lambda en, b=b, shift=shift: en.tensor_scalar(out=b[:], in0=ang[:], scalar1=shift, scalar2=None,
                                                                   op0=ALU.add), reads=[ang], writes=[b])
        sc.op("dve", lambda en, b=b, ki=ki: en.tensor_scalar(out=ki[:], in0=b[:], scalar1=1.0 / (2 * math.pi),
                                                             scalar2=None, op0=ALU.mult), reads=[b], writes=[ki])
        sc.op("dve", lambda en, ki=ki, kf=kf: en.tensor_copy(out=kf[:], in_=ki[:]), reads=[ki], writes=[kf])
        sc.op("dve", lambda en, b=b, kf=kf: en.scalar_tensor_tensor(out=b[:], in0=kf[:], scalar=-2 * math.pi,
                                                                    in1=b[:], op0=ALU.mult, op1=ALU.add),
              reads=[kf, b], writes=[b])
        sc.op("dve", lambda en, b=b, cr=cr: en.tensor_scalar(out=cr[:], in0=b[:], scalar1=math.pi,
                                                             scalar2=-2 * math.pi, op0=ALU.is_gt, op1=ALU.mult),
              reads=[b], writes=[cr])
        sc.op("dve", lambda en, b=b, cr=cr: en.tensor_tensor(out=b[:], in0=b[:], in1=cr[:], op=ALU.add),
              reads=[b, cr], writes=[b])
        sc.op("dve", lambda en, b=b, cr=cr: en.tensor_scalar(out=cr[:], in0=b[:], scalar1=-math.pi,
                                                             scalar2=2 * math.pi, op0=ALU.is_lt, op1=ALU.mult),
              reads=[b], writes=[cr])
        sc.op("dve", lambda en, b=b, cr=cr: en.tensor_tensor(out=b[:], in0=b[:], in1=cr[:], op=ALU.add),
              reads=[b, cr], writes=[b])
        sc.op("act", lambda en, b=b, tab=tab: en.activation(out=tab[:], in_=b[:], func=AF.Sin), reads=[b],
              writes=[tab])
        tabs.append(tab)
    return tabs


def phase_proj(nc, sc, S, x1, qkv, gates, mix_norm, w_in, gate_bias):
    NT = S // 128
    cx = Ctx(nc)
    NC_ = 2304 + 2048
    ws = cx.sb("win", [128, 8, NC_], BF16)
    stg = [cx.sb("stg%d" % i, [128, 1152], F32) for i in range(2)]
    gcol = cx.sb("gcol", [128, 8], F32)
    idf, idb = make_ident(sc, cx, BF16)
    sc.dma("sp", gcol[:], mix_norm.rearrange("(kc p) -> p kc", p=128), writes=[gcol], allow_slow_non_contiguous=True)
    load_weight_bf16(sc, stg, ws, lambda kc, f0, fw: ws[:, kc, f0:f0 + fw], w_in[:, 0:2304], D, 2304, 1152, gcol)
    load_weight_bf16(sc, stg, ws, lambda kc, f0, fw: ws[:, kc, 2304 + f0:2304 + f0 + fw], w_in[:, OFF_G:OFF_G + 2048],
                     D, 2048, 1024, gcol)
    gb = cx.sb("gb", [128, 2048], F32)
    sc.dma("sp", gb[:], gate_bias.partition_broadcast(128), writes=[gb])
    cos_t, sin_t = build_rope_tables(sc, cx, NT)

    xs = [cx.sb("xs%d" % b, [128, D], F32) for b in range(2)]
    hb = cx.sb("hb", [128, D], BF16)
    hT = [cx.sb("hT%d" % b, [128, 8, 128], BF16) for b in range(2)]
    junk = cx.sb("junk", [128, D], F32)
    ss = cx.sb("ss", [128, 1], F32)
    rstd = cx.sb("rstd", [128, 1], F32)
    qs = [cx.sb("qs%d" % b, [128, 2304], BF16) for b in range(2)]
    gs = [cx.sb("gs%d" % b, [128, 2048], BF16) for b in range(2)]
    gf = cx.sb("gf", [128, 512], F32)
    rt = [cx.sb("rt%d" % i, [128, 24, 8], F32) for i in range(4)]
    ptp = cx.ps("ptp", [128, D], BF16)
    pp = [cx.ps("pp%d" % i, [128, 512], F32) for i in range(6)]
    pk = 0
    def front(ti):
        x = xs[ti % 2]
        sc.dma("sp", x[:], x1[ti * 128:(ti + 1) * 128, :], writes=[x])
        norm_transpose(sc, x, junk, ss, rstd, hb, ptp, idb, hT[ti % 2][:], hT[ti % 2])

    front(0)
    for ti in range(NT):
        b = ti % 2
        qk_ps = []
        for c0 in range(0, 2304, 512):
            cw = min(512, 2304 - c0)
            p = pp[pk % 6]
            pk += 1
            for kc in range(8):
                sc.op("pe", lambda en, p=p, kc=kc, c0=c0, cw=cw, b=b: en.matmul(
                    out=p[:, 0:cw], lhsT=hT[b][:, kc, :], rhs=ws[:, kc, c0:c0 + cw], start=(kc == 0), stop=(kc == 7)),
                      reads=[hT[b], ws], writes=[p])
            qk_ps.append((p, c0, cw))
        if ti + 1 < NT:
            front(ti + 1)
        q_ = qs[b]
        for i, (p, c0, cw) in enumerate(qk_ps):
            eng = "act" if i % 2 == 0 else "dve"
            if eng == "act":
                sc.op("act", lambda en, p=p, c0=c0, cw=cw, q_=q_: en.copy(out=q_[:, c0:c0 + cw], in_=p[:, 0:cw]),
                      reads=[p], writes=[q_])
            else:
                sc.op("dve", lambda en, p=p, c0=c0, cw=cw, q_=q_: en.tensor_copy(out=q_[:, c0:c0 + cw],
                                                                                  in_=p[:, 0:cw]),
                      reads=[p], writes=[q_])
        cosb = cos_t[:, ti, :].unsqueeze(1).broadcast_to([128, 8, 8])
        sinb = sin_t[:, ti, :].unsqueeze(1).broadcast_to([128, 8, 8])
        for i in range(3):
            p = qk_ps[i][0]
            pv = p[:].rearrange("p (h d) -> p h d", h=8)
            qv = q_[:, i * 512:(i + 1) * 512].rearrange("p (h d) -> p h d", h=8)
            x1v = pv[:, :, 0:8]
            x2v = pv[:, :, 8:16]
            hs = slice(i * 8, (i + 1) * 8)
            sc.op("dve", lambda en, x1v=x1v, hs=hs, cosb=cosb: en.tensor_tensor(out=rt[0][:, hs, :], in0=x1v, in1=cosb,
                                                                      op=ALU.mult), reads=[p, cos_t], writes=[rt[0]])
            sc.op("dve", lambda en, x2v=x2v, hs=hs, sinb=sinb: en.tensor_tensor(out=rt[1][:, hs, :], in0=x2v, in1=sinb,
                                                                      op=ALU.mult), reads=[p, sin_t], writes=[rt[1]])
            sc.op("dve", lambda en, x2v=x2v, hs=hs, cosb=cosb: en.tensor_tensor(out=rt[2][:, hs, :], in0=x2v, in1=cosb,
                                                                      op=ALU.mult), reads=[p, cos_t], writes=[rt[2]])
            sc.op("dve", lambda en, x1v=x1v, hs=hs, sinb=sinb: en.tensor_tensor(out=rt[3][:, hs, :], in0=x1v, in1=sinb,
                                                                      op=ALU.mult), reads=[p, sin_t], writes=[rt[3]])
            sc.op("dve", lambda en, qv=qv, hs=hs: en.tensor_tensor(out=qv[:, :, 0:8], in0=rt[0][:, hs, :],
                                                                    in1=rt[1][:, hs, :], op=ALU.subtract),
                  reads=[rt[0], rt[1]], writes=[q_])
            sc.op("dve", lambda en, qv=qv, hs=hs: en.tensor_tensor(out=qv[:, :, 8:16], in0=rt[2][:, hs, :],
                                                                    in1=rt[3][:, hs, :], op=ALU.add),
                  reads=[rt[2], rt[3]], writes=[q_])
        sc.dma("pool", qkv[ti * 128:(ti + 1) * 128, :], q_[:], reads=[q_])
        g_ = gs[b]
        for c0 in range(0, 2048, 512):
            p = pp[pk % 6]
            pk += 1
            for kc in range(8):
                sc.op("pe", lambda en, p=p, kc=kc, c0=c0, b=b: en.matmul(
                    out=p[:], lhsT=hT[b][:, kc, :], rhs=ws[:, kc, 2304 + c0:2304 + c0 + 512], start=(kc == 0),
                    stop=(kc == 7)), reads=[hT[b], ws], writes=[p])
            sc.op("dve", lambda en, p=p, c0=c0: en.tensor_tensor(out=gf[:], in0=p[:], in1=gb[:, c0:c0 + 512],
                                                                  op=ALU.add), reads=[p, gb], writes=[gf])
            sc.op("act", lambda en, c0=c0, g_=g_: en.activation(out=g_[:, c0:c0 + 512], in_=gf[:], func=AF.Sigmoid),
                  reads=[gf], writes=[g_])
        sc.dma("pool", gates[ti * 128:(ti + 1) * 128, :], g_[:], reads=[g_])
    sc.barrier()
    sc.emit()
    cx.close()


ATT_GROUPS = ((128, 1), (512, 4), (2048, 16))
NEG = -30000.0


import os
ATT_LEVEL = int(os.environ.get('ATT_LEVEL', '9'))


def phase_attn(nc, sc, S, qkv, att):
    cx = Ctx(nc)
    idf, idb = make_ident(sc, cx, BF16)
    mask = cx.sb("mask", [128, 256], F32)
    mask0 = cx.sb("mask0", [128, 256], F32)
    sc.op("pool", lambda en: en.memset(mask[:], 0.0), writes=[mask])
    sc.op("pool", lambda en: en.affine_select(out=mask[:, 0:128], in_=mask[:, 0:128], pattern=[[1, 128]],
                                               compare_op=ALU.is_ge, fill=NEG, base=0, channel_multiplier=-1),
          reads=[mask], writes=[mask])
    sc.op("pool", lambda en: en.affine_select(out=mask[:, 128:256], in_=mask[:, 128:256], pattern=[[-1, 128]],
                                               compare_op=ALU.is_ge, fill=NEG, base=0, channel_multiplier=1),
          reads=[mask], writes=[mask])
    sc.op("pool", lambda en: en.tensor_copy(out=mask0[:], in_=mask[:]), reads=[mask], writes=[mask0])
    sc.op("pool", lambda en: en.memset(mask0[:, 0:128], NEG), reads=[mask0], writes=[mask0])

    qb = [cx.sb("qb%d" % i, [128, 256], BF16) for i in range(2)]
    kb = [cx.sb("kb%d" % i, [128, 256], BF16) for i in range(2)]
    vb = [cx.sb("vb%d" % i, [128, 256], BF16) for i in range(3)]
    QT = [cx.sb("QT%d" % i, [128, 2, 128], BF16) for i in range(2)]
    KT = [cx.sb("KT%d" % i, [128, 2, 128], BF16) for i in range(3)]
    for t in vb + KT:
        sc.op("pool", lambda en, t=t: en.memset(t[:], 0.0), writes=[t])
    sm = [cx.sb("sm%d" % i, [128, 2, 256], F32) for i in range(2)]
    m2 = [cx.sb("m2_%d" % i, [128, 2], F32) for i in range(2)]
    nb2 = [cx.sb("nb2_%d" % i, [128, 2], F32) for i in range(2)]
    pb = [cx.sb("pb%d" % i, [128, 256], BF16) for i in range(4)]
    PT = [cx.sb("PT%d" % i, [128, 2, 128], BF16) for i in range(4)]
    ob = [cx.sb("ob%d" % i, [128, 4, 66], F32) for i in range(2)]
    ptq = [cx.ps("ptq%d" % i, [128, 8, 128], BF16) for i in range(2)]
    sp = [cx.ps("sp%d" % i, [128, 2, 256], F32) for i in range(2)]
    ptp = [cx.ps("ptpp%d" % i, [128, 8, 128], BF16) for i in range(2)]
    po = [cx.ps("po%d" % i, [128, 8, 64], F32) for i in range(2)]

    hkc = [0]
    blocks = []
    for g, (window, d) in enumerate(ATT_GROUPS):
        for r in range(d):
            for i in range((S // d) // 128):
                blocks.append((g, d, r, i))

    def front(blk):
        g, d, r, i = blocks[blk]
        qv = qkv.rearrange("(l d) c -> d l c", d=d)
        rows = slice(i * 128, (i + 1) * 128)
        q_, k_, vcur, kcur, qt = qb[blk % 2], kb[blk % 2], vb[blk % 3], KT[blk % 3], QT[blk % 2]
        sc.dma("sp", q_[:], qv[r, rows, 256 * g:256 * g + 256], writes=[q_])
        sc.dma("sp", k_[:], qv[r, rows, 768 + 256 * g:768 + 256 * g + 256], writes=[k_])
        sc.dma("sp", vcur[:], qv[r, rows, 1536 + 256 * g:1536 + 256 * g + 256], writes=[vcur])
        for src, dst in ((q_, qt), (k_, kcur)):
            pt = ptq[hkc[0] % 2]
            hkc[0] += 1
            for pr in range(2):
                sc.op("pe", f_tr(pt[:, pr, :], src[:, pr * 128:(pr + 1) * 128], idb[:]), reads=[src, idb],
                      writes=[pt])
            sc.op("act", f_acopy(dst[:], pt[:, 0:2, :]), reads=[pt], writes=[dst])

    front(0)
    for blk in range(len(blocks)):
        if True:
            if True:
                g, d, r, i = blocks[blk]
                av = att.rearrange("(l d) g h e -> d l g (h e)", d=d)
                rows = slice(i * 128, (i + 1) * 128)
                vcur = vb[blk % 3]
                vprev = vb[(blk - 1) % 3]
                kcur = KT[blk % 3]
                kprev = KT[(blk - 1) % 3]
                qt = QT[blk % 2]
                o_ = ob[blk % 2]
                po_ = po[blk % 2]
                mk = mask0 if i == 0 else mask
                for pr in range(2):
                    s_ = sp[pr]
                    for hh in range(2):
                        ps_ = slice(64 * hh, 64 * hh + 64)
                        sc.op("pe", f_mm(s_[:, hh, 0:128], qt[ps_, pr, :], kprev[ps_, pr, :]), reads=[qt, kprev],
                              writes=[s_])
                        sc.op("pe", f_mm(s_[:, hh, 128:256], qt[ps_, pr, :], kcur[ps_, pr, :]), reads=[qt, kcur],
                              writes=[s_])
                if blk + 1 < len(blocks):
                    front(blk + 1)
                for pr in range(2):
                    s_, sm_, m_, n_ = sp[pr], sm[pr], m2[pr], nb2[pr]
                    sc.op("dve", f_tt(sm_[:], s_[:], mk[:].unsqueeze(1).broadcast_to([128, 2, 256]), ALU.add),
                          reads=[s_, mk], writes=[sm_])
                    sc.op("dve", f_red(m_[:], sm_[:], ALU.max), reads=[sm_], writes=[m_])
                    sc.op("dve", f_ts(n_[:], m_[:], -0.125, ALU.mult), reads=[m_], writes=[n_])
                    sc.op("dve", f_ts(o_[:, 2 * pr:2 * pr + 2, 64], m_[:], 0.125, ALU.mult), reads=[m_], writes=[o_])
                for head in range(4):
                    pr, hh = head // 2, head % 2
                    sc.op("act", f_act(pb[head][:], sm[pr][:, hh, :], AF.Exp, bias=nb2[pr][:, hh:hh + 1], scale=0.125,
                                       accum=o_[:, head, 65:66]), reads=[sm[pr], nb2[pr]], writes=[pb[head], o_])
                for head in range(4):
                    tp = ptp[head % 2]
                    for half in range(2):
                        sc.op("pe", f_tr(tp[:, half, :], pb[head][:, half * 128:(half + 1) * 128], idb[:]),
                              reads=[pb[head], idb], writes=[tp])
                    sc.op("dve", f_copy(PT[head][:], tp[:, 0:2, :]), reads=[tp], writes=[PT[head]])
                for head in range(4):
                    pT = PT[head]
                    sc.op("pe", f_mm(po_[:, head, :], pT[:, 0, :], vprev[:, head * 64:(head + 1) * 64], True, False),
                          reads=[pT, vprev], writes=[po_])
                    sc.op("pe", f_mm(po_[:, head, :], pT[:, 1, :], vcur[:, head * 64:(head + 1) * 64], False, True),
                          reads=[pT, vcur], writes=[po_])
                sc.op("act", f_acopy(o_[:, :, 0:64], po_[:, 0:4, :]), reads=[po_], writes=[o_])
                sc.dma("pool", av[r, rows, g, :], o_[:].rearrange("p h e -> p (h e)"), reads=[o_])
    sc.barrier()
    sc.emit()
    cx.close()


def f_tt(o, a, b, op):
    return lambda en: en.tensor_tensor(out=o, in0=a, in1=b, op=op)


def f_ts(o, a, s1, op0, s2=None, op1=None):
    if op1 is None:
        return lambda en: en.tensor_scalar(out=o, in0=a, scalar1=s1, scalar2=None, op0=op0)
    return lambda en: en.tensor_scalar(out=o, in0=a, scalar1=s1, scalar2=s2, op0=op0, op1=op1)


def f_stt(o, a, sca, b, op0, op1):
    return lambda en: en.scalar_tensor_tensor(out=o, in0=a, scalar=sca, in1=b, op0=op0, op1=op1)


def f_act(o, a, func, bias=None, scale=None, accum=None):
    kw = {}
    if bias is not None:
        kw["bias"] = bias
    if scale is not None:
        kw["scale"] = scale
    if accum is not None:
        kw["accum_out"] = accum
    return lambda en: en.activation(out=o, in_=a, func=func, **kw)


def f_acopy(o, a):
    return lambda en: en.copy(out=o, in_=a)


def f_copy(o, a):
    return lambda en: en.tensor_copy(out=o, in_=a)


def f_red(o, a, op):
    return lambda en: en.tensor_reduce(out=o, in_=a, axis=AX.X, op=op)


def f_mm(o, l, r, st=True, sp=True):
    fn = lambda en: en.matmul(out=o, lhsT=l, rhs=r, start=st, stop=sp)
    fn.rt = (l.base_partition(), l.partition_size())
    return fn


def f_tr(o, a, ident):
    fn = lambda en: en.transpose(out=o, in_=a, identity=ident)
    fn.rt = (a.base_partition(), a.partition_size())
    return fn


def f_memset(o, v):
    return lambda en: en.memset(o, v)


GN_EPS = 64e-5
VEC_NAMES = ["rwkv_w0", "rwkv_a0", "rwkv_k_k", "rwkv_k_a", "rwkv_r_k", "rwkv_ln_w", "rwkv_ln_b"]


def phase_rwkv(nc, sc, S, x1, outr, W):
    NT = S // 128
    cx = Ctx(nc)
    op = sc.op
    idf, idb = make_ident(sc, cx, BF16)
    wr = cx.sb("wr", [128, 8, RS], BF16)
    stg = [cx.sb("stg%d" % i, [128, 1120], F32) for i in range(1)]
    gcol = cx.sb("gcol", [128, 8], F32)
    sc.dma("sp", gcol[:], W["mix_norm"].rearrange("(kc p) -> p kc", p=128), writes=[gcol],
           allow_slow_non_contiguous=True)
    load_weight_bf16(sc, stg, wr, lambda kc, f0, fw: wr[:, kc, f0:f0 + fw], W["w_in"][:, OFF_R:OFF_R + RS], D, RS,
                     1120, gcol)
    w2a2 = cx.sb("w2a2", [128, D], BF16)
    g2s = cx.sb("g2s", [128, 2, D], BF16)
    for (dst_ap, src, rows) in ((w2a2[0:64, :], W["rwkv_w2"], 64), (w2a2[64:128, :], W["rwkv_a2"], 64),
                                (g2s[:, 0, :], W["rwkv_g2"][0:128, :], 128), (g2s[0:32, 1, :], W["rwkv_g2"][128:160, :], 32)):
        s_ = stg[0]
        base = 64 if dst_ap is not None and rows == 64 and src is W["rwkv_a2"] else 0
        sc.dma("sp", s_[base:base + rows, 0:D], src, writes=[s_])
        op("dve", f_copy(dst_ap, s_[base:base + rows, 0:D]), reads=[s_], writes=[w2a2, g2s])
    vecs = cx.sb("vecs", [8, D], F32)
    op("dve", f_memset(vecs[:], 0.0), writes=[vecs])
    for j, nm in enumerate(VEC_NAMES):
        sc.dma("sp", vecs[j:j + 1, :], W[nm].rearrange("(o n) -> o n", o=1), writes=[vecs])
    muv = cx.sb("muv", [8, 512], F32)
    op("dve", f_memset(muv[:], 0.0), writes=[muv])
    sc.dma("sp", muv[0:6, :], W["rwkv_mu"][0:3072].rearrange("(j c) -> j c", c=512), reads=[], writes=[muv])
    sc.dma("sp", muv[6:7, 0:288], W["rwkv_mu"][3072:3360].rearrange("(o n) -> o n", o=1), writes=[muv])
    selm = cx.sb("selm", [8, 8, 128], F32)
    op("pool", f_memset(selm[:], 1.0), writes=[selm])
    op("pool", lambda en: en.affine_select(out=selm[:], in_=selm[:], pattern=[[-1, 8], [0, 128]],
                                           compare_op=ALU.is_equal, fill=0.0, base=0, channel_multiplier=1),
       reads=[selm], writes=[selm])
    selb = cx.sb("selb", [8, 8, 128], BF16)
    vhi = cx.sb("vhi", [8, D], BF16)
    vlo = cx.sb("vlo", [8, D], BF16)
    mhi = cx.sb("mhi", [8, 512], BF16)
    mlo = cx.sb("mlo", [8, 512], BF16)
    op("dve", f_copy(selb[:], selm[:]), reads=[selm], writes=[selb])
    for (src, hi, lo) in ((vecs, vhi, vlo), (muv, mhi, mlo)):
        op("dve", f_copy(hi[:], src[:]), reads=[src], writes=[hi])
        op("dve", f_tt(lo[:], src[:], hi[:], ALU.subtract), reads=[src, hi], writes=[lo])
    m_ui = cx.sb("m_ui", [128, 128], F32)
    m_su = cx.sb("m_su", [128, 128], F32)
    m_sl = cx.sb("m_sl", [128, 128], F32)
    mask4 = cx.sb("mask4", [128, 512], F32)
    ech = cx.sb("ech", [128, 2], F32)
    for m, pat, cm, cmp_ in ((m_ui, [[1, 128]], -1, ALU.is_ge), (m_su, [[1, 128]], -1, ALU.is_gt),
                             (m_sl, [[-1, 128]], 1, ALU.is_gt)):
        op("pool", f_memset(m[:], 1.0), writes=[m])
        op("pool", lambda en, m=m, pat=pat, cm=cm, cmp_=cmp_: en.affine_select(
            out=m[:], in_=m[:], pattern=pat, compare_op=cmp_, fill=0.0, base=0, channel_multiplier=cm),
           reads=[m], writes=[m])
        op("pool", f_memset(m[0:64, 64:128], 0.0), reads=[m], writes=[m])
        op("pool", f_memset(m[64:128, 0:64], 0.0), reads=[m], writes=[m])
    for i, m in enumerate((m_su, m_ui, m_su, m_ui)):
        op("pool", f_copy(mask4[:, i * 128:(i + 1) * 128], m[:]), reads=[m], writes=[mask4])
    op("pool", f_memset(ech[:], 0.0), writes=[ech])
    op("pool", f_memset(ech[0:64, 0:1], 1.0), reads=[ech], writes=[ech])
    op("pool", f_memset(ech[64:128, 1:2], 1.0), reads=[ech], writes=[ech])

    xs = [cx.sb("xs%d" % b, [128, D], F32) for b in range(1)]
    hb = cx.sb("hb", [128, D], BF16)
    hTc = cx.sb("hTc", [128, 8, 128], BF16)
    hTp = cx.sb("hTp", [128, 8, 128], BF16)
    carry = cx.sb("carry", [128, 8, 1], BF16)
    op("pool", f_memset(carry[:], 0.0), writes=[carry])
    ss = cx.sb("ss", [128, 1], F32)
    rstd = cx.sb("rstd", [128, 1], F32)
    z = cx.sb("z", [128, RS], F32)
    junkb = cx.sb("junkb", [128, D], BF16)
    bon = cx.sb("bon", [128, D], BF16)
    ztmp = cx.sb("ztmp", [128, 512], F32)
    lor = cx.sb("lor", [128, 288], BF16)
    lorT = cx.sb("lorT", [128, 3, 128], BF16)
    TM = {n: cx.sb("tm_" + n, [128, D], BF16 if n in ("At", "Bt", "Kt", "Rt", "vb") else F32)
          for n in ("lw", "a", "g", "kk", "kp", "E", "At", "Bt", "Kt", "Rt", "t1", "vb")}
    arT = cx.sb("arT", [128, 8, 256], BF16)
    bkT = cx.sb("bkT", [128, 8, 256], BF16)
    Hs = cx.sb("Hs", [64, 16, 64], F32)
    op("pool", f_memset(Hs[:], 0.0), writes=[Hs])
    Hhi = cx.sb("Hhi", [64, 16, 64], BF16)
    Hlo = cx.sb("Hlo", [64, 16, 64], BF16)
    op("pool", f_memset(Hhi[:], 0.0), writes=[Hhi])
    op("pool", f_memset(Hlo[:], 0.0), writes=[Hlo])
    Wc = cx.sb("Wc", [64, 32], F32)
    sm16 = {n: cx.sb("s16_" + n, [128, 16], F32) for n in ("ssq", "rn", "rks", "mu", "vs", "rs")}
    HG = 8
    M4 = [cx.sb("M4_%d" % i, [128, 512], BF16) for i in range(HG)]
    PP = [[cx.sb("PP%d_%d" % (i, j), [128, 256], BF16) for j in range(2)] for i in range(HG)]
    TTb = [[cx.sb("TT%d_%d" % (i, j), [128, 128], BF16) for j in range(2)] for i in range(HG)]
    Zs = [cx.sb("Zs%d" % i, [128, 64], BF16) for i in range(HG)]
    AU = [cx.sb("AU%d" % i, [128, 128], BF16) for i in range(HG)]
    QT = [cx.sb("QTs%d" % i, [64, 128], BF16) for i in range(HG)]
    GTs = cx.sb("GTs", [64, HG, 2, 64], BF16)
    NW = cx.sb("NW", [64, HG, 2, 64], F32)
    Y0s = cx.sb("Y0s", [64, HG, 128], F32)
    orb = [cx.sb("orb%d" % i, [128, D], BF16) for i in range(1)]
    ptp = cx.ps("ptp", [128, D], BF16)
    ptp2 = cx.ps("ptp2", [128, D], BF16)
    ring = [cx.ps("pr%d" % i, [128, 512], F32) for i in range(4)]
    py = [cx.ps("py%d" % i, [128, 512], F32) for i in range(2)]
    rk = [0]

    def nxt():
        p = ring[rk[0] % len(ring)]
        rk[0] += 1
        return p

    def bc_vec(j, hf):
        p = nxt()
        op("pe", f_mm(p[:], selb[0:8, j, :], vhi[0:8, hf * 512:(hf + 1) * 512], True, False), reads=[selb, vhi],
           writes=[p])
        op("pe", f_mm(p[:], selb[0:8, j, :], vlo[0:8, hf * 512:(hf + 1) * 512], False, True), reads=[selb, vlo],
           writes=[p])
        return p

    zr, zk, zv = z[:, 0:D], z[:, D:2 * D], z[:, 2 * D:3 * D]

    def v3(ap):
        return ap.rearrange("p (h d) -> p h d", h=16)

    def b16(buf):
        return buf[:].unsqueeze(2).broadcast_to([128, 16, 64])

    def front1(ti):
        x = xs[0]
        sc.dma("sp", x[:], x1[ti * 128:(ti + 1) * 128, :], writes=[x])
        norm_transpose(sc, x, junkb, ss, rstd, hb, ptp, idb, hTc[:], hTc)
        op("dve", f_copy(hTp[:, :, 0:1], carry[:]), reads=[carry], writes=[hTp])
        op("dve", f_copy(hTp[:, :, 1:128], hTc[:, :, 0:127]), reads=[hTc], writes=[hTp])
        op("dve", f_copy(carry[:], hTc[:, :, 127:128]), reads=[hTc], writes=[carry])
        yield
        for cb in range(7):
            c0 = cb * 512
            cw = min(512, RS - c0)
            pP = nxt()
            pQ = nxt()
            pM = nxt()
            for kc in range(8):
                op("pe", f_mm(pP[:, 0:cw], hTc[:, kc, :], wr[:, kc, c0:c0 + cw], kc == 0, kc == 7), reads=[hTc, wr],
                   writes=[pP])
            for kc in range(8):
                op("pe", f_mm(pQ[:, 0:cw], hTp[:, kc, :], wr[:, kc, c0:c0 + cw], kc == 0, kc == 7), reads=[hTp, wr],
                   writes=[pQ])
            op("pe", f_mm(pM[:, 0:cw], selb[0:8, cb, :], mhi[0:8, 0:cw], True, False), reads=[selb, mhi], writes=[pM])
            op("pe", f_mm(pM[:, 0:cw], selb[0:8, cb, :], mlo[0:8, 0:cw], False, True), reads=[selb, mlo], writes=[pM])
            op("act", f_acopy(z[:, c0:c0 + cw], pP[:, 0:cw]), reads=[pP], writes=[z])
            op("dve", f_tt(ztmp[:, 0:cw], pQ[:, 0:cw], z[:, c0:c0 + cw], ALU.subtract), reads=[pQ, z], writes=[ztmp])
            op("dve", f_tt(ztmp[:, 0:cw], ztmp[:, 0:cw], pM[:, 0:cw], ALU.mult), reads=[ztmp, pM], writes=[ztmp])
            op("pool", f_tt(z[:, c0:c0 + cw], z[:, c0:c0 + cw], ztmp[:, 0:cw], ALU.add), reads=[z, ztmp], writes=[z])
            yield

    nxt_front = front1(0)
    for _ in nxt_front:
        pass
    for ti in range(NT):
        nxt_front = front1(ti + 1) if ti + 1 < NT else iter(())
        op("act", f_act(lor[:, 0:64], z[:, 3072:3136], AF.Tanh), reads=[z], writes=[lor])
        op("act", f_act(lor[:, 128:288], z[:, 3200:3360], AF.Sigmoid), reads=[z], writes=[lor])
        op("dve", f_copy(lor[:, 64:128], z[:, 3136:3200]), reads=[z], writes=[lor])
        op("pe", f_tr(ptp[:, 0:128], lor[:, 0:128], idb[:]), reads=[lor, idb], writes=[ptp])
        op("pe", f_tr(ptp[:, 128:256], lor[:, 128:256], idb[:]), reads=[lor, idb], writes=[ptp])
        op("pe", f_tr(ptp[0:32, 256:384], lor[:, 256:288], idb[:]), reads=[lor, idb], writes=[ptp])
        op("act", f_acopy(lorT[:, 0:2, :], ptp[:, 0:256].rearrange("p (a t) -> p a t", a=2)), reads=[ptp], writes=[lorT])
        op("act", f_acopy(lorT[0:32, 2, :], ptp[0:32, 256:384]), reads=[ptp], writes=[lorT])
        ysb = TM["a"]
        lw, a_, g_, kk, kp, E, At, Bt, Kt, Rt, t1 = (TM[n] for n in ("lw", "a", "g", "kk", "kp", "E", "At", "Bt", "Kt",
                                                                      "Rt", "t1"))
        for hf in range(2):
            cs_ = slice(hf * 512, (hf + 1) * 512)
            p = nxt()
            op("pe", f_mm(p[:], lorT[0:64, 0, :], w2a2[0:64, cs_], True, False), reads=[lorT, w2a2], writes=[p])
            op("pe", f_mm(p[:], selb[0:8, 0, :], vhi[0:8, cs_], False, False), reads=[selb, vhi], writes=[p])
            op("pe", f_mm(p[:], selb[0:8, 0, :], vlo[0:8, cs_], False, True), reads=[selb, vlo], writes=[p])
            op("act", f_act(lw[:, cs_], p[:], AF.Sigmoid), reads=[p], writes=[lw])
            p = nxt()
            op("pe", f_mm(p[:], lorT[64:128, 0, :], w2a2[64:128, cs_], True, False), reads=[lorT, w2a2], writes=[p])
            op("pe", f_mm(p[:], selb[0:8, 1, :], vhi[0:8, cs_], False, False), reads=[selb, vhi], writes=[p])
            op("pe", f_mm(p[:], selb[0:8, 1, :], vlo[0:8, cs_], False, True), reads=[selb, vlo], writes=[p])
            op("act", f_act(a_[:, cs_], p[:], AF.Sigmoid), reads=[p], writes=[a_])
            p = nxt()
            op("pe", f_mm(p[:], lorT[:, 1, :], g2s[:, 0, cs_], True, False), reads=[lorT, g2s], writes=[p])
            op("pe", f_mm(p[:], lorT[0:32, 2, :], g2s[0:32, 1, cs_], False, True), reads=[lorT, g2s], writes=[p])
            op("act", f_acopy(g_[:, cs_], p[:]), reads=[p], writes=[g_])
        op("pool", f_ts(lw[:], lw[:], -math.exp(-0.5), ALU.mult), reads=[lw], writes=[lw])
        for hf in range(2):
            cs_ = slice(hf * 512, (hf + 1) * 512)
            p = bc_vec(2, hf)
            op("dve", f_tt(kk[:, cs_], zk[:, cs_], p[:], ALU.mult), reads=[z, p], writes=[kk])
            p = bc_vec(3, hf)
            op("dve", f_stt(t1[:, cs_], a_[:, cs_], -1.0, p[:], ALU.add, ALU.mult), reads=[a_, p], writes=[t1])
            p = bc_vec(4, hf)
            op("dve", f_tt(E[:, cs_], zr[:, cs_], p[:], ALU.mult), reads=[z, p], writes=[E])
        op("dve", f_stt(kp[:], t1[:], 1.0, zk, ALU.add, ALU.mult), reads=[t1, z], writes=[kp])
        op("pool", f_tt(t1[:], kk[:], kk[:], ALU.mult), reads=[kk], writes=[t1])
        op("dve", f_red(sm16["ssq"][:], v3(t1[:]), ALU.add), reads=[t1], writes=[sm16["ssq"]])
        op("act", f_act(sm16["rn"][:], sm16["ssq"][:], AF.Sqrt), reads=[sm16["ssq"]], writes=[sm16["rn"]])
        op("dve", f_ts(sm16["rn"][:], sm16["rn"][:], 1e-12, ALU.max), reads=[sm16["rn"]], writes=[sm16["rn"]])
        op("dve", lambda en: en.reciprocal(out=sm16["rn"][:], in_=sm16["rn"][:]), reads=[sm16["rn"]],
           writes=[sm16["rn"]])
        op("dve", f_tt(v3(kk[:]), v3(kk[:]), b16(sm16["rn"]), ALU.mult), reads=[kk, sm16["rn"]], writes=[kk])
        op("pool", f_tt(E[:], E[:], kp[:], ALU.mult), reads=[E, kp], writes=[E])
        op("dve", f_red(sm16["rks"][:], v3(E[:]), ALU.add), reads=[E], writes=[sm16["rks"]])
        op("dve", f_tt(v3(bon[:]), v3(zv), b16(sm16["rks"]), ALU.mult), reads=[z, sm16["rks"]], writes=[bon])
        pcs = []
        for hf in range(2):
            cs_ = slice(hf * 512, (hf + 1) * 512)
            p = nxt()
            op("pe", f_mm(p[:], m_ui[:], lw[:, cs_]), reads=[m_ui, lw], writes=[p])
            pcs.append(p)
            op("act", f_act(E[:, cs_], p[:], AF.Exp), reads=[p], writes=[E])
            op("dve", f_tt(Rt[:, cs_], zr[:, cs_], E[:, cs_], ALU.mult), reads=[z, E], writes=[Rt])
        op("pool", f_tt(t1[:], kk[:], a_[:], ALU.mult), reads=[kk, a_], writes=[t1])
        for hf in range(2):
            cs_ = slice(hf * 512, (hf + 1) * 512)
            p = pcs[hf]
            op("act", f_act(E[:, cs_], p[:], AF.Exp, scale=-1.0), reads=[p], writes=[E])
            op("dve", f_tt(Bt[:, cs_], t1[:, cs_], E[:, cs_], ALU.mult), reads=[t1, E], writes=[Bt])
            op("pool", f_tt(Kt[:, cs_], kp[:, cs_], E[:, cs_], ALU.mult), reads=[kp, E], writes=[Kt])
            op("dve", f_tt(At[:, cs_], p[:], lw[:, cs_], ALU.subtract), reads=[p, lw], writes=[At])
        op("act", f_act(E[:], At[:], AF.Exp), reads=[At], writes=[E])
        op("dve", f_stt(At[:], kk[:], -1.0, E[:], ALU.mult, ALU.mult), reads=[kk, E], writes=[At])
        p = nxt()
        for h in range(16):
            op("pe", f_mm(p[0:64, 2 * h:2 * h + 2], lw[:, 64 * h:64 * h + 64], ech[:]), reads=[lw, ech], writes=[p])
        op("act", f_act(Wc[:], p[0:64, 0:32], AF.Exp), reads=[p], writes=[Wc])
        op("pool", f_copy(TM["vb"][:], zv), reads=[z], writes=[TM["vb"]])
        for pr in range(8):
            p = (ptp, ptp2)[pr % 2]
            cs_ = slice(pr * 128, (pr + 1) * 128)
            for i, src in enumerate((At, Rt, Bt, Kt)):
                op("pe", f_tr(p[:, i * 128:(i + 1) * 128], src[:, cs_], idb[:]), reads=[src, idb], writes=[p])
            op("act", f_acopy(arT[:, pr, :], p[:, 0:256]), reads=[p], writes=[arT])
            op("dve", f_copy(bkT[:, pr, :], p[:, 256:512]), reads=[p], writes=[bkT])
        for _ in nxt_front:
            pass
        for hg in range(16 // HG):
            heads = [hg * HG + i for i in range(HG)]
            info = []
            for i, h in enumerate(heads):
                pr, hb_ = h // 2, 64 * (h % 2)
                ps_ = slice(hb_, hb_ + 64)
                hc = slice(64 * h, 64 * h + 64)
                info.append((i, h, pr, ps_, hc))
            for (i, h, pr, ps_, hc) in info:
                pa = nxt()
                pb = nxt()
                op("pe", f_mm(pa[:, 0:128], arT[ps_, pr, 0:128], bkT[ps_, pr, 0:128]), reads=[arT, bkT], writes=[pa])
                op("pe", f_mm(pb[:, 0:256], bkT[ps_, pr, 0:128], arT[ps_, pr, 0:256]), reads=[arT, bkT], writes=[pb])
                op("pe", f_mm(pb[:, 256:512], bkT[ps_, pr, 128:256], arT[ps_, pr, 0:256]), reads=[arT, bkT],
                   writes=[pb])
                op("dve", f_tt(PP[i][0][:, 0:128], pa[:, 0:128], m_sl[:], ALU.mult), reads=[pa, m_sl],
                   writes=[PP[i][0]])
                op("dve", f_tt(M4[i][:], pb[:], mask4[:], ALU.mult), reads=[pb, mask4], writes=[M4[i]])
                op("pool", f_copy(PP[i][0][:, 128:256], M4[i][:, 0:128]), reads=[M4[i]], writes=[PP[i][0]])
                op("pool", f_tt(TTb[i][0][:], M4[i][:, 0:128], idb[:], ALU.add), reads=[M4[i], idb],
                   writes=[TTb[i][0]])
            for j in range(5):
                cur, nx = j % 2, (j + 1) % 2
                pcl = {}
                for (i, h, pr, ps_, hc) in info:
                    pc = nxt()
                    pcl[i] = pc
                    op("pe", f_mm(pc[:, 0:128], PP[i][cur][:, 128:256], PP[i][cur][:, 0:128]), reads=[PP[i][cur]],
                       writes=[pc])
                    if j < 4:
                        op("pe", f_mm(pc[:, 128:256], PP[i][cur][:, 0:128], PP[i][cur][:, 128:256]),
                           reads=[PP[i][cur]], writes=[pc])
                    wdt = 256 if j < 4 else 128
                    op("act", f_acopy(PP[i][nx][:, 0:wdt], pc[:, 0:wdt]), reads=[pc], writes=[PP[i][nx]])
                for (i, h, pr, ps_, hc) in info:
                    pc = pcl[i]
                    op("pe", f_mm(pc[:, 256:384], PP[i][nx][:, 0:128], TTb[i][cur][:]), reads=[PP[i][nx], TTb[i][cur]],
                       writes=[pc])
                    op("dve", f_tt(TTb[i][nx][:], pc[:, 256:384], TTb[i][cur][:], ALU.add),
                       reads=[pc, TTb[i][cur]], writes=[TTb[i][nx]])
            pzl = {}
            for (i, h, pr, ps_, hc) in info:
                TT = TTb[i][1]
                pz = nxt()
                pzl[i] = pz
                vh = TM["vb"][:, 64 * h:64 * h + 64]
                op("pe", f_mm(pz[:, 0:64], M4[i][:, 256:384], vh), reads=[M4[i], TM["vb"]], writes=[pz])
                op("act", f_acopy(Zs[i][:], pz[:, 0:64]), reads=[pz], writes=[Zs[i]])
            for (i, h, pr, ps_, hc) in info:
                TT = TTb[i][1]
                pz = pzl[i]
                op("pe", f_mm(pz[:, 128:192], TT[:], At[:, hc]), reads=[TT, At], writes=[pz])
                op("pe", f_mm(pz[:, 192:256], TT[:], Zs[i][:]), reads=[TT, Zs[i]], writes=[pz])
                op("act", f_acopy(AU[i][:], pz[:, 128:256]), reads=[pz], writes=[AU[i]])
            for (i, h, pr, ps_, hc) in info:
                pz = pzl[i]
                op("pe", f_mm(pz[0:64, 256:384], AU[i][:, 0:64], M4[i][:, 128:256], True, False),
                   reads=[AU[i], M4[i]], writes=[pz])
                op("pe", f_mm(pz[0:64, 256:384], Rt[:, hc], idb[:], False, True), reads=[Rt, idb], writes=[pz])
                op("act", f_acopy(QT[i][:], pz[0:64, 256:384]), reads=[pz], writes=[QT[i]])
            for q0 in range(0, HG, 4):
                quad = info[q0:q0 + 4]
                pN, pG, pY = nxt(), nxt(), nxt()
                for c in range(2):
                    cs_ = slice(64 * c, 64 * c + 64)
                    for (i, h, pr, ps_, hc) in quad:
                        j = i - q0
                        vh = TM["vb"][:, 64 * h:64 * h + 64]
                        col = 64 * (2 * j + c)
                        op("pe", f_mm(pN[0:64, col:col + 64], Bt[cs_, hc], AU[i][cs_, 64:128], True, False),
                           reads=[Bt, AU[i]], writes=[pN])
                        op("pe", f_mm(pN[0:64, col:col + 64], Kt[cs_, hc], vh[cs_, :], False, True),
                           reads=[Kt, TM["vb"]], writes=[pN])
                        op("pe", f_mm(pG[0:64, col:col + 64], AU[i][cs_, 0:64], Bt[cs_, hc]), reads=[AU[i], Bt],
                           writes=[pG])
                for (i, h, pr, ps_, hc) in quad:
                    j = i - q0
                    vh = TM["vb"][:, 64 * h:64 * h + 64]
                    yo = pY[0:64, 128 * j:128 * j + 128]
                    op("pe", f_mm(yo, AU[i][:, 64:128], M4[i][:, 128:256], True, False), reads=[AU[i], M4[i]],
                       writes=[pY])
                    op("pe", f_mm(yo, vh, M4[i][:, 384:512], False, True), reads=[TM["vb"], M4[i]], writes=[pY])
                for (i, h, pr, ps_, hc) in quad:
                    j = i - q0
                    for c in range(2):
                        col = 64 * (2 * j + c)
                        op("act", f_act(NW[0:64, i, c, :], pN[0:64, col:col + 64], AF.Copy,
                                        scale=Wc[0:64, 2 * h + c:2 * h + c + 1]), reads=[pN, Wc], writes=[NW])
                op("dve", f_tt(GTs[0:64, q0:q0 + 4, :, :].rearrange("p a c k -> p (a c) k"),
                               pG[0:64, :].rearrange("p (a k) -> p a k", a=8),
                               idf[0:64, 0:64].unsqueeze(1).broadcast_to([64, 8, 64]), ALU.add), reads=[pG, idf],
                   writes=[GTs])
                op("act", f_acopy(Y0s[0:64, q0:q0 + 4, :], pY[0:64, :].rearrange("p (a t) -> p a t", a=4)),
                   reads=[pY], writes=[Y0s])
            for c in range(2):
                cs_ = slice(64 * c, 64 * c + 64)
                pS = py[c]
                pH = nxt()
                for (i, h, pr, ps_, hc) in info:
                    op("pe", f_mm(pS[0:64, 64 * i:64 * i + 64], Hhi[0:64, h, :], QT[i][:, cs_], True, False),
                       reads=[Hhi, QT[i]], writes=[pS])
                    op("pe", f_mm(pS[0:64, 64 * i:64 * i + 64], Hlo[0:64, h, :], QT[i][:, cs_], False, True),
                       reads=[Hlo, QT[i]], writes=[pS])
                for (i, h, pr, ps_, hc) in info:
                    op("pe", f_mm(pH[0:64, 64 * i:64 * i + 64], GTs[0:64, i, c, :], Hhi[0:64, h, :], True, False),
                       reads=[GTs, Hhi], writes=[pH])
                    op("pe", f_mm(pH[0:64, 64 * i:64 * i + 64], GTs[0:64, i, c, :], Hlo[0:64, h, :], False, True),
                       reads=[GTs, Hlo], writes=[pH])
                for (i, h, pr, ps_, hc) in info:
                    op("dve", f_stt(Hs[0:64, h, :], pH[0:64, 64 * i:64 * i + 64], Wc[0:64, 2 * h + c:2 * h + c + 1],
                                    NW[0:64, i, c, :], ALU.mult, ALU.add), reads=[pH, Wc, NW], writes=[Hs])
                hsl = slice(HG * hg, HG * hg + HG)
                op("pool", f_copy(Hhi[0:64, hsl, :], Hs[0:64, hsl, :]), reads=[Hs], writes=[Hhi])
                op("pool", f_tt(Hlo[0:64, hsl, :], Hs[0:64, hsl, :], Hhi[0:64, hsl, :], ALU.subtract), reads=[Hs, Hhi],
                   writes=[Hlo])
                op("dve", f_tt(Y0s[0:64, :, cs_], Y0s[0:64, :, cs_],
                               pS[0:64, :].rearrange("p (a t) -> p a t", a=8), ALU.add), reads=[Y0s, pS],
                   writes=[Y0s])
            pT = nxt()
            for (i, h, pr, ps_, hc) in info:
                op("pe", f_tr(pT[:, 64 * i:64 * i + 64], Y0s[0:64, i, :], idf[0:64, 0:64]), reads=[Y0s, idf],
                   writes=[pT])
            op("act", f_acopy(ysb[:, 512 * hg:512 * hg + 512], pT[:]), reads=[pT], writes=[ysb])
            for _ in range(4):
                next(nxt_front, None)
        for _ in nxt_front:
            pass
        q16 = sm16
        op("dve", f_red(q16["mu"][:], v3(ysb[:]), ALU.add), reads=[ysb], writes=[q16["mu"]])
        op("dve", f_ts(q16["mu"][:], q16["mu"][:], -1.0 / 64, ALU.mult), reads=[q16["mu"]], writes=[q16["mu"]])
        op("dve", f_tt(v3(ysb[:]), v3(ysb[:]), b16(q16["mu"]), ALU.add), reads=[ysb, q16["mu"]], writes=[ysb])
        op("pool", f_tt(t1[:], ysb[:], ysb[:], ALU.mult), reads=[ysb], writes=[t1])
        op("dve", f_red(q16["vs"][:], v3(t1[:]), ALU.add), reads=[t1], writes=[q16["vs"]])
        op("act", f_act(q16["rs"][:], q16["vs"][:], AF.Sqrt, bias=GN_EPS, scale=1.0 / 64), reads=[q16["vs"]],
           writes=[q16["rs"]])
        op("dve", lambda en: en.reciprocal(out=q16["rs"][:], in_=q16["rs"][:]), reads=[q16["rs"]],
           writes=[q16["rs"]])
        op("dve", f_tt(v3(ysb[:]), v3(ysb[:]), b16(q16["rs"]), ALU.mult), reads=[ysb, q16["rs"]], writes=[ysb])
        for hf in range(2):
            cs_ = slice(hf * 512, (hf + 1) * 512)
            p = bc_vec(5, hf)
            op("dve", f_tt(ysb[:, cs_], ysb[:, cs_], p[:], ALU.mult), reads=[ysb, p], writes=[ysb])
            p = bc_vec(6, hf)
            op("dve", f_tt(ysb[:, cs_], ysb[:, cs_], p[:], ALU.add), reads=[ysb, p], writes=[ysb])
        op("pool", f_tt(ysb[:], ysb[:], bon[:], ALU.add), reads=[ysb, bon], writes=[ysb])
        o_ = orb[0]
        op("dve", f_tt(o_[:], ysb[:], g_[:], ALU.mult), reads=[ysb, g_], writes=[o_])
        sc.dma("pool", outr[ti * 128:(ti + 1) * 128, :], o_[:], reads=[o_])
    sc.barrier()
    sc.emit()
    cx.close()


def phase_merge(nc, sc, S, x1, x2, gates, att, outr, W):
    NT = S // 128
    cx = Ctx(nc)
    op = sc.op
    idf, idb = make_ident(sc, cx, BF16)
    wout = cx.sb("wout", [128, 8, D], BF16)
    wo = cx.sb("wo", [128, 8, D], BF16)
    wup = cx.sb("wup", [128, 2, D], BF16)
    stg = [cx.sb("stg%d" % i, [128, D], F32) for i in range(2)]
    load_weight_bf16(sc, stg, wout, lambda kc, f0, fw: wout[:, kc, f0:f0 + fw], W["rwkv_w_out"], D, D, D)
    load_weight_bf16(sc, stg, wo, lambda kc, f0, fw: wo[:, kc, f0:f0 + fw], W["w_o"], D, D, D)
    load_weight_bf16(sc, stg, wup, lambda kc, f0, fw: wup[:, kc, f0:f0 + fw], W["attn_w_up"], 256, D, D)
    xs = [cx.sb("xs%d" % b, [128, D], F32) for b in range(2)]
    gt = [cx.sb("gt%d" % b, [128, 2 * D], BF16) for b in range(2)]
    at = [cx.sb("at%d" % b, [128, 3, 4, 66], F32) for b in range(2)]
    orr = [cx.sb("orr%d" % b, [128, D], BF16) for b in range(2)]
    orT = cx.sb("orT", [128, 8, 128], BF16)
    mgT = cx.sb("mgT", [128, 8, 128], BF16)
    oaT = cx.sb("oaT", [128, 2, 128], BF16)
    mx = cx.sb("mx", [128, 4], F32)
    cc = cx.sb("cc", [128, 3, 4], F32)
    den = cx.sb("den", [128, 4], F32)
    dtmp = cx.sb("dtmp", [128, 4], F32)
    num = cx.sb("num", [128, 4, 64], F32)
    ntmp = cx.sb("ntmp", [128, 4, 64], F32)
    oab = cx.sb("oab", [128, 256], BF16)
    ta = cx.sb("ta", [128, D], F32)
    tb = cx.sb("tb", [128, D], F32)
    mgb = cx.sb("mgb", [128, D], BF16)
    ptp = cx.ps("ptp", [128, D], BF16)
    pya = [cx.ps("pya%d" % i, [128, 512], F32) for i in range(2)]
    pyr = [cx.ps("pyr%d" % i, [128, 512], F32) for i in range(2)]
    pout = [cx.ps("pout%d" % i, [128, 512], F32) for i in range(2)]
    for ti in range(NT):
        b = ti % 2
        rows = slice(ti * 128, (ti + 1) * 128)
        x, g_, a_, o_ = xs[b], gt[b], at[b], orr[b]
        sc.dma("sp", x[:], x1[rows, :], writes=[x])
        sc.dma("sp", g_[:], gates[rows, :], writes=[g_])
        sc.dma("sp", a_[:], att[rows], writes=[a_])
        sc.dma("sp", o_[:], outr[rows, :], writes=[o_])
        for kc in range(8):
            op("pe", f_tr(ptp[:, kc * 128:(kc + 1) * 128], o_[:, kc * 128:(kc + 1) * 128], idb[:]), reads=[o_, idb],
               writes=[ptp])
        op("act", f_acopy(orT[:], ptp[:].rearrange("p (k t) -> p k t", k=8)), reads=[ptp], writes=[orT])
        for hf in range(2):
            for kc in range(8):
                op("pe", f_mm(pyr[hf][:], orT[:, kc, :], wout[:, kc, hf * 512:(hf + 1) * 512], kc == 0, kc == 7),
                   reads=[orT, wout], writes=[pyr[hf]])
        m0, m1, m2_ = (a_[:, g, :, 64] for g in range(3))
        op("dve", f_tt(mx[:], m0, m1, ALU.max), reads=[a_], writes=[mx])
        op("dve", f_tt(mx[:], mx[:], m2_, ALU.max), reads=[a_, mx], writes=[mx])
        for g in range(3):
            op("dve", f_tt(cc[:, g, :], a_[:, g, :, 64], mx[:], ALU.subtract), reads=[a_, mx], writes=[cc])
        op("act", f_act(cc[:], cc[:], AF.Exp), reads=[cc], writes=[cc])
        for g in range(3):
            if g == 0:
                op("dve", f_tt(den[:], cc[:, 0, :], a_[:, 0, :, 65], ALU.mult), reads=[cc, a_], writes=[den])
                op("dve", f_tt(num[:], a_[:, 0, :, 0:64], cc[:, 0, :].unsqueeze(2).broadcast_to([128, 4, 64]),
                               ALU.mult), reads=[cc, a_], writes=[num])
            else:
                op("dve", f_tt(dtmp[:], cc[:, g, :], a_[:, g, :, 65], ALU.mult), reads=[cc, a_], writes=[dtmp])
                op("dve", f_tt(den[:], den[:], dtmp[:], ALU.add), reads=[den, dtmp], writes=[den])
                op("dve", f_tt(ntmp[:], a_[:, g, :, 0:64], cc[:, g, :].unsqueeze(2).broadcast_to([128, 4, 64]),
                               ALU.mult), reads=[cc, a_], writes=[ntmp])
                op("pool", f_tt(num[:], num[:], ntmp[:], ALU.add), reads=[num, ntmp], writes=[num])
        op("dve", lambda en: en.reciprocal(out=den[:], in_=den[:]), reads=[den], writes=[den])
        op("dve", f_tt(oab[:].rearrange("p (h d) -> p h d", h=4), num[:],
                       den[:].unsqueeze(2).broadcast_to([128, 4, 64]), ALU.mult), reads=[num, den], writes=[oab])
        for kc in range(2):
            op("pe", f_tr(ptp[:, kc * 128:(kc + 1) * 128], oab[:, kc * 128:(kc + 1) * 128], idb[:]),
               reads=[oab, idb], writes=[ptp])
        op("act", f_acopy(oaT[:], ptp[:, 0:256].rearrange("p (k t) -> p k t", k=2)), reads=[ptp], writes=[oaT])
        for hf in range(2):
            for kc in range(2):
                op("pe", f_mm(pya[hf][:], oaT[:, kc, :], wup[:, kc, hf * 512:(hf + 1) * 512], kc == 0, kc == 1),
                   reads=[oaT, wup], writes=[pya[hf]])
        for hf in range(2):
            cs_ = slice(hf * 512, (hf + 1) * 512)
            op("dve", f_tt(ta[:, cs_], pya[hf][:], g_[:, hf * 512:(hf + 1) * 512], ALU.mult), reads=[pya[hf], g_],
               writes=[ta])
            op("dve", f_tt(tb[:, cs_], pyr[hf][:], g_[:, D + hf * 512:D + (hf + 1) * 512], ALU.mult),
               reads=[pyr[hf], g_], writes=[tb])
        op("pool", f_tt(mgb[:], ta[:], tb[:], ALU.add), reads=[ta, tb], writes=[mgb])
        for kc in range(8):
            op("pe", f_tr(ptp[:, kc * 128:(kc + 1) * 128], mgb[:, kc * 128:(kc + 1) * 128], idb[:]),
               reads=[mgb, idb], writes=[ptp])
        op("act", f_acopy(mgT[:], ptp[:].rearrange("p (k t) -> p k t", k=8)), reads=[ptp], writes=[mgT])
        for hf in range(2):
            cs_ = slice(hf * 512, (hf + 1) * 512)
            for kc in range(8):
                op("pe", f_mm(pout[hf][:], mgT[:, kc, :], wo[:, kc, cs_], kc == 0, kc == 7), reads=[mgT, wo],
                   writes=[pout[hf]])
            op("dve", f_tt(x[:, cs_], x[:, cs_], pout[hf][:], ALU.add), reads=[x, pout[hf]], writes=[x])
        sc.dma("pool", x2[rows, :], x[:], reads=[x])
    sc.barrier()
    sc.emit()
    cx.close()


def build_nc(S, stages="ABCDE", debug=False):
    nc = bass.Bass("TRN2", target_bir_lowering=False)

    def inp(name, shape):
        return nc.dram_tensor(name, list(shape), F32, kind="ExternalInput").ap()

    def scr(name, shape, dt):
        return nc.dram_tensor(name, list(shape), dt, kind="ExternalOutput" if debug else "Internal").ap()

    W = {}
    for name, shape in WEIGHT_SHAPES:
        W[name] = inp(name, shape)
    x = inp("x", [S, D])
    out = nc.dram_tensor("out", [S, D], F32, kind="ExternalOutput").ap()
    x1 = scr("x1_scr", [S, D], F32)
    x2 = scr("x2_scr", [S, D], F32)
    qkv = scr("qkv_scr", [S, 2304], BF16)
    gates = scr("gates_scr", [S, 2048], BF16)
    att = scr("att_scr", [S, 3, 4, 66], F32)
    outr = scr("outr_scr", [S, D], BF16)
    sc = Sched(nc)
    if "A" in stages:
        phase_ffn(nc, sc, S, x, x1, W["ffn1_norm"], W["ffn1_w_gate"], W["ffn1_w_up"], W["ffn1_w_down"])
    if "B" in stages:
        phase_proj(nc, sc, S, x1, qkv, gates, W["mix_norm"], W["w_in"], W["gate_bias"])
    if "C" in stages:
        phase_attn(nc, sc, S, qkv, att)
    if "D" in stages:
        phase_rwkv(nc, sc, S, x1, outr, W)
        phase_merge(nc, sc, S, x1, x2, gates, att, outr, W)
    if "E" in stages:
        phase_ffn(nc, sc, S, x2 if "D" in stages else x1, out, W["ffn2_norm"], W["ffn2_w_gate"], W["ffn2_w_up"],
                  W["ffn2_w_down"], final_gain=W["final_norm"])
    sc.close()
    return nc


WEIGHT_SHAPES = [
    ("ffn1_norm", [D]), ("ffn1_w_gate", [D, FF]), ("ffn1_w_up", [D, FF]), ("ffn1_w_down", [FF, D]),
    ("mix_norm", [D]), ("w_in", [D, IN_COLS]), ("gate_bias", [2 * D]), ("attn_w_up", [256, D]),
    ("rwkv_mu", [RS]), ("rwkv_w0", [D]), ("rwkv_w2", [64, D]), ("rwkv_a0", [D]), ("rwkv_a2", [64, D]),
    ("rwkv_g2", [160, D]), ("rwkv_k_k", [D]), ("rwkv_k_a", [D]), ("rwkv_r_k", [D]), ("rwkv_ln_w", [D]),
    ("rwkv_ln_b", [D]), ("rwkv_w_out", [D, D]), ("w_o", [D, D]),
    ("ffn2_norm", [D]), ("ffn2_w_gate", [D, FF]), ("ffn2_w_up", [D, FF]), ("ffn2_w_down", [FF, D]),
    ("final_norm", [D]),
]


def make_in_maps(inputs):
    B = inputs["x"].shape[0]
    shared = {}
    for name, shape in WEIGHT_SHAPES:
        a = np.asarray(inputs[name], dtype=np.float32)
        shared[name] = np.ascontiguousarray(a.reshape(shape))
    xin = np.asarray(inputs["x"], dtype=np.float32)
    in_maps = []
    for c in range(B):
        m = dict(shared)
        m["x"] = np.ascontiguousarray(xin[c])
        in_maps.append(m)
    return in_maps


def kernel(**inputs):
    B, S, _ = inputs["x"].shape
    nc = build_nc(S)
    in_maps = make_in_maps(inputs)
    res = run_bass_kernel_spmd(nc, in_maps, core_ids=list(range(B)))
    return np.stack([np.asarray(r["out"]) for r in res.results], axis=0).astype(np.float32)
```
